# Optimizing a Trainium2 kernel written in Bass

```python
import math
import jax, jax.numpy as jnp
from jax import lax
import numpy as np

D_MODEL = 1024
BATCH = 2
SEQ = 8192
DEPTH = 1
DEC_BATCH = 4
DEC_SEQ = 4096
PAST_LEN = 128

MIX_WIDTH = D_MODEL
MLA_HEADS = 8
QK_NOPE = 64
QK_ROPE = 32
V_HEAD = 64
MLA_WIDTH = MLA_HEADS * V_HEAD
Q_LORA = 256
KV_LORA = 256
FNET_GROUPS = 4
FNET_WIDTH = MIX_WIDTH - MLA_WIDTH
FNET_GROUP_DIM = FNET_WIDTH // FNET_GROUPS
IN_COLS = Q_LORA + KV_LORA + QK_ROPE + FNET_WIDTH
D_FF = 2816
CONV_W = 3
ROPE_THETA = 10000.0
RMS_EPS = 1e-6
Q_BLOCK = 128
ATTN_SCALE = 1.0 / math.sqrt(QK_NOPE + QK_ROPE)

kernel_name = "hybrid_mla_fnet_convffn_encoder"


def rmsnorm(x, g):
    xf = x.astype(jnp.float32)
    out = xf * lax.rsqrt(jnp.mean(xf * xf, axis=-1, keepdims=True) + RMS_EPS)
    return (out * g.astype(jnp.float32)).astype(x.dtype)


def rope_tables(seq_len):
    pos = jnp.arange(seq_len, dtype=jnp.float32)
    inv_freq = ROPE_THETA ** (-jnp.arange(0, QK_ROPE, 2, dtype=jnp.float32) / QK_ROPE)
    ang = pos[:, None] * inv_freq[None, :]
    return jnp.cos(ang), jnp.sin(ang)


def apply_rope(x, cos, sin):
    xf = x.astype(jnp.float32)
    x1, x2 = jnp.split(xf, 2, axis=-1)
    out = jnp.concatenate([x1 * cos - x2 * sin, x2 * cos + x1 * sin], axis=-1)
    return out.astype(x.dtype)


def mla(c_q, c_kv, k_rope_raw, g_q, w_uq, g_kv, w_ukv):
    b, s, _ = c_q.shape
    cos, sin = rope_tables(s)
    q = (rmsnorm(c_q, g_q) @ w_uq).reshape(b, s, MLA_HEADS, QK_NOPE + QK_ROPE)
    q_nope = q[..., :QK_NOPE]
    q_rope = apply_rope(q[..., QK_NOPE:], cos[None, :, None, :], sin[None, :, None, :])
    kv = (rmsnorm(c_kv, g_kv) @ w_ukv).reshape(b, s, MLA_HEADS, QK_NOPE + V_HEAD)
    k_nope = kv[..., :QK_NOPE]
    v = kv[..., QK_NOPE:]
    k_rope = apply_rope(k_rope_raw, cos[None], sin[None])

    nb = s // Q_BLOCK
    qn_blk = q_nope.reshape(b, nb, Q_BLOCK, MLA_HEADS, QK_NOPE).transpose(1, 0, 2, 3, 4)
    qr_blk = q_rope.reshape(b, nb, Q_BLOCK, MLA_HEADS, QK_ROPE).transpose(1, 0, 2, 3, 4)

    def block(args):
        qn, qr = args
        scores = (jnp.einsum('bqhd,bkhd->bhqk', qn, k_nope, preferred_element_type=jnp.float32)
                  + jnp.einsum('bqhr,bkr->bhqk', qr, k_rope, preferred_element_type=jnp.float32)) * ATTN_SCALE
        probs = jax.nn.softmax(scores, axis=-1).astype(v.dtype)
        return jnp.einsum('bhqk,bkhd->bqhd', probs, v)

    out = lax.map(block, (qn_blk, qr_blk))
    return out.transpose(1, 0, 2, 3, 4).reshape(b, s, MLA_WIDTH)


def fourier_mix(u, w_fnet):
    b, s, _ = u.shape
    ug = u.reshape(b, s, FNET_GROUPS, FNET_GROUP_DIM).astype(jnp.float32)
    fr = jnp.fft.fftn(ug, axes=(1, 3), norm="ortho").real.astype(u.dtype)
    out = jnp.einsum('bsgc,gcd->bsgd', fr, w_fnet)
    return out.reshape(b, s, FNET_WIDTH)


def dwconv_centred(h, w, bias):
    s = h.shape[1]
    half = CONV_W // 2
    hp = jnp.pad(h, ((0, 0), (half, half), (0, 0)))
    out = bias
    for j in range(CONV_W):
        out = out + hp[:, j:j + s, :] * w[j]
    return out


def conv_ffn(h, w_gate, w_up, conv_w, conv_b, w_down):
    gate = dwconv_centred(h @ w_gate, conv_w, conv_b)
    act = jax.nn.gelu(gate, approximate=True) * (h @ w_up)
    return act @ w_down


def trunk(x, g_pre_mix, w_in, g_q, w_uq, g_kv, w_ukv, w_fnet, w_out, g_post_mix,
          g_pre_ffn, w_gate, w_up, conv_w, conv_b, w_down, g_post_ffn):
    for l in range(DEPTH):
        h = rmsnorm(x, g_pre_mix[l])
        p = h @ w_in[l]
        c_q = p[..., :Q_LORA]
        c_kv = p[..., Q_LORA:Q_LORA + KV_LORA]
        k_rope = p[..., Q_LORA + KV_LORA:Q_LORA + KV_LORA + QK_ROPE]
        f_in = p[..., Q_LORA + KV_LORA + QK_ROPE:]
        a = mla(c_q, c_kv, k_rope, g_q[l], w_uq[l], g_kv[l], w_ukv[l])
        f = fourier_mix(f_in, w_fnet[l])
        mix = jnp.concatenate([a, f], axis=-1) @ w_out[l]
        x = x + rmsnorm(mix, g_post_mix[l])
        h = rmsnorm(x, g_pre_ffn[l])
        x = x + rmsnorm(conv_ffn(h, w_gate[l], w_up[l], conv_w[l], conv_b[l], w_down[l]), g_post_ffn[l])
    return x


def setup_inputs(seed: int = 0) -> dict:
    key = jax.random.key(seed)
    ks = jax.random.split(key, 20)
    f32 = jnp.float32

    def nrm(k, shape, fan_in):
        return jax.random.normal(k, shape, f32) * (fan_in ** -0.5)

    def gain(k, dim):
        return 1.0 + 0.02 * jax.random.normal(k, (DEPTH, dim), f32)

    return {
        "x_prompt": jax.random.normal(ks[0], (BATCH, SEQ, D_MODEL), f32),
        "x_sample": jax.random.normal(ks[1], (DEC_BATCH, DEC_SEQ, D_MODEL), f32),
        "g_pre_mix": gain(ks[2], D_MODEL),
        "w_in": nrm(ks[3], (DEPTH, D_MODEL, IN_COLS), D_MODEL),
        "g_q": gain(ks[4], Q_LORA),
        "w_uq": nrm(ks[5], (DEPTH, Q_LORA, MLA_HEADS * (QK_NOPE + QK_ROPE)), Q_LORA),
        "g_kv": gain(ks[6], KV_LORA),
        "w_ukv": nrm(ks[7], (DEPTH, KV_LORA, MLA_HEADS * (QK_NOPE + V_HEAD)), KV_LORA),
        "w_fnet": nrm(ks[8], (DEPTH, FNET_GROUPS, FNET_GROUP_DIM, FNET_GROUP_DIM), FNET_GROUP_DIM),
        "w_out": nrm(ks[9], (DEPTH, MIX_WIDTH, D_MODEL), MIX_WIDTH),
        "g_post_mix": gain(ks[10], D_MODEL),
        "g_pre_ffn": gain(ks[11], D_MODEL),
        "w_gate": nrm(ks[12], (DEPTH, D_MODEL, D_FF), D_MODEL),
        "w_up": nrm(ks[13], (DEPTH, D_MODEL, D_FF), D_MODEL),
        "conv_w": nrm(ks[14], (DEPTH, CONV_W, D_FF), CONV_W),
        "conv_b": 0.01 * jax.random.normal(ks[15], (DEPTH, D_FF), f32),
        "w_down": nrm(ks[16], (DEPTH, D_FF, D_MODEL), D_FF),
        "g_post_ffn": gain(ks[17], D_MODEL),
    }


def reference(x_prompt, x_sample, g_pre_mix, w_in, g_q, w_uq, g_kv, w_ukv, w_fnet, w_out,
              g_post_mix, g_pre_ffn, w_gate, w_up, conv_w, conv_b, w_down, g_post_ffn):
    y_prompt = trunk(x_prompt, g_pre_mix, w_in, g_q, w_uq, g_kv, w_ukv, w_fnet, w_out, g_post_mix,
                     g_pre_ffn, w_gate, w_up, conv_w, conv_b, w_down, g_post_ffn)
    y_sample = trunk(x_sample, g_pre_mix, w_in, g_q, w_uq, g_kv, w_ukv, w_fnet, w_out, g_post_mix,
                     g_pre_ffn, w_gate, w_up, conv_w, conv_b, w_down, g_post_ffn)
    return (y_prompt, y_sample)
```

```python
import math
from contextlib import ExitStack

import numpy as np
import ml_dtypes

import concourse.bass as bass
import concourse.mybir as mybir
from concourse.bass_utils import run_bass_kernel_spmd

F32 = mybir.dt.float32
BF16 = mybir.dt.bfloat16
AF = mybir.ActivationFunctionType
ALU = mybir.AluOpType
NPBF = ml_dtypes.bfloat16

D = 1024
KC = 8
QL = 256
KVL = 256
NH = 8
DN = 64
DR = 32
DV = 64
DQ = DN + DR
FW = 512
NG = 4
DFF = 2816
NF = DFF // 128
EPS = 1e-6
THETA = 10000.0
SCALE = 1.0 / math.sqrt(DQ)
WIN_COLS = QL + KVL + 2 * DR + FW

ENGS = ("pe", "act", "dve", "pool", "sp")
NDMA = 24


class Tile:
    __slots__ = ("name", "w", "r", "rd")

    def __init__(self, name=""):
        self.name = name
        self.w = None
        self.r = {}
        self.rd = []


class Op:
    __slots__ = ("eng", "fn", "deps", "signal", "count", "is_dma", "sem", "waits", "prewait")

    def __init__(self, eng, fn, is_dma):
        self.eng = eng
        self.fn = fn
        self.deps = []
        self.signal = False
        self.count = 0
        self.is_dma = is_dma
        self.sem = None
        self.waits = None
        self.prewait = None


class Sched:
    def __init__(self):
        self.ops = {e: [] for e in ENGS}
        self.all_ops = []
        self.last = {e: None for e in ENGS}
        self.dma_rr = 0
        self.dma_rr2 = 0
        self.dma_last = [None] * NDMA

    def add(self, eng, fn, reads=(), writes=(), deps=(), is_dma=False):
        op = Op(eng, fn, is_dma)
        d = []
        for t in reads:
            if t.w is not None:
                d.append(t.w)
        for t in writes:
            if t.w is not None:
                d.append(t.w)
            d.extend(t.r.values())
            d.extend(t.rd)
        d.extend(deps)
        seen = set()
        for x in d:
            if x is None or x is op or id(x) in seen:
                continue
            seen.add(id(x))
            if x.eng == "pe" and eng == "pe" and not x.is_dma and not is_dma:
                continue
            op.deps.append(x)
        for t in reads:
            if is_dma:
                t.rd.append(op)
            else:
                t.r[eng] = op
        for t in writes:
            t.w = op
            t.r = {}
            t.rd = []
        if is_dma:
            if eng == "sp":
                s = self.dma_rr % 16
                self.dma_rr += 1
            else:
                s = 16 + self.dma_rr2 % (NDMA - 16)
                self.dma_rr2 += 1
            op.sem = s
            op.prewait = self.dma_last[s]
            self.dma_last[s] = op
        self.ops[eng].append(op)
        self.all_ops.append(op)
        if fn is not None:
            self.last[eng] = op
        return op

    def dma(self, fn, reads=(), writes=(), deps=()):
        return self.add("sp", fn, reads, writes, deps, is_dma=True)

    def barrier(self):
        pend = [self.last[e] for e in ENGS if self.last[e] is not None]
        pend += [o for o in self.dma_last if o is not None]
        for e in ENGS:
            self.add(e, None, deps=pend)

    def finalize(self):
        for op in self.all_ops:
            for d in op.deps:
                d.signal = True
            if op.prewait is not None:
                op.prewait.signal = True
        cnt = {e: 0 for e in ENGS}
        dcnt = [0] * NDMA
        for op in self.all_ops:
            if op.is_dma:
                op.signal = True
                dcnt[op.sem] += 16
                op.count = dcnt[op.sem]
            elif op.signal:
                assert op.fn is not None
                cnt[op.eng] += 1
                op.count = cnt[op.eng]
        waited = {e: {} for e in ENGS}
        for op in self.all_ops:
            w = {}
            dl = list(op.deps)
            if op.prewait is not None:
                dl.append(op.prewait)
            for d in dl:
                key = ("dma", d.sem) if d.is_dma else ("eng", d.eng)
                if w.get(key, 0) < d.count:
                    w[key] = d.count
            wd = waited[op.eng]
            out = []
            for key, val in w.items():
                if wd.get(key, 0) >= val:
                    continue
                wd[key] = val
                out.append((key, val))
            op.waits = out
        self.final_counts = cnt
        self.final_dma = dcnt

    def emit(self, block, sems_eng, sems_dma):
        self.finalize()
        engmap = {"pe": "tensor", "act": "scalar", "dve": "vector", "pool": "gpsimd", "sp": "sync"}

        def semof(key):
            return sems_dma[key[1]] if key[0] == "dma" else sems_eng[key[1]]

        def run(ename):
            def body(e):
                for op in self.ops[ename]:
                    for key, val in op.waits:
                        e.wait_ge(semof(key), val)
                    if op.fn is None:
                        continue
                    ins = op.fn(e)
                    if op.signal:
                        if op.is_dma:
                            ins.then_inc(sems_dma[op.sem], 16)
                        else:
                            ins.then_inc(sems_eng[ename], 1)
                if ename == "sp":
                    for k in ENGS:
                        if self.final_counts[k] > 0:
                            e.wait_ge(sems_eng[k], self.final_counts[k])
                    for s, c in enumerate(self.final_dma):
                        if c > 0:
                            e.wait_ge(sems_dma[s], c)
            return body

        for ename in ENGS:
            getattr(block, engmap[ename])(run(ename))


class Plan:
    def __init__(self, cap):
        self.cap = cap
        self.items = []

    def add(self, name, shape, dtype, live):
        esz = 4 if dtype == F32 else 2
        n = 1
        for s in shape[1:]:
            n *= s
        size = (n * esz + 63) // 64 * 64
        it = dict(name=name, shape=list(shape), dtype=dtype, live=frozenset(live), size=size, off=None)
        self.items.append(it)
        return it

    def solve(self):
        placed = []
        for it in sorted(self.items, key=lambda i: -i["size"]):
            cands = sorted([(p["off"], p["off"] + p["size"]) for p in placed if p["live"] & it["live"]])
            off = 0
            for a, b in cands:
                if off + it["size"] <= a:
                    break
                off = max(off, b)
            assert off + it["size"] <= self.cap, ("SBUF overflow", it["name"], off, it["size"])
            it["off"] = off
            placed.append(it)


def make_cfg(sp=8192, ss=4096):
    parts = []
    for (S, nsplit, nb) in ((sp, 4, 2), (ss, 2, 4)):
        nown = S // nsplit
        n1 = S // 128
        assert 128 % n1 == 0 and nown % 512 == 0 and nown % n1 == 0
        fg = min(1024, nown)
        parts.append(dict(S=S, nsplit=nsplit, nbatch=nb, NOWN=nown, NQ=nown + 2, N1=n1, NJ=128 // n1, NT=S // 128,
                          NST=S // 512, NK2=nown // n1, N2C=nown // n1 + 2, FG=fg, NFG=nown // fg))
    return dict(parts=parts)


def perm_rows(pc):
    N1, NJ, NT = pc["N1"], pc["NJ"], pc["NT"]
    t = np.arange(NT)[:, None, None]
    j = np.arange(NJ)[None, :, None]
    s1 = np.arange(N1)[None, None, :]
    return (128 * s1 + NJ * t + j).reshape(-1)


def rope_tab(pos):
    inv = THETA ** (-np.arange(0, DR, 2, dtype=np.float64) / DR)
    ang = pos.astype(np.float64)[None, :] * inv[:, None]
    c, s = np.cos(ang), np.sin(ang)
    cc = np.concatenate([c, c], 0)
    ss = np.concatenate([-s, s], 0)
    return cc, ss


def host_consts(pc, q):
    S, N1, NJ, NT, NOWN = pc["S"], pc["N1"], pc["NJ"], pc["NT"], pc["NOWN"]
    perm = perm_rows(pc)
    dft = np.zeros((NT, 128, 3, 128), np.float64)
    k1 = np.arange(N1)
    for t in range(NT):
        for j in range(NJ):
            s = perm[128 * t + j * N1: 128 * t + (j + 1) * N1]
            ang = 2 * np.pi * ((s[:, None] * k1[None, :]) % S) / S
            c, sn = np.cos(ang) / math.sqrt(S), np.sin(ang) / math.sqrt(S)
            sl = slice(j * N1, (j + 1) * N1)
            dft[t, sl, 0, sl] = c
            dft[t, sl, 1, sl] = sn
            dft[t, sl, 2, sl] = -sn
    cc, ss = rope_tab(perm)
    csk = np.concatenate([cc, ss], 0).reshape(64, pc["NST"], 512).transpose(1, 0, 2)
    a = q * NOWN
    pos_own = np.concatenate([np.arange(a, a + NOWN), [(a - 1) % S, (a + NOWN) % S]])
    cq, sq = rope_tab(pos_own)
    csq = np.stack([cq, sq], 1)
    k2lo = a // N1
    k2 = (np.arange(k2lo - 1, k2lo + pc["NK2"] + 1)) % 128
    s2 = np.arange(128)
    ang2 = 2 * np.pi * ((s2[:, None] * k2[None, :]) % 128) / 128
    c2 = np.stack([np.cos(ang2), -np.sin(ang2)], 1)
    mask = np.ones((128, 3), np.float32)
    mask[:, 0] = 0.0 if a == 0 else 1.0
    mask[:, 1] = 0.0 if a + NOWN >= S else 1.0
    return dict(dft=dft.astype(NPBF), csk=csk.astype(np.float32), csq=csq.astype(NPBF),
                c2=c2.astype(NPBF), mask=mask, pos_own=pos_own, perm=perm)


def host_weights(w_in, w_uq, w_ukv, w_fnet, w_out, w_gate, w_up, conv_w, conv_b, w_down,
                 g_pre_mix, g_q, g_kv, g_post_mix, g_pre_ffn, g_post_ffn):
    def kmaj(w):
        k, n = w.shape
        return np.ascontiguousarray(w.reshape(k // 128, 128, n).transpose(1, 0, 2))

    def gcol(g):
        return np.ascontiguousarray(g.reshape(-1, 128).T)

    o = {}
    kr = w_in[:, QL + KVL:QL + KVL + DR]
    kr_sw = np.concatenate([kr[:, DR // 2:], kr[:, :DR // 2]], 1)
    o["w_in"] = kmaj(np.concatenate([w_in[:, :QL + KVL], kr, kr_sw, w_in[:, QL + KVL + DR:]], 1))
    wq = w_uq.reshape(QL, NH, DQ)
    wq_sw = np.concatenate([wq[:, :, :DN], wq[:, :, DN + DR // 2:], wq[:, :, DN:DN + DR // 2]], 2)
    o["w_uq"] = kmaj(np.concatenate([wq.reshape(QL, -1), wq_sw.reshape(QL, -1)], 1))
    wkv = w_ukv.reshape(KVL, NH, DN + DV)
    o["w_ukv"] = kmaj(np.concatenate([wkv[:, :, :DN].reshape(KVL, -1), wkv[:, :, DN:].reshape(KVL, -1)], 1))
    o["w_fnet"] = np.ascontiguousarray(w_fnet.transpose(1, 0, 2))
    o["w_out"] = kmaj(w_out)
    o["w_gate"] = np.ascontiguousarray(w_gate.reshape(KC, 128, NF, 128).transpose(2, 1, 0, 3))
    o["w_up"] = np.ascontiguousarray(w_up.reshape(KC, 128, NF, 128).transpose(2, 1, 0, 3))
    o["w_down"] = np.ascontiguousarray(w_down.reshape(NF, 128, D))
    cw = np.concatenate([conv_w, conv_b[None, :]], 0)
    o["conv"] = np.ascontiguousarray(cw.reshape(4, NF, 128).transpose(2, 1, 0))
    o["gcols"] = np.ascontiguousarray(np.concatenate([gcol(g_pre_mix), gcol(g_pre_ffn), gcol(g_q), gcol(g_kv)], 1))
    o["g_post_mix"] = np.ascontiguousarray(np.broadcast_to(g_post_mix[None, :], (128, D)))
    o["g_post_ffn"] = np.ascontiguousarray(np.broadcast_to(g_post_ffn[None, :], (128, D)))
    c = np.arange(128)
    ang = 2 * np.pi * ((c[:, None] * c[None, :]) % 128) / 128
    o["ccsc"] = np.stack([np.cos(ang), np.sin(ang)], 1).astype(np.float64) / math.sqrt(128)
    o["ccsc"] = o["ccsc"].astype(NPBF)
    o["ident"] = np.eye(128, dtype=np.float32).astype(NPBF)
    return {k: (v if v.dtype == NPBF else np.ascontiguousarray(v, dtype=np.float32)) for k, v in o.items()}


def build_program(cfg):
    nc = bass.Bass("TRN2", target_bir_lowering=False)
    parts = cfg["parts"]
    S_ = Sched()

    def din(name, shape, dt=F32):
        return nc.dram_tensor(name, list(shape), dt, kind="ExternalInput").ap()

    def dscr(name, shape, dt):
        return nc.dram_tensor(name, list(shape), dt, kind="Internal").ap()

    W = dict(
        w_in=din("w_in", [128, KC, WIN_COLS]), w_uq=din("w_uq", [128, 2, 2 * NH * DQ]), w_ukv=din("w_ukv", [128, 2, 1024]),
        w_fnet=din("w_fnet", [128, NG, 128]), w_out=din("w_out", [128, KC, D]),
        w_gate=din("w_gate", [NF, 128, KC, 128]), w_up=din("w_up", [NF, 128, KC, 128]), w_down=din("w_down", [NF, 128, D]),
        conv=din("conv", [128, NF, 4]), gcols=din("gcols", [128, 20]), g_post_mix=din("g_post_mix", [128, D]),
        g_post_ffn=din("g_post_ffn", [128, D]), ccsc=din("ccsc", [128, 2, 128], BF16), ident=din("ident", [128, 128], BF16),
    )
    WS = dict(wg=dscr("wgs", [NF, 128, 2, KC * 128], BF16), wd=dscr("wds", [NF, 128, D], BF16),
              w_in=dscr("wins", [128, KC, WIN_COLS], BF16), w_uq=dscr("wuqs", [128, 2, 2 * NH * DQ], BF16),
              w_ukv=dscr("wukvs", [128, 2, 1024], BF16), w_out=dscr("wouts", [128, KC, D], BF16))
    PD = []
    for pi, pc in enumerate(parts):
        PD.append(dict(
            xs=din("xs%d" % pi, [pc["S"], D]), xo=din("xo%d" % pi, [pc["NQ"], D]),
            dft=din("dft%d" % pi, [pc["NT"], 128, 3, 128], BF16), csk=din("csk%d" % pi, [pc["NST"], 64, 512]),
            csq=din("csq%d" % pi, [32, 2, pc["NQ"]], BF16), c2=din("c2own%d" % pi, [128, 2, pc["N2C"]], BF16),
            mask=din("mask%d" % pi, [128, 3]),
            y=nc.dram_tensor("y%d" % pi, [pc["NOWN"], D], F32, kind="ExternalOutput").ap(),
            zd=dscr("zd%d" % pi, [pc["N1"], 128, 1024], BF16), x1s=dscr("x1s%d" % pi, [pc["NOWN"], D], F32),
        ))

    SB_BASE = 16512
    plan = Plan(229344 - SB_BASE)
    ALLPH = set(range(10))

    def PH(pi, *names):
        m = dict(A=0, ATTW=1, ATT=2, B=3, FFN=4)
        return {5 * pi + m[n] for n in names}

    B_ = {}

    def decl(name, shape, dt, live):
        B_[name] = plan.add(name, shape, dt, live)

    decl("ident", [128, 128], BF16, ALLPH)
    decl("ones_bf", [128, 128], BF16, ALLPH)
    decl("ones_f", [128, 64], F32, ALLPH)
    decl("sel", [64, 32], BF16, ALLPH)
    decl("gpm", [128, D], F32, ALLPH)
    decl("gpf", [128, D], F32, ALLPH)
    decl("gcols", [128, 20], F32, ALLPH)
    decl("conv", [128, NF, 4], F32, ALLPH)
    decl("ab", [128, NG, 256], BF16, ALLPH)
    decl("junk", [128, D], BF16, ALLPH)
    decl("sm", [128, 128], F32, ALLPH)
    MIX = lambda pi: PH(pi, "A", "ATT", "B")
    for pi, pc in enumerate(parts):
        S, NQ = pc["S"], pc["NQ"]
        p = "p%d_" % pi
        decl(p + "w_in", [128, KC, WIN_COLS], BF16, PH(pi, "A"))
        decl(p + "w_uq", [128, 2, 2 * NH * DQ], BF16, PH(pi, "ATTW", "ATT"))
        decl(p + "w_ukv", [128, 2, 1024], BF16, PH(pi, "ATTW", "ATT"))
        decl(p + "w_out", [128, KC, D], BF16, PH(pi, "B"))
        decl(p + "wstage", [128, 2, 1536], F32, PH(pi, "A", "ATTW", "B"))
        decl(p + "ckvn", [128, 2, S], BF16, PH(pi, "A", "ATTW", "ATT"))
        decl(p + "kt", [128, 2, S], BF16, PH(pi, "A", "ATTW", "ATT"))
        decl(p + "cqn", [128, 2, NQ], BF16, PH(pi, "A", "ATTW", "ATT"))
        decl(p + "csq", [128, 2, NQ], BF16, PH(pi, "ATTW", "ATT"))
        decl(p + "attnT", [128, 4, NQ], BF16, PH(pi, "ATT", "B"))
        decl(p + "fT", [128, 4, NQ], BF16, PH(pi, "ATT", "B"))
        decl(p + "xst", [128, 2, 4, D], F32, PH(pi, "A"))
        decl(p + "hb", [128, 2, 4, D], BF16, PH(pi, "A"))
        decl(p + "hT", [128, 2, KC, 512], BF16, PH(pi, "A"))
        decl(p + "sq", [128, 2, 512], BF16, PH(pi, "A"))
        decl(p + "rr", [128, 512], F32, PH(pi, "A"))
        decl(p + "ut", [128, 2, NG, 512], BF16, PH(pi, "A"))
        decl(p + "pq", [128, 2, 1024], BF16, PH(pi, "A"))
        decl(p + "zt", [128, 2, 1024], BF16, PH(pi, "A"))
        decl(p + "csk", [64, 2, 512], F32, PH(pi, "A"))
        decl(p + "krt", [64, 512], BF16, PH(pi, "A"))
        decl(p + "dft", [128, 2, 3, 128], BF16, PH(pi, "A"))
        decl(p + "v2", [128, 2, S // 128, 2, DV + 1], BF16, PH(pi, "ATT"))
        decl(p + "qt", [128, NQ], BF16, PH(pi, "ATT"))
        decl(p + "pT", [128, 4, 512], BF16, PH(pi, "ATT"))
        decl(p + "qtmp", [128, 1, 2, 512], F32, PH(pi, "ATT"))
        decl(p + "den", [128, 2, 512], F32, PH(pi, "ATT"))
        if pi == 0:
            decl("pc_st", [128, D], F32, PH(pi, "ATT"))
            decl("pc_ob", [128, 2, D], BF16, PH(pi, "ATT"))
        decl(p + "zk", [128, 2, 1024], BF16, PH(pi, "ATT"))
        decl(p + "c2", [128, 2, pc["N2C"]], BF16, PH(pi, "ATTW", "ATT"))
        decl(p + "xb", [128, 6, D], F32, PH(pi, "B"))
        decl(p + "x1", [128, 6, D], F32, PH(pi, "B"))
        decl(p + "h2b", [128, 6, D], BF16, PH(pi, "B"))
        decl(p + "h2T", [128, KC, NQ], BF16, PH(pi, "B", "FFN"))
        FG = pc["FG"]
        decl(p + "actT", [128, NF, FG], BF16, PH(pi, "FFN"))
        decl(p + "wd", [128, NF, D], BF16, PH(pi, "FFN"))
        decl(p + "wgb", [128, 2, 2, KC * 128], BF16, PH(pi, "FFN"))
        decl(p + "gsb", [128, 2, FG + 2], F32, PH(pi, "FFN"))
        decl(p + "usb", [128, 2, FG], BF16, PH(pi, "FFN"))
        decl(p + "cv", [128, 2, 3, 512], F32, PH(pi, "FFN"))
        decl(p + "nbh", [128, KC, 2], BF16, PH(pi, "FFN"))
        decl(p + "mask", [128, 3], F32, PH(pi, "FFN"))
        decl(p + "x1l", [128, 2, D], F32, PH(pi, "FFN"))
        decl(p + "yo", [128, 2, D], F32, PH(pi, "FFN"))
    plan.solve()

    def sb(name):
        it = B_[name]
        if "h" not in it:
            it["h"] = nc.alloc_sbuf_tensor_at(name, it["shape"], it["dtype"], offset=SB_BASE + it["off"])
        return it["h"]

    es = ExitStack()
    with es:
        PSB = [es.enter_context(nc.psum_tensor("psb%d" % i, [128, 512], F32)) for i in range(8)]
        PT = [Tile("ps%d" % i) for i in range(8)]
        sems_eng = {e: es.enter_context(nc.semaphore("s_" + e)) for e in ENGS}
        sems_dma = [es.enter_context(nc.semaphore("sd%d" % i)) for i in range(NDMA)]
        block = es.enter_context(nc.Block())

        add = S_.add
        dma = S_.dma

        class Rot:
            def __init__(self, banks):
                self.b = list(banks)
                self.i = 0

            def next(self):
                b = self.b[self.i % len(self.b)]
                self.i += 1
                return b

        sm = sb("sm")
        sm_t = [Tile("sm%d" % i) for i in range(8)]
        sm_i = [0]

        def sm_slot():
            i = sm_i[0] % 8
            sm_i[0] += 1
            return sm[:, i * 16:(i + 1) * 16], sm_t[i]

        ident = sb("ident"); ones_bf = sb("ones_bf"); ones_f = sb("ones_f"); gpm = sb("gpm"); gpf = sb("gpf")
        gcols = sb("gcols"); convw = sb("conv"); ab = sb("ab"); junk = sb("junk")
        T_const = Tile("const")
        T_junk = Tile("junk")
        sel = sb("sel")
        T_sel = Tile("sel")
        T_ident = Tile("ident")
        dma(lambda e: e.dma_start(out=ident[:], in_=W["ident"]), writes=[T_ident])
        dma(lambda e: e.dma_start(out=gpm[:], in_=W["g_post_mix"]))
        dma(lambda e: e.dma_start(out=gpf[:], in_=W["g_post_ffn"]))
        dma(lambda e: e.dma_start(out=gcols[:], in_=W["gcols"]))
        dma(lambda e: e.dma_start(out=convw[:], in_=W["conv"]))
        add("dve", lambda e: e.tensor_copy(out=sel[0:32, :], in_=ident[0:32, 0:32]), reads=[T_ident], writes=[T_sel])
        add("dve", lambda e: e.tensor_copy(out=sel[32:64, :], in_=ident[32:64, 32:64]), reads=[T_ident], writes=[T_sel])
        add("pool", lambda e: e.memset(ones_bf[:], 1.0))
        add("pool", lambda e: e.memset(ones_f[:], 1.0))
        wst0 = sb("p0_wstage")
        ccsc_sb = sb("p0_sq")
        T_ab = Tile("ab")
        o1 = dma(lambda e: e.dma_start(out=ccsc_sb[:, 0, 0:256], in_=W["ccsc"].rearrange("p a b -> p (a b)")))
        o2 = dma(lambda e: e.dma_start(out=wst0[:, 0, 0:512], in_=W["w_fnet"].rearrange("p a b -> p (a b)")))
        o3 = add("dve", lambda e: e.tensor_copy(out=ccsc_sb[:, 1, :], in_=wst0[:, 0, 0:512]), deps=[o2])
        for g in range(NG):
            for cs in range(2):
                add("pe", lambda e, g=g, cs=cs: e.matmul(PSB[cs][:, g * 128:(g + 1) * 128], lhsT=ccsc_sb[:, 0, cs * 128:(cs + 1) * 128],
                                                         rhs=ccsc_sb[:, 1, g * 128:(g + 1) * 128], start=True, stop=True),
                    writes=[PT[cs]], deps=[o1, o3])
        for cs in range(2):
            add("dve", lambda e, cs=cs: e.tensor_copy(out=ab[:, :, cs * 128:(cs + 1) * 128],
                                                      in_=PSB[cs][:, :].rearrange("p (g d) -> p g d", g=NG)),
                writes=[PT[cs], T_ab])
        S_.barrier()

        def do_part(pi, pc):
            S, NQ, NOWN, N1, NJ, NT, NST = pc["S"], pc["NQ"], pc["NOWN"], pc["N1"], pc["NJ"], pc["NT"], pc["NST"]
            NK2, N2C, FG, NFG = pc["NK2"], pc["N2C"], pc["FG"], pc["NFG"]
            P = PD[pi]
            p = "p%d_" % pi
            NKC = S // 128
            blocks = [(b * 512, 512) for b in range(NOWN // 512)] + [(NOWN, 2)]

            wstage = sb(p + "wstage")
            T_wst = [Tile(), Tile()]
            wst_i = [0]

            def load_cast(dst, src, ncols, gcol0=None, nkc=None, eng="dve", T_dst=None):
                ops = []
                per = max(1, 1536 // ncols)
                for k0 in range(0, nkc, per):
                    kn = min(per, nkc - k0)
                    i = wst_i[0] % 2
                    wst_i[0] += 1
                    dma(lambda e, i=i, k0=k0, kn=kn: e.dma_start(
                        out=wstage[:, i, 0:kn * ncols].rearrange("p (k c) -> p k c", k=kn), in_=src[:, k0:k0 + kn, :]), writes=[T_wst[i]])
                    for k in range(kn):
                        if gcol0 is None:
                            ops.append(add(eng, lambda e, i=i, k=k, k0=k0: e.tensor_copy(
                                out=dst[:, k0 + k, :], in_=wstage[:, i, k * ncols:(k + 1) * ncols]), reads=[T_wst[i]],
                                writes=([T_dst[k0 + k]] if T_dst else [])))
                        else:
                            ops.append(add(eng, lambda e, i=i, k=k, k0=k0: e.tensor_scalar(
                                out=dst[:, k0 + k, :], in0=wstage[:, i, k * ncols:(k + 1) * ncols],
                                scalar1=gcols[:, gcol0 + k0 + k:gcol0 + k0 + k + 1], scalar2=None, op0=ALU.mult), reads=[T_wst[i]],
                                writes=([T_dst[k0 + k]] if T_dst else [])))
                return ops

            w_in = sb(p + "w_in")
            T_win = [Tile() for _ in range(KC)]
            if pi == 0:
                load_cast(w_in, W["w_in"], WIN_COLS, gcol0=0, nkc=KC, T_dst=T_win)
                add("pool", lambda e: e.dma_start(out=WS["w_in"], in_=w_in[:, :, :]), reads=T_win, is_dma=True)
            else:
                dma(lambda e: e.dma_start(out=w_in[:, :, :], in_=WS["w_in"]), writes=T_win)

            ckvn = sb(p + "ckvn"); kt = sb(p + "kt"); cqn = sb(p + "cqn")
            xst = sb(p + "xst"); hb = sb(p + "hb"); hT = sb(p + "hT"); sq = sb(p + "sq"); rr = sb(p + "rr")
            ut = sb(p + "ut"); pq = sb(p + "pq"); zt = sb(p + "zt"); csk = sb(p + "csk"); krt = sb(p + "krt"); dftb = sb(p + "dft")
            T_xst = [[Tile() for _ in range(4)] for _ in range(2)]
            T_hb = [[Tile() for _ in range(4)] for _ in range(2)]
            T_hT = [[Tile() for _ in range(4)] for _ in range(2)]
            T_sq = Tile(); T_rr = Tile(); T_ut = [[Tile() for _ in range(NG)] for _ in range(2)]
            T_pq = [Tile(), Tile()]; T_zt = [Tile(), Tile()]; T_csk = [Tile(), Tile()]; T_krt = Tile(); T_dft = [Tile(), Tile()]
            T_ckvn = [Tile() for _ in range(NST)]
            T_ktr = [[Tile() for _ in range(NST)] for _ in range(2)]
            T_ktn = [[Tile() for _ in range(NST)] for _ in range(2)]
            T_cqn = [Tile() for _ in range(len(blocks))]
            rot_tp = Rot([0, 1])
            rot_cv = Rot([2, 3])
            rot_pj = Rot([4, 5, 6, 7])

            def stage_load(i, tl):
                par = i % 2
                for j, (src, nt) in enumerate(tl):
                    dma(lambda e, src=src, nt=nt, j=j: e.dma_start(out=xst[0:nt, par, j, :], in_=src), writes=[T_xst[par][j]])

            def stage_norm(i, tl):
                par = i % 2
                sl, T_s = sm_slot()
                ntm = tl[0][1]
                n = len(tl)
                items = []
                for j, (src, nt) in enumerate(tl):
                    items.append(lambda nt=nt, j=j: add("act", lambda e: e.activation(out=junk[0:nt, :], in_=xst[0:nt, par, j, :], func=AF.Square,
                                                                                    accum_out=sl[0:nt, j:j + 1]), reads=[T_xst[par][j]], writes=[T_s]))

                def lnexp():
                    add("act", lambda e: e.activation(out=sl[0:ntm, 4:4 + n], in_=sl[0:ntm, 0:n], func=AF.Ln, scale=1.0 / D, bias=EPS), writes=[T_s])
                    add("act", lambda e: e.activation(out=sl[0:ntm, 8:8 + n], in_=sl[0:ntm, 4:4 + n], func=AF.Exp, scale=-0.5), writes=[T_s])
                items.append(lnexp)
                for j, (src, nt) in enumerate(tl):
                    items.append(lambda nt=nt, j=j: add("act", lambda e: e.activation(out=hb[0:nt, par, j, :], in_=xst[0:nt, par, j, :], func=AF.Copy,
                                                                                    scale=sl[0:nt, 8 + j:9 + j]), reads=[T_xst[par][j], T_s], writes=[T_hb[par][j]]))
                return items

            def transpose_rows(h_src, T_h, nt, hT_dst_cols, T_dst, eng="act"):
                b = rot_tp.next()
                pb = PSB[b][:, :].bitcast(BF16)
                for c in range(KC):
                    add("pe", lambda e, c=c: e.transpose(out=pb[:, c * 128:c * 128 + nt], in_=h_src[0:nt, c * 128:(c + 1) * 128],
                                                         identity=ident[0:nt, 0:nt]), reads=[T_h], writes=[PT[b]])
                src = pb.rearrange("p (c t) -> p c t", c=KC)[:, :, 0:nt]
                if eng == "act":
                    add("act", lambda e: e.copy(out=hT_dst_cols, in_=src), writes=[PT[b], T_dst])
                else:
                    add("dve", lambda e: e.tensor_copy(out=hT_dst_cols, in_=src), writes=[PT[b], T_dst])

            def stage_tr(i, tl):
                par = i % 2
                items = []
                for j, (src, nt) in enumerate(tl):
                    items.append(lambda j=j, nt=nt: transpose_rows(hb[:, par, j, :], T_hb[par][j], nt, hT[:, par, :, j * 128:j * 128 + nt],
                                                                   T_hT[par][j], eng=("act" if j % 2 == 0 else "dve")))
                return items

            def feat_rmsnorm(ps_banks, n, dst_fn, T_dst_list, width):
                for m in range(2):
                    add("act", lambda e, m=m: e.activation(out=sq[:, m, 0:n], in_=PSB[ps_banks[m]][:, 0:n], func=AF.Square),
                        writes=[PT[ps_banks[m]], T_sq])

                def tail():
                    b = rot_pj.next()
                    for m in range(2):
                        add("pe", lambda e, m=m: e.matmul(PSB[b][:, 0:n], lhsT=ones_bf[:, :], rhs=sq[:, m, 0:n], start=(m == 0), stop=(m == 1)),
                            reads=[T_sq], writes=[PT[b]])
                    add("act", lambda e: e.activation(out=rr[:, 0:n], in_=PSB[b][:, 0:n], func=AF.Ln, scale=1.0 / width, bias=EPS),
                        writes=[PT[b], T_rr])
                    add("act", lambda e: e.activation(out=rr[:, 0:n], in_=rr[:, 0:n], func=AF.Exp, scale=-0.5), writes=[T_rr])
                    for m in range(2):
                        add("dve", lambda e, m=m: e.tensor_tensor(out=dst_fn(m), in0=PSB[ps_banks[m]][:, 0:n], in1=rr[:, 0:n], op=ALU.mult),
                            reads=[T_rr], writes=[PT[ps_banks[m]]] + T_dst_list)
                return tail

            def proj(bank, ncols_out, col0, n, rhs_fn, reads):
                for k in range(KC):
                    add("pe", lambda e, k=k: e.matmul(PSB[bank][0:ncols_out, 0:n], lhsT=w_in[:, k, col0:col0 + ncols_out], rhs=rhs_fn(k),
                                                      start=(k == 0), stop=(k == KC - 1)), reads=list(reads) + [T_win[k]], writes=[PT[bank]])

            def stage_proj(st):
                par = st % 2
                ci = st % 2
                dma(lambda e: e.dma_start(out=csk[:, ci, :], in_=P["csk"][st]), writes=[T_csk[ci]])
                bk = [rot_cv.next(), rot_cv.next()]
                groups = []
                state = {}

                def g_ckv(m):
                    def w():
                        proj(bk[m], 128, QL + m * 128, 512, lambda k: hT[:, par, k, :], T_hT[par])
                        if m == 1:
                            state["tail1"] = feat_rmsnorm(bk, 512, lambda mm: ckvn[:, mm, st * 512:(st + 1) * 512], [T_ckvn[st]], KVL)
                    return w
                groups.append(g_ckv(0))
                groups.append(g_ckv(1))

                def g_kr():
                    b = rot_pj.next()
                    proj(b, 64, QL + KVL, 512, lambda k: hT[:, par, k, :], T_hT[par])
                    add("dve", lambda e: e.tensor_tensor(out=krt[:, :], in0=PSB[b][0:64, :], in1=csk[:, ci, :], op=ALU.mult),
                        reads=[T_csk[ci]], writes=[PT[b], T_krt])
                groups.append(g_kr)

                def g_u(g):
                    def w():
                        b = rot_pj.next()
                        proj(b, 128, QL + KVL + 2 * DR + g * 128, 512, lambda k: hT[:, par, k, :], T_hT[par])
                        if g % 2 == 0:
                            add("act", lambda e: e.copy(out=ut[:, par, g, :], in_=PSB[b][:, :]), writes=[PT[b], T_ut[par][g]])
                        else:
                            add("dve", lambda e: e.tensor_copy(out=ut[:, par, g, :], in_=PSB[b][:, :]), writes=[PT[b], T_ut[par][g]])
                    return w
                for g in range(NG):
                    groups.append(g_u(g))

                def tail():
                    b2 = rot_pj.next()
                    add("pe", lambda e: e.matmul(PSB[b2][0:32, :], lhsT=sel[0:64, :], rhs=krt[:, :], start=True, stop=True),
                        reads=[T_krt, T_sel], writes=[PT[b2]])
                    add("dve", lambda e: e.tensor_copy(out=kt[64:96, 0, st * 512:(st + 1) * 512], in_=PSB[b2][0:32, :]),
                        writes=[PT[b2], T_ktr[0][st]])
                    add("act", lambda e: e.copy(out=kt[64:96, 1, st * 512:(st + 1) * 512], in_=PSB[b2][0:32, :]),
                        writes=[PT[b2], T_ktr[1][st]])
                groups.append(tail)
                groups.append(lambda: state["tail1"]())
                return groups

            def stage_fnet(st):
                par = st % 2
                pqs, zs = [], []
                for j in range(4):
                    pqs.append(lambda j=j: fnet_pq(st, par, j))
                    zs.append(lambda j=j: fnet_z(st, par, j))
                return [pqs[0], pqs[1], zs[0], pqs[2], zs[1], pqs[3], zs[2], zs[3]]

            def fnet_pq(st, par, j):
                if True:
                    t = st * 4 + j
                    di = t % 2
                    dma(lambda e, di=di, t=t: e.dma_start(out=dftb[:, di, :, :], in_=P["dft"][t]), writes=[T_dft[di]])
                    bb = [rot_pj.next(), rot_pj.next()]
                    for g in range(NG):
                        bsel = bb[g // 2]
                        add("pe", lambda e, g=g, j=j, bsel=bsel: e.matmul(PSB[bsel][:, (g % 2) * 256:(g % 2) * 256 + 256],
                                                                          lhsT=ut[:, par, g, j * 128:(j + 1) * 128], rhs=ab[:, g, :], start=True, stop=True),
                            reads=[T_ut[par][g], T_ab], writes=[PT[bsel]])
                    for h2 in range(2):
                        add("dve", lambda e, h2=h2, di=di, bb=bb: e.tensor_copy(
                            out=pq[:, di, :].rearrange("p (a g d) -> p g a d", a=2, g=NG)[:, 2 * h2:2 * h2 + 2, :, :],
                            in_=PSB[bb[h2]][:, :].rearrange("p (g a d) -> p g a d", g=2, a=2)), writes=[PT[bb[h2]], T_pq[di]])

            def fnet_z(st, par, j):
                if True:
                    t = st * 4 + j
                    di = t % 2
                    if True:
                        zb = [rot_pj.next(), rot_pj.next()]
                        Pv = pq[:, di, 0:512]
                        Qv = pq[:, di, 512:1024]
                        seq = [(zb[0], 0, Pv, True, False), (zb[1], 0, Qv, True, False), (zb[1], 1, Pv, False, True), (zb[0], 2, Qv, False, True)]
                        for (zbk, mi, rhs, st_, sp_) in seq:
                            add("pe", lambda e, zbk=zbk, mi=mi, rhs=rhs, st_=st_, sp_=sp_: e.matmul(
                                PSB[zbk][:, :], lhsT=dftb[:, di, mi, :], rhs=rhs, start=st_, stop=sp_),
                                reads=[T_dft[di], T_pq[di]], writes=[PT[zbk]])
                        add("act", lambda e: e.copy(out=zt[:, di, 0:512], in_=PSB[zb[0]][:, :]), writes=[PT[zb[0]], T_zt[di]])
                        add("dve", lambda e: e.tensor_copy(out=zt[:, di, 512:1024], in_=PSB[zb[1]][:, :]), writes=[PT[zb[1]], T_zt[di]])
                        for jj in range(NJ):
                            add("pool", lambda e, jj=jj: e.dma_start(out=P["zd"][:, NJ * t + jj, :], in_=zt[jj * N1:(jj + 1) * N1, di, :]),
                                reads=[T_zt[di]], is_dma=True)

            seq_tl = [[(P["xs"][(st * 4 + j) * 128:(st * 4 + j + 1) * 128, :], 128) for j in range(4)] for st in range(NST)]
            own_tl = []
            for (c0, n) in blocks:
                tl = []
                for j in range((n + 127) // 128):
                    nt = min(128, n - j * 128)
                    tl.append((P["xo"][c0 + j * 128:c0 + j * 128 + nt, :], nt))
                own_tl.append(tl)
            all_tl = seq_tl + own_tl
            NU = len(all_tl)

            def stage_cq(bi, par):
                c0, n = blocks[bi]
                bk = [rot_cv.next(), rot_cv.next()]
                state = {}

                def g(m):
                    def w():
                        proj(bk[m], 128, m * 128, n, lambda k: hT[:, par, k, 0:n], T_hT[par])
                        if m == 1:
                            state["t"] = feat_rmsnorm(bk, n, lambda mm: cqn[:, mm, c0:c0 + n], [T_cqn[bi]], QL)
                    return w
                return [g(0), g(1), lambda: state["t"]()]

            stage_load(0, all_tl[0])
            for it in range(NU + 3):
                if it + 1 < NU:
                    stage_load(it + 1, all_tl[it + 1])
                tr = stage_tr(it - 1, all_tl[it - 1]) if 0 <= it - 1 < NU else []
                u2 = it - 2
                if 0 <= u2 < NST:
                    ga = stage_proj(u2)
                    ga_slots = (1, 3, 6, 9, 13, 17, 19, 15, 11)
                elif NST <= u2 < NU:
                    ga = stage_cq(u2 - NST, u2 % 2)
                    ga_slots = (1, 3, 11)
                else:
                    ga, ga_slots = [], ()
                gb = stage_fnet(it - 3) if 0 <= it - 3 < NST else []
                sched_ = []
                for sl_, w_ in zip((0, 5, 9.5, 13.5), tr):
                    sched_.append((sl_, w_))
                for sl_, w_ in zip(ga_slots, ga):
                    sched_.append((sl_, w_))
                for sl_, w_ in zip((4, 7, 10, 14, 16, 18, 20, 21), gb):
                    sched_.append((sl_, w_))
                if it < NU:
                    ni = stage_norm(it, all_tl[it])
                    nsq = (len(ni) - 1) // 2
                    for sl_, w_ in zip((1.5, 3.5, 5.5, 7.5)[:nsq], ni[:nsq]):
                        sched_.append((sl_, w_))
                    sched_.append((9.2, ni[nsq]))
                    for sl_, w_ in zip((11.5, 13.2, 15.5, 17.5)[:nsq], ni[nsq + 1:]):
                        sched_.append((sl_, w_))
                for _, w_ in sorted(sched_, key=lambda x_: x_[0]):
                    w_()
            S_.barrier()

            w_uq = sb(p + "w_uq"); w_ukv = sb(p + "w_ukv"); csq = sb(p + "csq")
            if pi == 0:
                load_cast(w_uq, W["w_uq"], 2 * NH * DQ, gcol0=16, nkc=2)
                load_cast(w_ukv, W["w_ukv"], 1024, gcol0=18, nkc=2)
            else:
                dma(lambda e: e.dma_start(out=w_uq[:, :, :], in_=WS["w_uq"]))
                dma(lambda e: e.dma_start(out=w_ukv[:, :, :], in_=WS["w_ukv"]))
            dma(lambda e: e.dma_start(out=csq[64:96, :, :], in_=P["csq"]))
            c2 = sb(p + "c2")
            dma(lambda e: e.dma_start(out=c2[:, :, :], in_=P["c2"]))
            S_.barrier()
            if pi == 0:
                dma(lambda e: e.dma_start(out=WS["w_uq"], in_=w_uq[:, :, :]))
                dma(lambda e: e.dma_start(out=WS["w_ukv"], in_=w_ukv[:, :, :]))
            v2 = sb(p + "v2"); qt = sb(p + "qt"); pT = sb(p + "pT"); qtmp = sb(p + "qtmp"); den = sb(p + "den"); rb = den
            attnT = sb(p + "attnT"); fT = sb(p + "fT"); zk = sb(p + "zk")
            T_v2 = [[Tile() for _ in range(NKC // 4)] for _ in range(2)]
            add("pool", lambda e: e.memset(v2[:, :, :, :, DV:DV + 1].rearrange("p a k x o -> p (a k x) o"), 1.0),
                writes=[t_ for l_ in T_v2 for t_ in l_])
            T_zk = [Tile(), Tile()]
            if pi == 0:
                pst = sb("pc_st"); pob = sb("pc_ob")
                T_pst = Tile(); T_pob = [Tile(), Tile()]
                kk_ = 0
                for f in range(NF):
                    jobs = ((W["w_gate"][f].rearrange("p k c -> p (k c)"), WS["wg"][f][:, 0, :], True),
                            (W["w_up"][f].rearrange("p k c -> p (k c)"), WS["wg"][f][:, 1, :], True),
                            (W["w_down"][f], WS["wd"][f], False))
                    for (src_, dst_, sc_) in jobs:
                        oi = kk_ % 2
                        kk_ += 1
                        add("pool", lambda e, src_=src_: e.dma_start(out=pst[:, :], in_=src_), writes=[T_pst], is_dma=True)
                        if sc_:
                            add("pool", lambda e, oi=oi: e.tensor_tensor(
                                out=pob[:, oi, :].rearrange("p (k c) -> p k c", k=KC), in0=pst[:, :].rearrange("p (k c) -> p k c", k=KC),
                                in1=gcols[:, 8:16].unsqueeze(2).to_broadcast([128, KC, 128]), op=ALU.mult), reads=[T_pst], writes=[T_pob[oi]])
                        else:
                            add("pool", lambda e, oi=oi: e.tensor_copy(out=pob[:, oi, :], in_=pst[:, :]), reads=[T_pst], writes=[T_pob[oi]])
                        add("pool", lambda e, oi=oi, dst_=dst_: e.dma_start(out=dst_, in_=pob[:, oi, :]), reads=[T_pob[oi]], is_dma=True)

            def gen_F(k1):
                def w():
                    zi = k1 % 2
                    dma(lambda e: e.dma_start(out=zk[:, zi, :], in_=P["zd"][k1]), writes=[T_zk[zi]])
                    b = rot_g.next()
                    for c in range(4):
                        for ri in range(2):
                            add("pe", lambda e, c=c, ri=ri: e.matmul(
                                PSB[b][:, c * N2C:(c + 1) * N2C], lhsT=zk[:, zi, ri * 512 + c * 128:ri * 512 + (c + 1) * 128],
                                rhs=c2[:, ri, :], start=(ri == 0), stop=(ri == 1)), reads=[T_zk[zi]], writes=[PT[b]])
                    src = PSB[b][:, 0:4 * N2C].rearrange("p (c k) -> p c k", c=4)
                    add("dve", lambda e: e.tensor_copy(
                        out=fT[:, :, 0:NOWN].rearrange("p c (k a) -> p c k a", a=N1)[:, :, :, k1], in_=src[:, :, 1:1 + NK2]), writes=[PT[b]])
                    if k1 == N1 - 1:
                        add("dve", lambda e: e.tensor_copy(out=fT[:, :, NOWN:NOWN + 1], in_=src[:, :, 0:1]), writes=[PT[b]])
                    if k1 == 0:
                        add("dve", lambda e: e.tensor_copy(out=fT[:, :, NOWN + 1:NOWN + 2], in_=src[:, :, N2C - 1:N2C]), writes=[PT[b]])
                return w
            f_items = [gen_F(k1) for k1 in range(N1)]
            T_qt = [Tile() for _ in range(len(blocks))]
            T_pT = [Tile() for _ in range(4)]
            T_qtmp = [Tile(), Tile()]; T_den = [Tile(), Tile()]; T_rb = [Tile(), Tile()]
            T_attn = [[Tile() for _ in blocks] for _ in range(4)]
            rot_g = Rot([0, 1])
            rot_s = Rot([2, 3, 4, 5])
            rot_o = Rot([6, 7])
            pT_i = [0]
            qi_ = [0]

            def gen_K(h):
                hp = h % 2
                items = []
                for st in range(NST):
                    def w(st=st):
                        b = rot_g.next()
                        for m in range(2):
                            add("pe", lambda e, m=m: e.matmul(PSB[b][0:DN, :], lhsT=w_ukv[:, m, h * DN:(h + 1) * DN],
                                                             rhs=ckvn[:, m, st * 512:(st + 1) * 512], start=(m == 0), stop=(m == 1)),
                                reads=[T_ckvn[st]], writes=[PT[b]])
                        add("dve", lambda e: e.tensor_copy(out=kt[0:DN, hp, st * 512:(st + 1) * 512], in_=PSB[b][0:DN, :]),
                            writes=[PT[b], T_ktn[hp][st]])
                    items.append(w)
                return items

            def gen_V(pr):
                vp = pr % 2
                items = []
                for kg in range(NKC // 4):
                    def w(kg=kg):
                        b = rot_g.next()
                        for kk in range(4):
                            kc = kg * 4 + kk
                            for m in range(2):
                                add("pe", lambda e, m=m, kc=kc, kk=kk: e.matmul(
                                    PSB[b][:, kk * 128:(kk + 1) * 128], lhsT=ckvn[:, m, kc * 128:(kc + 1) * 128],
                                    rhs=w_ukv[:, m, 512 + pr * 128:512 + (pr + 1) * 128], start=(m == 0), stop=(m == 1)),
                                    reads=[T_ckvn[kc // 4]], writes=[PT[b]])
                        for x2 in range(2):
                            add("dve", lambda e, x2=x2: e.tensor_copy(
                                out=v2[:, vp, kg * 4:(kg + 1) * 4, x2, 0:DV],
                                in_=PSB[b][:, :].rearrange("p (k x d) -> p k x d", k=4, x=2)[:, :, x2, :]), writes=[PT[b], T_v2[vp][kg]])
                    items.append(w)
                return items

            def gen_Q(h, bi):
                c0, n = blocks[bi]

                def w():
                    ba, bb2 = rot_g.next(), rot_g.next()
                    qi = 0
                    for (bk_, off) in ((ba, 0), (bb2, NH * DQ)):
                        for m in range(2):
                            add("pe", lambda e, bk_=bk_, off=off, m=m: e.matmul(
                                PSB[bk_][0:DQ, 0:n], lhsT=w_uq[:, m, off + h * DQ:off + (h + 1) * DQ], rhs=cqn[:, m, c0:c0 + n],
                                start=(m == 0), stop=(m == 1)), reads=[T_cqn[bi]], writes=[PT[bk_]])
                    add("dve", lambda e: e.tensor_copy(out=qt[0:DN, c0:c0 + n], in_=PSB[ba][0:DN, 0:n]), writes=[PT[ba], T_qt[bi]])
                    add("dve", lambda e: e.tensor_tensor(out=qtmp[64:96, qi, 0, 0:n], in0=PSB[ba][64:96, 0:n],
                                                         in1=csq[64:96, 0, c0:c0 + n], op=ALU.mult), writes=[PT[ba], T_qtmp[qi]])
                    add("dve", lambda e: e.tensor_tensor(out=qtmp[64:96, qi, 1, 0:n], in0=PSB[bb2][64:96, 0:n],
                                                         in1=csq[64:96, 1, c0:c0 + n], op=ALU.mult), writes=[PT[bb2], T_qtmp[qi]])
                    add("dve", lambda e: e.tensor_tensor(out=qt[64:96, c0:c0 + n], in0=qtmp[64:96, qi, 0, 0:n],
                                                          in1=qtmp[64:96, qi, 1, 0:n], op=ALU.add), reads=[T_qtmp[qi]], writes=[T_qt[bi]])
                return [w]

            rot_g.b = [0, 1, 2, 3, 4, 5, 6, 7]
            for w in gen_K(0) + gen_V(0) + gen_Q(0, 0):
                w()
            rot_g.b = [0, 1]
            rot_g.i = 0
            units = [(h, bi) for h in range(NH) for bi in range(len(blocks))]
            nfull = len(blocks) - 1
            carry = []
            fin_i = [0]
            for ui, (h, bi) in enumerate(units):
                hh = h % 2
                hp = h % 2
                vp = (h // 2) % 2
                c0, n = blocks[bi]
                todo = list(carry)
                carry = []
                if ui + 1 < len(units):
                    todo += gen_Q(*units[ui + 1])
                if bi < nfull:
                    perf = (N1 + NH * nfull - 1) // (NH * nfull)
                    for _ in range(perf):
                        if f_items:
                            todo.append(f_items.pop(0))
                if h + 1 < NH and bi < nfull:
                    ks = gen_K(h + 1)
                    per = (len(ks) + nfull - 1) // nfull
                    todo += ks[bi * per:(bi + 1) * per]
                    if hh == 1:
                        vs = gen_V(h // 2 + 1)
                        per = (len(vs) + nfull - 1) // nfull
                        todo += vs[bi * per:(bi + 1) * per]
                G = max(1, min(NKC, 512 // n))
                ngrp = NKC // G
                ob = rot_o.next()
                pend = []

                def pv(gi, slot, G=G, n=n, ob=ob, hh=hh, vp=vp):
                    for kk in range(G):
                        kc = gi * G + kk
                        add("pe", lambda e, kc=kc, kk=kk: e.matmul(
                            PSB[ob][0:DV + 1, 0:n], lhsT=v2[:, vp, kc, hh, :], rhs=pT[:, slot, kk * n:(kk + 1) * n],
                            start=(kc == 0), stop=(kc == NKC - 1)), reads=[T_v2[vp][kc // 4], T_pT[slot]], writes=[PT[ob]])

                every = max(1, (ngrp - 3) // max(1, len(todo))) if ngrp > 4 else 1
                for gi in range(ngrp):
                    sbk = rot_s.next()
                    for kk in range(G):
                        kc = gi * G + kk
                        add("pe", lambda e, sbk=sbk, kc=kc, kk=kk, n=n, hp=hp, c0=c0: e.matmul(
                            PSB[sbk][:, kk * n:(kk + 1) * n], lhsT=kt[0:DQ, hp, kc * 128:(kc + 1) * 128], rhs=qt[0:DQ, c0:c0 + n],
                            start=True, stop=True), reads=[T_ktn[hp][kc // 4], T_ktr[hp][kc // 4], T_qt[bi]], writes=[PT[sbk]])
                    slot = pT_i[0] % 4
                    pT_i[0] += 1
                    add("act", lambda e, sbk=sbk, slot=slot, G=G, n=n: e.activation(out=pT[:, slot, 0:G * n], in_=PSB[sbk][:, 0:G * n],
                                                                                    func=AF.Exp, scale=SCALE), writes=[PT[sbk], T_pT[slot]])
                    pend.append((gi, slot))
                    if len(pend) > 2:
                        pv(*pend.pop(0))
                    if todo and gi >= 2 and (gi - 2) % every == 0:
                        todo.pop(0)()
                while pend:
                    pv(*pend.pop(0))
                while todo:
                    todo.pop(0)()
                fi = fin_i[0] % 2
                fin_i[0] += 1
                add("dve", lambda e, ob=ob, n=n, fi=fi: e.reciprocal(out=den[64:65, fi, 0:n], in_=PSB[ob][64:65, 0:n]), writes=[PT[ob], T_den[fi]])

                def fin(ob=ob, n=n, c0=c0, h=h, hh=hh, bi=bi, fi=fi):
                    bb3 = rot_g.next()
                    add("pe", lambda e: e.matmul(PSB[bb3][0:DV, 0:n], lhsT=ones_f[64:65, 0:DV], rhs=den[64:65, fi, 0:n], start=True, stop=True),
                        reads=[T_den[fi]], writes=[PT[bb3]])
                    add("dve", lambda e: e.tensor_copy(out=rb[0:DV, fi, 0:n], in_=PSB[bb3][0:DV, 0:n]), writes=[PT[bb3], T_rb[fi]])
                    add("dve", lambda e: e.tensor_tensor(
                        out=attnT[hh * 64:(hh + 1) * 64, h // 2, c0:c0 + n], in0=PSB[ob][0:DV, 0:n], in1=rb[0:DV, fi, 0:n], op=ALU.mult),
                        reads=[T_rb[fi]], writes=[PT[ob], T_attn[h // 2][bi]])
                carry = [fin]
            for w in carry + f_items:
                w()
            S_.barrier()

            w_out = sb(p + "w_out")
            T_wout = [Tile() for _ in range(KC)]
            if pi == 0:
                load_cast(w_out, W["w_out"], D, gcol0=None, nkc=KC, T_dst=T_wout)
                add("pool", lambda e: e.dma_start(out=WS["w_out"], in_=w_out[:, :, :]), reads=T_wout, is_dma=True)
            else:
                dma(lambda e: e.dma_start(out=w_out[:, :, :], in_=WS["w_out"]), writes=T_wout)
            xb = sb(p + "xb"); x1 = sb(p + "x1"); h2b = sb(p + "h2b"); h2T = sb(p + "h2T")
            NSL = 6
            T_xb = [Tile() for _ in range(NSL)]; T_x1 = [Tile() for _ in range(NSL)]; T_h2b = [Tile() for _ in range(NSL)]
            T_h2T = [Tile() for _ in range((NQ + 127) // 128)]
            rot_y = Rot([0, 1, 2, 3, 4, 5])
            rot_tp = Rot([6, 7])
            tiles = [(r0, min(128, NOWN - r0)) for r0 in range(0, NOWN, 128)] + [(NOWN, 2)]
            ybs = {}

            def b_mm(ti):
                r0, nt = tiles[ti]
                i2 = ti % NSL
                bi = min(r0 // 512, len(blocks) - 1) if r0 < NOWN else len(blocks) - 1
                dma(lambda e: e.dma_start(out=xb[0:nt, i2, :], in_=P["xo"][r0:r0 + nt, :]), writes=[T_xb[i2]])
                yb = [rot_y.next(), rot_y.next()]
                ybs[ti] = yb
                for hf in range(2):
                    for c in range(8):
                        src_t = attnT[:, c, r0:r0 + nt] if c < 4 else fT[:, c - 4, r0:r0 + nt]
                        add("pe", lambda e, hf=hf, c=c, src_t=src_t: e.matmul(
                            PSB[yb[hf]][0:nt, :], lhsT=src_t, rhs=w_out[:, c, hf * 512:(hf + 1) * 512], start=(c == 0), stop=(c == 7)),
                            reads=[T_attn[cc][bi] for cc in range(4)] + [T_wout[c]], writes=[PT[yb[hf]]])

            sls = {}

            def b_epi_a(ti):
                r0, nt = tiles[ti]
                yb = ybs[ti]
                sl, T_s = sm_slot()
                sls[ti] = (sl, T_s)
                for hf in range(2):
                    add("act", lambda e, hf=hf: e.activation(out=junk[0:nt, 0:512], in_=PSB[yb[hf]][0:nt, :], func=AF.Square,
                                                             accum_out=sl[0:nt, hf:hf + 1]), writes=[PT[yb[hf]], T_s])
                add("dve", lambda e: e.tensor_tensor(out=sl[0:nt, 2:3], in0=sl[0:nt, 0:1], in1=sl[0:nt, 1:2], op=ALU.add), writes=[T_s])
                add("act", lambda e: e.activation(out=sl[0:nt, 3:4], in_=sl[0:nt, 2:3], func=AF.Ln, scale=1.0 / D, bias=EPS), writes=[T_s])
                add("act", lambda e: e.activation(out=sl[0:nt, 4:5], in_=sl[0:nt, 3:4], func=AF.Exp, scale=-0.5), writes=[T_s])

            def b_epi_b(ti):
                r0, nt = tiles[ti]
                i2 = ti % NSL
                yb = ybs[ti]
                sl, T_s = sls[ti]
                for hf in range(2):
                    add("dve", lambda e, hf=hf: e.scalar_tensor_tensor(
                        out=x1[0:nt, i2, hf * 512:(hf + 1) * 512], in0=PSB[yb[hf]][0:nt, :], scalar=sl[0:nt, 4:5],
                        in1=gpm[0:nt, hf * 512:(hf + 1) * 512], op0=ALU.mult, op1=ALU.mult), reads=[T_s], writes=[PT[yb[hf]], T_x1[i2]])
                add("dve", lambda e: e.tensor_tensor(out=x1[0:nt, i2, :], in0=x1[0:nt, i2, :], in1=xb[0:nt, i2, :], op=ALU.add),
                    reads=[T_xb[i2]], writes=[T_x1[i2]])
                if r0 < NOWN:
                    add("pool", lambda e: e.dma_start(out=P["x1s"][r0:r0 + nt, :], in_=x1[0:nt, i2, :]), reads=[T_x1[i2]], is_dma=True)

            def b_epi_c(ti):
                r0, nt = tiles[ti]
                i2 = ti % NSL
                sl, T_s = sls[ti]
                add("act", lambda e: e.activation(out=junk[0:nt, :], in_=x1[0:nt, i2, :], func=AF.Square, accum_out=sl[0:nt, 8:9]),
                    reads=[T_x1[i2]], writes=[T_s])
                add("act", lambda e: e.activation(out=sl[0:nt, 9:10], in_=sl[0:nt, 8:9], func=AF.Ln, scale=1.0 / D, bias=EPS), writes=[T_s])
                add("act", lambda e: e.activation(out=sl[0:nt, 10:11], in_=sl[0:nt, 9:10], func=AF.Exp, scale=-0.5), writes=[T_s])
                add("act", lambda e: e.activation(out=h2b[0:nt, i2, :], in_=x1[0:nt, i2, :], func=AF.Copy, scale=sl[0:nt, 10:11]),
                    reads=[T_x1[i2], T_s], writes=[T_h2b[i2]])

            def b_tr(ti):
                r0, nt = tiles[ti]
                i2 = ti % NSL
                transpose_rows(h2b[:, i2, :], T_h2b[i2], nt, h2T[:, :, r0:r0 + nt], T_h2T[ti], eng="dve")

            for it in range(len(tiles) + 4):
                if it < len(tiles):
                    b_mm(it)
                if 0 <= it - 1 < len(tiles):
                    b_epi_a(it - 1)
                if 0 <= it - 3 < len(tiles):
                    b_epi_c(it - 3)
                if 0 <= it - 2 < len(tiles):
                    b_epi_b(it - 2)
                if 0 <= it - 4 < len(tiles):
                    b_tr(it - 4)
            S_.barrier()

            actT = sb(p + "actT"); wd = sb(p + "wd"); wgb = sb(p + "wgb"); wgst = None; wdst = None
            gsb = sb(p + "gsb"); usb = sb(p + "usb"); cv = sb(p + "cv"); nbh = sb(p + "nbh"); maskt = sb(p + "mask")
            x1l = sb(p + "x1l"); yo = sb(p + "yo")
            T_mask = Tile()
            dma(lambda e: e.dma_start(out=maskt[:, :], in_=P["mask"]), writes=[T_mask])
            T_wgst = [Tile(), Tile()]; T_wgb = [Tile(), Tile()]; T_wdst = [Tile(), Tile()]
            T_wd = [Tile() for _ in range(NF)]
            T_gh = [Tile(), Tile()]; T_gb = [[Tile() for _ in range(FG // 512)] for _ in range(2)]
            T_usb = [[Tile() for _ in range(FG // 512)] for _ in range(2)]; T_cv = [Tile(), Tile()]; T_nbh = Tile()
            T_act = [Tile() for _ in range(NF)]
            T_x1l = [Tile(), Tile()]; T_yo = [Tile(), Tile()]
            for fg in range(NFG):
                first_group = False
                t0 = fg * FG
                lcol = NOWN if fg == 0 else t0 - 1
                rcol = NOWN + 1 if fg == NFG - 1 else t0 + FG
                lm = 0 if fg == 0 else 2
                rm = 1 if fg == NFG - 1 else 2
                nblk = FG // 512
                add("pool", lambda e, lcol=lcol: e.tensor_copy(out=nbh[:, :, 0:1], in_=h2T[:, :, lcol:lcol + 1]), writes=[T_nbh])
                add("pool", lambda e, rcol=rcol: e.tensor_copy(out=nbh[:, :, 1:2], in_=h2T[:, :, rcol:rcol + 1]), writes=[T_nbh])
                rot_gu = Rot([0, 1, 2, 3, 4, 5])
                rot_nb = Rot([6, 7])

                def conv_stage(f, bl, nblk=nblk):
                    fp = f % 2
                    ci = bl % 2
                    o = bl * 512
                    g_reads = [T_gh[fp]] + [T_gb[fp][x_] for x_ in range(max(0, bl - 1), min(nblk, bl + 2))]
                    add("dve", lambda e: e.tensor_scalar(out=cv[:, ci, 0, :], in0=gsb[:, fp, o:o + 512], scalar1=convw[:, f, 0:1],
                                                         scalar2=convw[:, f, 3:4], op0=ALU.mult, op1=ALU.add), reads=g_reads, writes=[T_cv[ci]])
                    add("dve", lambda e: e.scalar_tensor_tensor(out=cv[:, ci, 1, :], in0=gsb[:, fp, o + 1:o + 513], scalar=convw[:, f, 1:2],
                                                                in1=cv[:, ci, 0, :], op0=ALU.mult, op1=ALU.add), reads=g_reads, writes=[T_cv[ci]])
                    add("dve", lambda e: e.scalar_tensor_tensor(out=cv[:, ci, 2, :], in0=gsb[:, fp, o + 2:o + 514], scalar=convw[:, f, 2:3],
                                                                in1=cv[:, ci, 1, :], op0=ALU.mult, op1=ALU.add), reads=g_reads, writes=[T_cv[ci]])
                    add("act", lambda e: e.activation(out=cv[:, ci, 0, :], in_=cv[:, ci, 2, :], func=AF.Gelu_apprx_tanh), writes=[T_cv[ci]])
                    add("dve", lambda e: e.tensor_tensor(out=actT[:, f, o:o + 512], in0=cv[:, ci, 0, :], in1=usb[:, fp, o:o + 512], op=ALU.mult),
                        reads=[T_cv[ci], T_usb[fp][bl]], writes=[T_act[f]])

                for f in range(NF):
                    wi = f % 2
                    fp = f % 2
                    if first_group:
                        for gu, wsrc in ((0, W["w_gate"]), (1, W["w_up"])):
                            dma(lambda e, wi=wi, gu=gu, wsrc=wsrc, f=f: e.dma_start(out=wgst[:, wi, gu, :], in_=wsrc[f].rearrange("p k c -> p (k c)")),
                                writes=[T_wgst[wi]])
                        for gu in range(2):
                            add("pool", lambda e, wi=wi, gu=gu: e.tensor_tensor(
                                out=wgb[:, wi, gu, :].rearrange("p (k c) -> p k c", k=KC), in0=wgst[:, wi, gu, :].rearrange("p (k c) -> p k c", k=KC),
                                in1=gcols[:, 8:16].unsqueeze(2).to_broadcast([128, KC, 128]), op=ALU.mult), reads=[T_wgst[wi]], writes=[T_wgb[wi]])
                        dma(lambda e, wi=wi, f=f: e.dma_start(out=WS["wg"][f], in_=wgb[:, wi, :, :]), reads=[T_wgb[wi]])
                        dma(lambda e, wi=wi, f=f: e.dma_start(out=wdst[:, wi, :], in_=W["w_down"][f]), writes=[T_wdst[wi]])
                        add("act", lambda e, wi=wi, f=f: e.copy(out=wd[:, f, :], in_=wdst[:, wi, :]), reads=[T_wdst[wi]], writes=[T_wd[f]])
                        dma(lambda e, f=f: e.dma_start(out=WS["wd"][f], in_=wd[:, f, :]), reads=[T_wd[f]])
                    else:
                        dma(lambda e, wi=wi, f=f: e.dma_start(out=wgb[:, wi, :, :], in_=WS["wg"][f]), writes=[T_wgb[wi]])
                        add("pool", lambda e, f=f: e.dma_start(out=wd[:, f, :], in_=WS["wd"][f]), writes=[T_wd[f]], is_dma=True)
                    b = rot_nb.next()
                    for k in range(KC):
                        add("pe", lambda e, b=b, k=k, wi=wi: e.matmul(PSB[b][:, 0:2], lhsT=wgb[:, wi, 0, k * 128:(k + 1) * 128], rhs=nbh[:, k, :],
                                                                      start=(k == 0), stop=(k == KC - 1)), reads=[T_wgb[wi], T_nbh], writes=[PT[b]])
                    add("dve", lambda e, b=b, lm=lm, fp=fp: e.tensor_tensor(out=gsb[:, fp, 0:1], in0=PSB[b][:, 0:1], in1=maskt[:, lm:lm + 1], op=ALU.mult),
                        reads=[T_mask], writes=[PT[b], T_gh[fp]])
                    add("dve", lambda e, b=b, rm=rm, fp=fp: e.tensor_tensor(out=gsb[:, fp, FG + 1:FG + 2], in0=PSB[b][:, 1:2], in1=maskt[:, rm:rm + 1], op=ALU.mult),
                        reads=[T_mask], writes=[PT[b], T_gh[fp]])
                    for bl in range(nblk):
                        c0 = t0 + bl * 512
                        bg, bu = rot_gu.next(), rot_gu.next()
                        for (bk_, gu) in ((bg, 0), (bu, 1)):
                            for k in range(KC):
                                add("pe", lambda e, bk_=bk_, gu=gu, k=k, wi=wi, c0=c0: e.matmul(
                                    PSB[bk_][:, :], lhsT=wgb[:, wi, gu, k * 128:(k + 1) * 128], rhs=h2T[:, k, c0:c0 + 512],
                                    start=(k == 0), stop=(k == KC - 1)), reads=[T_wgb[wi]], writes=[PT[bk_]])
                        add("act", lambda e, bg=bg, bl=bl, fp=fp: e.copy(out=gsb[:, fp, 1 + bl * 512:1 + (bl + 1) * 512], in_=PSB[bg][:, :]),
                            writes=[PT[bg], T_gb[fp][bl]])
                        add("act", lambda e, bu=bu, bl=bl, fp=fp: e.copy(out=usb[:, fp, bl * 512:(bl + 1) * 512], in_=PSB[bu][:, :]),
                            writes=[PT[bu], T_usb[fp][bl]])
                        if f >= 1:
                            conv_stage(f - 1, bl)
                for bl in range(nblk):
                    conv_stage(NF - 1, bl)
                rot_d = Rot([0, 1, 2, 3, 4, 5, 6, 7])
                for tt in range(FG // 128):
                    r0 = t0 + tt * 128
                    i2 = tt % 2
                    dma(lambda e, i2=i2, r0=r0: e.dma_start(out=x1l[:, i2, :], in_=P["x1s"][r0:r0 + 128, :]), writes=[T_x1l[i2]])
                    yb = [rot_d.next(), rot_d.next()]
                    for hf in range(2):
                        for f in range(NF):
                            add("pe", lambda e, hf=hf, f=f, yb=yb, tt=tt: e.matmul(
                                PSB[yb[hf]][:, :], lhsT=actT[:, f, tt * 128:(tt + 1) * 128], rhs=wd[:, f, hf * 512:(hf + 1) * 512],
                                start=(f == 0), stop=(f == NF - 1)), reads=[T_act[f], T_wd[f]], writes=[PT[yb[hf]]])
                    sl, T_s = sm_slot()
                    for hf in range(2):
                        add("act", lambda e, hf=hf, yb=yb, sl=sl: e.activation(out=junk[:, 0:512], in_=PSB[yb[hf]][:, :], func=AF.Square,
                                                                             accum_out=sl[:, hf:hf + 1]), writes=[PT[yb[hf]], T_s])
                    add("dve", lambda e, sl=sl: e.tensor_tensor(out=sl[:, 2:3], in0=sl[:, 0:1], in1=sl[:, 1:2], op=ALU.add), writes=[T_s])
                    add("act", lambda e, sl=sl: e.activation(out=sl[:, 3:4], in_=sl[:, 2:3], func=AF.Ln, scale=1.0 / D, bias=EPS), writes=[T_s])
                    add("act", lambda e, sl=sl: e.activation(out=sl[:, 4:5], in_=sl[:, 3:4], func=AF.Exp, scale=-0.5), writes=[T_s])
                    for hf in range(2):
                        add("dve", lambda e, hf=hf, yb=yb, sl=sl, i2=i2: e.scalar_tensor_tensor(
                            out=yo[:, i2, hf * 512:(hf + 1) * 512], in0=PSB[yb[hf]][:, :], scalar=sl[:, 4:5],
                            in1=gpf[:, hf * 512:(hf + 1) * 512], op0=ALU.mult, op1=ALU.mult), reads=[T_s], writes=[PT[yb[hf]], T_yo[i2]])
                    add("dve", lambda e, i2=i2: e.tensor_tensor(out=yo[:, i2, :], in0=yo[:, i2, :], in1=x1l[:, i2, :], op=ALU.add),
                        reads=[T_x1l[i2]], writes=[T_yo[i2]])
                    add("pool", lambda e, i2=i2, r0=r0: e.dma_start(out=P["y"][r0:r0 + 128, :], in_=yo[:, i2, :]), reads=[T_yo[i2]], is_dma=True)
                if fg == NFG - 1:
                    S_.barrier()

        for pi_, pc_ in enumerate(parts):
            do_part(pi_, pc_)
        S_.emit(block, sems_eng, sems_dma)
    return nc


_CACHE = {}


def run(cfg, x_prompt, x_sample, g_pre_mix, w_in, g_q, w_uq, g_kv, w_ukv, w_fnet, w_out, g_post_mix, g_pre_ffn,
        w_gate, w_up, conv_w, conv_b, w_down, g_post_ffn):
    f = lambda a: np.asarray(a, dtype=np.float32)
    parts = cfg["parts"]
    xs = [f(x_prompt), f(x_sample)]
    hw = host_weights(f(w_in)[0], f(w_uq)[0], f(w_ukv)[0], f(w_fnet)[0], f(w_out)[0], f(w_gate)[0], f(w_up)[0], f(conv_w)[0],
                      f(conv_b)[0], f(w_down)[0], f(g_pre_mix)[0], f(g_q)[0], f(g_kv)[0], f(g_post_mix)[0], f(g_pre_ffn)[0],
                      f(g_post_ffn)[0])
    key = (parts[0]["S"], parts[1]["S"])
    if key not in _CACHE:
        _CACHE[key] = build_program(cfg)
    nc = _CACHE[key]
    in_maps = []
    for c in range(8):
        m = dict(hw)
        for pi, pc in enumerate(parts):
            seq = c // pc["nsplit"]
            q = c % pc["nsplit"]
            hc = host_consts(pc, q)
            x = xs[pi][seq]
            m["xs%d" % pi] = np.ascontiguousarray(x[hc["perm"]])
            m["xo%d" % pi] = np.ascontiguousarray(x[hc["pos_own"]])
            m["dft%d" % pi] = hc["dft"]
            m["csk%d" % pi] = np.ascontiguousarray(hc["csk"])
            m["csq%d" % pi] = np.ascontiguousarray(hc["csq"])
            m["c2own%d" % pi] = np.ascontiguousarray(hc["c2"])
            m["mask%d" % pi] = hc["mask"]
        in_maps.append(m)
    res = run_bass_kernel_spmd(nc, in_maps, core_ids=list(range(8)))
    outs = []
    for pi, pc in enumerate(parts):
        y = np.zeros((pc["nbatch"], pc["S"], D), np.float32)
        for c in range(8):
            seq = c // pc["nsplit"]
            q = c % pc["nsplit"]
            y[seq, q * pc["NOWN"]:(q + 1) * pc["NOWN"]] = res.results[c]["y%d" % pi]
        outs.append(y)
    return tuple(outs)


def kernel(**inputs):
    return run(make_cfg(), **inputs)
```

```python
import math
from contextlib import ExitStack

import numpy as np
import ml_dtypes

import concourse.bass as bass
import concourse.mybir as mybir
from concourse.bass_utils import run_bass_kernel_spmd

F32 = mybir.dt.float32
BF16 = mybir.dt.bfloat16
AF = mybir.ActivationFunctionType
ALU = mybir.AluOpType
NPBF = ml_dtypes.bfloat16

D = 1024
KC = 8
QL = 256
KVL = 256
NH = 8
DN = 64
DR = 32
DV = 64
DQ = DN + DR
FW = 512
NG = 4
DFF = 2816
NF = DFF // 128
EPS = 1e-6
THETA = 10000.0
SCALE = 1.0 / math.sqrt(DQ)
WIN_COLS = QL + KVL + 2 * DR + FW

ENGS = ("pe", "act", "dve", "pool", "sp")
NDMA = 24


class Tile:
    __slots__ = ("name", "w", "r", "rd")

    def __init__(self, name=""):
        self.name = name
        self.w = None
        self.r = {}
        self.rd = []


class Op:
    __slots__ = ("eng", "fn", "deps", "signal", "count", "is_dma", "sem", "waits", "prewait")

    def __init__(self, eng, fn, is_dma):
        self.eng = eng
        self.fn = fn
        self.deps = []
        self.signal = False
        self.count = 0
        self.is_dma = is_dma
        self.sem = None
        self.waits = None
        self.prewait = None


class Sched:
    def __init__(self):
        self.ops = {e: [] for e in ENGS}
        self.all_ops = []
        self.last = {e: None for e in ENGS}
        self.dma_rr = 0
        self.dma_rr2 = 0
        self.dma_last = [None] * NDMA

    def add(self, eng, fn, reads=(), writes=(), deps=(), is_dma=False):
        op = Op(eng, fn, is_dma)
        d = []
        for t in reads:
            if t.w is not None:
                d.append(t.w)
        for t in writes:
            if t.w is not None:
                d.append(t.w)
            d.extend(t.r.values())
            d.extend(t.rd)
        d.extend(deps)
        seen = set()
        for x in d:
            if x is None or x is op or id(x) in seen:
                continue
            seen.add(id(x))
            if x.eng == "pe" and eng == "pe" and not x.is_dma and not is_dma:
                continue
            op.deps.append(x)
        for t in reads:
            if is_dma:
                t.rd.append(op)
            else:
                t.r[eng] = op
        for t in writes:
            t.w = op
            t.r = {}
            t.rd = []
        if is_dma:
            if eng == "sp":
                s = self.dma_rr % 16
                self.dma_rr += 1
            else:
                s = 16 + self.dma_rr2 % (NDMA - 16)
                self.dma_rr2 += 1
            op.sem = s
            op.prewait = self.dma_last[s]
            self.dma_last[s] = op
        self.ops[eng].append(op)
        self.all_ops.append(op)
        if fn is not None:
            self.last[eng] = op
        return op

    def dma(self, fn, reads=(), writes=(), deps=()):
        return self.add("sp", fn, reads, writes, deps, is_dma=True)

    def barrier(self):
        pend = [self.last[e] for e in ENGS if self.last[e] is not None]
        pend += [o for o in self.dma_last if o is not None]
        for e in ENGS:
            self.add(e, None, deps=pend)

    def finalize(self):
        for op in self.all_ops:
            for d in op.deps:
                d.signal = True
            if op.prewait is not None:
                op.prewait.signal = True
        cnt = {e: 0 for e in ENGS}
        dcnt = [0] * NDMA
        for op in self.all_ops:
            if op.is_dma:
                op.signal = True
                dcnt[op.sem] += 16
                op.count = dcnt[op.sem]
            elif op.signal:
                assert op.fn is not None
                cnt[op.eng] += 1
                op.count = cnt[op.eng]
        waited = {e: {} for e in ENGS}
        for op in self.all_ops:
            w = {}
            dl = list(op.deps)
            if op.prewait is not None:
                dl.append(op.prewait)
            for d in dl:
                key = ("dma", d.sem) if d.is_dma else ("eng", d.eng)
                if w.get(key, 0) < d.count:
                    w[key] = d.count
            wd = waited[op.eng]
            out = []
            for key, val in w.items():
                if wd.get(key, 0) >= val:
                    continue
                wd[key] = val
                out.append((key, val))
            op.waits = out
        self.final_counts = cnt
        self.final_dma = dcnt

    def emit(self, block, sems_eng, sems_dma):
        self.finalize()
        engmap = {"pe": "tensor", "act": "scalar", "dve": "vector", "pool": "gpsimd", "sp": "sync"}

        def semof(key):
            return sems_dma[key[1]] if key[0] == "dma" else sems_eng[key[1]]

        def run(ename):
            def body(e):
                for op in self.ops[ename]:
                    for key, val in op.waits:
                        e.wait_ge(semof(key), val)
                    if op.fn is None:
                        continue
                    ins = op.fn(e)
                    if op.signal:
                        if op.is_dma:
                            ins.then_inc(sems_dma[op.sem], 16)
                        else:
                            ins.then_inc(sems_eng[ename], 1)
                if ename == "sp":
                    for k in ENGS:
                        if self.final_counts[k] > 0:
                            e.wait_ge(sems_eng[k], self.final_counts[k])
                    for s, c in enumerate(self.final_dma):
                        if c > 0:
                            e.wait_ge(sems_dma[s], c)
            return body

        for ename in ENGS:
            getattr(block, engmap[ename])(run(ename))


class Plan:
    def __init__(self, cap):
        self.cap = cap
        self.items = []

    def add(self, name, shape, dtype, live):
        esz = 4 if dtype == F32 else 2
        n = 1
        for s in shape[1:]:
            n *= s
        size = (n * esz + 63) // 64 * 64
        it = dict(name=name, shape=list(shape), dtype=dtype, live=frozenset(live), size=size, off=None)
        self.items.append(it)
        return it

    def solve(self):
        placed = []
        for it in sorted(self.items, key=lambda i: -i["size"]):
            cands = sorted([(p["off"], p["off"] + p["size"]) for p in placed if p["live"] & it["live"]])
            off = 0
            for a, b in cands:
                if off + it["size"] <= a:
                    break
                off = max(off, b)
            assert off + it["size"] <= self.cap, ("SBUF overflow", it["name"], off, it["size"])
            it["off"] = off
            placed.append(it)


def make_cfg(sp=8192, ss=4096):
    parts = []
    for (S, nsplit, nb) in ((sp, 4, 2), (ss, 2, 4)):
        nown = S // nsplit
        n1 = S // 128
        assert 128 % n1 == 0 and nown % 512 == 0 and nown % n1 == 0
        fg = min(1024, nown)
        parts.append(dict(S=S, nsplit=nsplit, nbatch=nb, NOWN=nown, NQ=nown + 2, N1=n1, NJ=128 // n1, NT=S // 128,
                          NST=S // 512, NK2=nown // n1, N2C=nown // n1 + 2, FG=fg, NFG=nown // fg))
    return dict(parts=parts)


def perm_rows(pc):
    N1, NJ, NT = pc["N1"], pc["NJ"], pc["NT"]
    t = np.arange(NT)[:, None, None]
    j = np.arange(NJ)[None, :, None]
    s1 = np.arange(N1)[None, None, :]
    return (128 * s1 + NJ * t + j).reshape(-1)


def rope_tab(pos):
    inv = THETA ** (-np.arange(0, DR, 2, dtype=np.float64) / DR)
    ang = pos.astype(np.float64)[None, :] * inv[:, None]
    c, s = np.cos(ang), np.sin(ang)
    cc = np.concatenate([c, c], 0)
    ss = np.concatenate([-s, s], 0)
    return cc, ss


def host_consts(pc, q):
    S, N1, NJ, NT, NOWN = pc["S"], pc["N1"], pc["NJ"], pc["NT"], pc["NOWN"]
    perm = perm_rows(pc)
    dft = np.zeros((NT, 128, 3, 128), np.float64)
    k1 = np.arange(N1)
    for t in range(NT):
        for j in range(NJ):
            s = perm[128 * t + j * N1: 128 * t + (j + 1) * N1]
            ang = 2 * np.pi * ((s[:, None] * k1[None, :]) % S) / S
            c, sn = np.cos(ang) / math.sqrt(S), np.sin(ang) / math.sqrt(S)
            sl = slice(j * N1, (j + 1) * N1)
            dft[t, sl, 0, sl] = c
            dft[t, sl, 1, sl] = sn
            dft[t, sl, 2, sl] = -sn
    cc, ss = rope_tab(perm)
    csk = np.concatenate([cc, ss], 0).reshape(64, pc["NST"], 512).transpose(1, 0, 2)
    a = q * NOWN
    pos_own = np.concatenate([np.arange(a, a + NOWN), [(a - 1) % S, (a + NOWN) % S]])
    cq, sq = rope_tab(pos_own)
    csq = np.stack([cq, sq], 1)
    k2lo = a // N1
    k2 = (np.arange(k2lo - 1, k2lo + pc["NK2"] + 1)) % 128
    s2 = np.arange(128)
    ang2 = 2 * np.pi * ((s2[:, None] * k2[None, :]) % 128) / 128
    c2 = np.stack([np.cos(ang2), -np.sin(ang2)], 1)
    mask = np.ones((128, 3), np.float32)
    mask[:, 0] = 0.0 if a == 0 else 1.0
    mask[:, 1] = 0.0 if a + NOWN >= S else 1.0
    return dict(dft=dft.astype(NPBF), csk=csk.astype(np.float32), csq=csq.astype(NPBF),
                c2=c2.astype(NPBF), mask=mask, pos_own=pos_own, perm=perm)


def host_weights(w_in, w_uq, w_ukv, w_fnet, w_out, w_gate, w_up, conv_w, conv_b, w_down,
                 g_pre_mix, g_q, g_kv, g_post_mix, g_pre_ffn, g_post_ffn):
    def kmaj(w):
        k, n = w.shape
        return np.ascontiguousarray(w.reshape(k // 128, 128, n).transpose(1, 0, 2))

    def gcol(g):
        return np.ascontiguousarray(g.reshape(-1, 128).T)

    o = {}
    kr = w_in[:, QL + KVL:QL + KVL + DR]
    kr_sw = np.concatenate([kr[:, DR // 2:], kr[:, :DR // 2]], 1)
    o["w_in"] = kmaj(np.concatenate([w_in[:, :QL + KVL], kr, kr_sw, w_in[:, QL + KVL + DR:]], 1))
    wq = w_uq.reshape(QL, NH, DQ)
    wq_sw = np.concatenate([wq[:, :, :DN], wq[:, :, DN + DR // 2:], wq[:, :, DN:DN + DR // 2]], 2)
    o["w_uq"] = kmaj(np.concatenate([wq.reshape(QL, -1), wq_sw.reshape(QL, -1)], 1))
    wkv = w_ukv.reshape(KVL, NH, DN + DV)
    o["w_ukv"] = kmaj(np.concatenate([wkv[:, :, :DN].reshape(KVL, -1), wkv[:, :, DN:].reshape(KVL, -1)], 1))
    o["w_fnet"] = np.ascontiguousarray(w_fnet.transpose(1, 0, 2))
    o["w_out"] = kmaj(w_out)
    o["w_gate"] = np.ascontiguousarray(w_gate.reshape(KC, 128, NF, 128).transpose(2, 1, 0, 3))
    o["w_up"] = np.ascontiguousarray(w_up.reshape(KC, 128, NF, 128).transpose(2, 1, 0, 3))
    o["w_down"] = np.ascontiguousarray(w_down.reshape(NF, 128, D))
    cw = np.concatenate([conv_w, conv_b[None, :]], 0)
    o["conv"] = np.ascontiguousarray(cw.reshape(4, NF, 128).transpose(2, 1, 0))
    o["gcols"] = np.ascontiguousarray(np.concatenate([gcol(g_pre_mix), gcol(g_pre_ffn), gcol(g_q), gcol(g_kv)], 1))
    o["g_post_mix"] = np.ascontiguousarray(np.broadcast_to(g_post_mix[None, :], (128, D)))
    o["g_post_ffn"] = np.ascontiguousarray(np.broadcast_to(g_post_ffn[None, :], (128, D)))
    c = np.arange(128)
    ang = 2 * np.pi * ((c[:, None] * c[None, :]) % 128) / 128
    o["ccsc"] = np.stack([np.cos(ang), np.sin(ang)], 1).astype(np.float64) / math.sqrt(128)
    o["ccsc"] = o["ccsc"].astype(NPBF)
    o["ident"] = np.eye(128, dtype=np.float32).astype(NPBF)
    return {k: (v if v.dtype == NPBF else np.ascontiguousarray(v, dtype=np.float32)) for k, v in o.items()}


def build_program(cfg):
    nc = bass.Bass("TRN2", target_bir_lowering=False)
    parts = cfg["parts"]
    S_ = Sched()

    def din(name, shape, dt=F32):
        return nc.dram_tensor(name, list(shape), dt, kind="ExternalInput").ap()

    def dscr(name, shape, dt):
        return nc.dram_tensor(name, list(shape), dt, kind="Internal").ap()

    W = dict(
        w_in=din("w_in", [128, KC, WIN_COLS]), w_uq=din("w_uq", [128, 2, 2 * NH * DQ]), w_ukv=din("w_ukv", [128, 2, 1024]),
        w_fnet=din("w_fnet", [128, NG, 128]), w_out=din("w_out", [128, KC, D]),
        w_gate=din("w_gate", [NF, 128, KC, 128]), w_up=din("w_up", [NF, 128, KC, 128]), w_down=din("w_down", [NF, 128, D]),
        conv=din("conv", [128, NF, 4]), gcols=din("gcols", [128, 20]), g_post_mix=din("g_post_mix", [128, D]),
        g_post_ffn=din("g_post_ffn", [128, D]), ccsc=din("ccsc", [128, 2, 128], BF16), ident=din("ident", [128, 128], BF16),
    )
    WS = dict(wg=dscr("wgs", [NF, 128, 2, KC * 128], BF16), wd=dscr("wds", [NF, 128, D], BF16),
              w_in=dscr("wins", [128, KC, WIN_COLS], BF16), w_uq=dscr("wuqs", [128, 2, 2 * NH * DQ], BF16),
              w_ukv=dscr("wukvs", [128, 2, 1024], BF16), w_out=dscr("wouts", [128, KC, D], BF16))
    PD = []
    for pi, pc in enumerate(parts):
        PD.append(dict(
            xs=din("xs%d" % pi, [pc["S"], D]), xo=din("xo%d" % pi, [pc["NQ"], D]),
            dft=din("dft%d" % pi, [pc["NT"], 128, 3, 128], BF16), csk=din("csk%d" % pi, [pc["NST"], 64, 512]),
            csq=din("csq%d" % pi, [32, 2, pc["NQ"]], BF16), c2=din("c2own%d" % pi, [128, 2, pc["N2C"]], BF16),
            mask=din("mask%d" % pi, [128, 3]),
            y=nc.dram_tensor("y%d" % pi, [pc["NOWN"], D], F32, kind="ExternalOutput").ap(),
            zd=dscr("zd%d" % pi, [pc["N1"], 128, 1024], BF16), x1s=dscr("x1s%d" % pi, [pc["NOWN"], D], F32),
        ))

    SB_BASE = 16512
    plan = Plan(229344 - SB_BASE)
    ALLPH = set(range(10))

    def PH(pi, *names):
        m = dict(A=0, ATTW=1, ATT=2, B=3, FFN=4)
        return {5 * pi + m[n] for n in names}

    B_ = {}

    def decl(name, shape, dt, live):
        B_[name] = plan.add(name, shape, dt, live)

    decl("ident", [128, 128], BF16, ALLPH)
    decl("ones_bf", [128, 128], BF16, ALLPH)
    decl("ones_f", [128, 64], F32, ALLPH)
    decl("sel", [64, 32], BF16, ALLPH)
    decl("gpm", [128, D], F32, ALLPH)
    decl("gpf", [128, D], F32, ALLPH)
    decl("gcols", [128, 20], F32, ALLPH)
    decl("conv", [128, NF, 4], F32, ALLPH)
    decl("ab", [128, NG, 256], BF16, ALLPH)
    decl("junk", [128, D], BF16, ALLPH)
    decl("sm", [128, 128], F32, ALLPH)
    MIX = lambda pi: PH(pi, "A", "ATT", "B")
    for pi, pc in enumerate(parts):
        S, NQ = pc["S"], pc["NQ"]
        p = "p%d_" % pi
        decl(p + "w_in", [128, KC, WIN_COLS], BF16, PH(pi, "A"))
        decl(p + "w_uq", [128, 2, 2 * NH * DQ], BF16, PH(pi, "ATTW", "ATT"))
        decl(p + "w_ukv", [128, 2, 1024], BF16, PH(pi, "ATTW", "ATT"))
        decl(p + "w_out", [128, KC, D], BF16, PH(pi, "B"))
        decl(p + "wstage", [128, 2, 1536], F32, PH(pi, "A", "ATTW", "B"))
        decl(p + "ckvn", [128, 2, S], BF16, PH(pi, "A", "ATTW", "ATT"))
        decl(p + "kt", [128, 2, S], BF16, PH(pi, "A", "ATTW", "ATT"))
        decl(p + "cqn", [128, 2, NQ], BF16, PH(pi, "A", "ATTW", "ATT"))
        decl(p + "csq", [128, 2, NQ], BF16, PH(pi, "ATTW", "ATT"))
        decl(p + "attnT", [128, 4, NQ], BF16, PH(pi, "ATT", "B"))
        decl(p + "fT", [128, 4, NQ], BF16, PH(pi, "ATT", "B"))
        decl(p + "xst", [128, 2, 4, D], F32, PH(pi, "A"))
        decl(p + "hb", [128, 2, 4, D], BF16, PH(pi, "A"))
        decl(p + "hT", [128, 2, KC, 512], BF16, PH(pi, "A"))
        decl(p + "sq", [128, 2, 512], BF16, PH(pi, "A"))
        decl(p + "rr", [128, 512], F32, PH(pi, "A"))
        decl(p + "ut", [128, 2, NG, 512], BF16, PH(pi, "A"))
        decl(p + "pq", [128, 2, 1024], BF16, PH(pi, "A"))
        decl(p + "zt", [128, 2, 1024], BF16, PH(pi, "A"))
        decl(p + "csk", [64, 2, 512], F32, PH(pi, "A"))
        decl(p + "krt", [64, 512], BF16, PH(pi, "A"))
        decl(p + "dft", [128, 2, 3, 128], BF16, PH(pi, "A"))
        decl(p + "v2", [128, 2, S // 128, 2, DV + 1], BF16, PH(pi, "ATT"))
        decl(p + "qt", [128, NQ], BF16, PH(pi, "ATT"))
        decl(p + "pT", [128, 4, 512], BF16, PH(pi, "ATT"))
        decl(p + "qtmp", [128, 1, 2, 512], F32, PH(pi, "ATT"))
        decl(p + "den", [128, 2, 512], F32, PH(pi, "ATT"))
        if pi == 0:
            decl("pc_st", [128, D], F32, PH(pi, "ATT"))
            decl("pc_ob", [128, 2, D], BF16, PH(pi, "ATT"))
        decl(p + "zk", [128, 2, 1024], BF16, PH(pi, "ATT"))
        decl(p + "c2", [128, 2, pc["N2C"]], BF16, PH(pi, "ATTW", "ATT"))
        decl(p + "xb", [128, 6, D], F32, PH(pi, "B"))
        decl(p + "x1", [128, 6, D], F32, PH(pi, "B"))
        decl(p + "h2b", [128, 6, D], BF16, PH(pi, "B"))
        decl(p + "h2T", [128, KC, NQ], BF16, PH(pi, "B", "FFN"))
        FG = pc["FG"]
        decl(p + "actT", [128, NF, FG], BF16, PH(pi, "FFN"))
        decl(p + "wd", [128, NF, D], BF16, PH(pi, "FFN"))
        decl(p + "wgb", [128, 2, 2, KC * 128], BF16, PH(pi, "FFN"))
        decl(p + "gsb", [128, 2, FG + 2], F32, PH(pi, "FFN"))
        decl(p + "usb", [128, 2, FG], BF16, PH(pi, "FFN"))
        decl(p + "cv", [128, 2, 3, 512], F32, PH(pi, "FFN"))
        decl(p + "nbh", [128, KC, 2], BF16, PH(pi, "FFN"))
        decl(p + "mask", [128, 3], F32, PH(pi, "FFN"))
        decl(p + "x1l", [128, 2, D], F32, PH(pi, "FFN"))
        decl(p + "yo", [128, 2, D], F32, PH(pi, "FFN"))
    plan.solve()

    def sb(name):
        it = B_[name]
        if "h" not in it:
            it["h"] = nc.alloc_sbuf_tensor_at(name, it["shape"], it["dtype"], offset=SB_BASE + it["off"])
        return it["h"]

    es = ExitStack()
    with es:
        PSB = [es.enter_context(nc.psum_tensor("psb%d" % i, [128, 512], F32)) for i in range(8)]
        PT = [Tile("ps%d" % i) for i in range(8)]
        sems_eng = {e: es.enter_context(nc.semaphore("s_" + e)) for e in ENGS}
        sems_dma = [es.enter_context(nc.semaphore("sd%d" % i)) for i in range(NDMA)]
        block = es.enter_context(nc.Block())

        add = S_.add
        dma = S_.dma

        class Rot:
            def __init__(self, banks):
                self.b = list(banks)
                self.i = 0

            def next(self):
                b = self.b[self.i % len(self.b)]
                self.i += 1
                return b

        sm = sb("sm")
        sm_t = [Tile("sm%d" % i) for i in range(8)]
        sm_i = [0]

        def sm_slot():
            i = sm_i[0] % 8
            sm_i[0] += 1
            return sm[:, i * 16:(i + 1) * 16], sm_t[i]

        ident = sb("ident"); ones_bf = sb("ones_bf"); ones_f = sb("ones_f"); gpm = sb("gpm"); gpf = sb("gpf")
        gcols = sb("gcols"); convw = sb("conv"); ab = sb("ab"); junk = sb("junk")
        T_const = Tile("const")
        T_junk = Tile("junk")
        sel = sb("sel")
        T_sel = Tile("sel")
        T_ident = Tile("ident")
        dma(lambda e: e.dma_start(out=ident[:], in_=W["ident"]), writes=[T_ident])
        dma(lambda e: e.dma_start(out=gpm[:], in_=W["g_post_mix"]))
        dma(lambda e: e.dma_start(out=gpf[:], in_=W["g_post_ffn"]))
        dma(lambda e: e.dma_start(out=gcols[:], in_=W["gcols"]))
        dma(lambda e: e.dma_start(out=convw[:], in_=W["conv"]))
        add("dve", lambda e: e.tensor_copy(out=sel[0:32, :], in_=ident[0:32, 0:32]), reads=[T_ident], writes=[T_sel])
        add("dve", lambda e: e.tensor_copy(out=sel[32:64, :], in_=ident[32:64, 32:64]), reads=[T_ident], writes=[T_sel])
        add("pool", lambda e: e.memset(ones_bf[:], 1.0))
        add("pool", lambda e: e.memset(ones_f[:], 1.0))
        wst0 = sb("p0_wstage")
        ccsc_sb = sb("p0_sq")
        T_ab = Tile("ab")
        o1 = dma(lambda e: e.dma_start(out=ccsc_sb[:, 0, 0:256], in_=W["ccsc"].rearrange("p a b -> p (a b)")))
        o2 = dma(lambda e: e.dma_start(out=wst0[:, 0, 0:512], in_=W["w_fnet"].rearrange("p a b -> p (a b)")))
        o3 = add("dve", lambda e: e.tensor_copy(out=ccsc_sb[:, 1, :], in_=wst0[:, 0, 0:512]), deps=[o2])
        for g in range(NG):
            for cs in range(2):
                add("pe", lambda e, g=g, cs=cs: e.matmul(PSB[cs][:, g * 128:(g + 1) * 128], lhsT=ccsc_sb[:, 0, cs * 128:(cs + 1) * 128],
                                                         rhs=ccsc_sb[:, 1, g * 128:(g + 1) * 128], start=True, stop=True),
                    writes=[PT[cs]], deps=[o1, o3])
        for cs in range(2):
            add("dve", lambda e, cs=cs: e.tensor_copy(out=ab[:, :, cs * 128:(cs + 1) * 128],
                                                      in_=PSB[cs][:, :].rearrange("p (g d) -> p g d", g=NG)),
                writes=[PT[cs], T_ab])
        S_.barrier()

        def do_part(pi, pc):
            S, NQ, NOWN, N1, NJ, NT, NST = pc["S"], pc["NQ"], pc["NOWN"], pc["N1"], pc["NJ"], pc["NT"], pc["NST"]
            NK2, N2C, FG, NFG = pc["NK2"], pc["N2C"], pc["FG"], pc["NFG"]
            P = PD[pi]
            p = "p%d_" % pi
            NKC = S // 128
            blocks = [(b * 512, 512) for b in range(NOWN // 512)] + [(NOWN, 2)]

            wstage = sb(p + "wstage")
            T_wst = [Tile(), Tile()]
            wst_i = [0]

            def load_cast(dst, src, ncols, gcol0=None, nkc=None, eng="dve", T_dst=None):
                ops = []
                per = max(1, 1536 // ncols)
                for k0 in range(0, nkc, per):
                    kn = min(per, nkc - k0)
                    i = wst_i[0] % 2
                    wst_i[0] += 1
                    dma(lambda e, i=i, k0=k0, kn=kn: e.dma_start(
                        out=wstage[:, i, 0:kn * ncols].rearrange("p (k c) -> p k c", k=kn), in_=src[:, k0:k0 + kn, :]), writes=[T_wst[i]])
                    for k in range(kn):
                        if gcol0 is None:
                            ops.append(add(eng, lambda e, i=i, k=k, k0=k0: e.tensor_copy(
                                out=dst[:, k0 + k, :], in_=wstage[:, i, k * ncols:(k + 1) * ncols]), reads=[T_wst[i]],
                                writes=([T_dst[k0 + k]] if T_dst else [])))
                        else:
                            ops.append(add(eng, lambda e, i=i, k=k, k0=k0: e.tensor_scalar(
                                out=dst[:, k0 + k, :], in0=wstage[:, i, k * ncols:(k + 1) * ncols],
                                scalar1=gcols[:, gcol0 + k0 + k:gcol0 + k0 + k + 1], scalar2=None, op0=ALU.mult), reads=[T_wst[i]],
                                writes=([T_dst[k0 + k]] if T_dst else [])))
                return ops

            w_in = sb(p + "w_in")
            T_win = [Tile() for _ in range(KC)]
            if pi == 0:
                load_cast(w_in, W["w_in"], WIN_COLS, gcol0=0, nkc=KC, T_dst=T_win)
                add("pool", lambda e: e.dma_start(out=WS["w_in"], in_=w_in[:, :, :]), reads=T_win, is_dma=True)
            else:
                dma(lambda e: e.dma_start(out=w_in[:, :, :], in_=WS["w_in"]), writes=T_win)

            ckvn = sb(p + "ckvn"); kt = sb(p + "kt"); cqn = sb(p + "cqn")
            xst = sb(p + "xst"); hb = sb(p + "hb"); hT = sb(p + "hT"); sq = sb(p + "sq"); rr = sb(p + "rr")
            ut = sb(p + "ut"); pq = sb(p + "pq"); zt = sb(p + "zt"); csk = sb(p + "csk"); krt = sb(p + "krt"); dftb = sb(p + "dft")
            T_xst = [[Tile() for _ in range(4)] for _ in range(2)]
            T_hb = [[Tile() for _ in range(4)] for _ in range(2)]
            T_hT = [[Tile() for _ in range(4)] for _ in range(2)]
            T_sq = Tile(); T_rr = Tile(); T_ut = [[Tile() for _ in range(NG)] for _ in range(2)]
            T_pq = [Tile(), Tile()]; T_zt = [Tile(), Tile()]; T_csk = [Tile(), Tile()]; T_krt = Tile(); T_dft = [Tile(), Tile()]
            T_ckvn = [Tile() for _ in range(NST)]
            T_ktr = [[Tile() for _ in range(NST)] for _ in range(2)]
            T_ktn = [[Tile() for _ in range(NST)] for _ in range(2)]
            T_cqn = [Tile() for _ in range(len(blocks))]
            rot_tp = Rot([0, 1])
            rot_cv = Rot([2, 3])
            rot_pj = Rot([4, 5, 6, 7])

            def stage_load(i, tl):
                par = i % 2
                for j, (src, nt) in enumerate(tl):
                    dma(lambda e, src=src, nt=nt, j=j: e.dma_start(out=xst[0:nt, par, j, :], in_=src), writes=[T_xst[par][j]])

            def stage_norm(i, tl):
                par = i % 2
                sl, T_s = sm_slot()
                ntm = tl[0][1]
                n = len(tl)
                items = []
                for j, (src, nt) in enumerate(tl):
                    items.append(lambda nt=nt, j=j: add("act", lambda e: e.activation(out=junk[0:nt, :], in_=xst[0:nt, par, j, :], func=AF.Square,
                                                                                    accum_out=sl[0:nt, j:j + 1]), reads=[T_xst[par][j]], writes=[T_s]))

                def lnexp():
                    add("act", lambda e: e.activation(out=sl[0:ntm, 4:4 + n], in_=sl[0:ntm, 0:n], func=AF.Ln, scale=1.0 / D, bias=EPS), writes=[T_s])
                    add("act", lambda e: e.activation(out=sl[0:ntm, 8:8 + n], in_=sl[0:ntm, 4:4 + n], func=AF.Exp, scale=-0.5), writes=[T_s])
                items.append(lnexp)
                for j, (src, nt) in enumerate(tl):
                    items.append(lambda nt=nt, j=j: add("act", lambda e: e.activation(out=hb[0:nt, par, j, :], in_=xst[0:nt, par, j, :], func=AF.Copy,
                                                                                    scale=sl[0:nt, 8 + j:9 + j]), reads=[T_xst[par][j], T_s], writes=[T_hb[par][j]]))
                return items

            def transpose_rows(h_src, T_h, nt, hT_dst_cols, T_dst, eng="act"):
                b = rot_tp.next()
                pb = PSB[b][:, :].bitcast(BF16)
                for c in range(KC):
                    add("pe", lambda e, c=c: e.transpose(out=pb[:, c * 128:c * 128 + nt], in_=h_src[0:nt, c * 128:(c + 1) * 128],
                                                         identity=ident[0:nt, 0:nt]), reads=[T_h], writes=[PT[b]])
                src = pb.rearrange("p (c t) -> p c t", c=KC)[:, :, 0:nt]
                if eng == "act":
                    add("act", lambda e: e.copy(out=hT_dst_cols, in_=src), writes=[PT[b], T_dst])
                else:
                    add("dve", lambda e: e.tensor_copy(out=hT_dst_cols, in_=src), writes=[PT[b], T_dst])

            def stage_tr(i, tl):
                par = i % 2
                items = []
                for j, (src, nt) in enumerate(tl):
                    items.append(lambda j=j, nt=nt: transpose_rows(hb[:, par, j, :], T_hb[par][j], nt, hT[:, par, :, j * 128:j * 128 + nt],
                                                                   T_hT[par][j], eng=("act" if j % 2 == 0 else "dve")))
                return items

            def feat_rmsnorm(ps_banks, n, dst_fn, T_dst_list, width):
                for m in range(2):
                    add("act", lambda e, m=m: e.activation(out=sq[:, m, 0:n], in_=PSB[ps_banks[m]][:, 0:n], func=AF.Square),
                        writes=[PT[ps_banks[m]], T_sq])

                def tail():
                    b = rot_pj.next()
                    for m in range(2):
                        add("pe", lambda e, m=m: e.matmul(PSB[b][:, 0:n], lhsT=ones_bf[:, :], rhs=sq[:, m, 0:n], start=(m == 0), stop=(m == 1)),
                            reads=[T_sq], writes=[PT[b]])
                    add("act", lambda e: e.activation(out=rr[:, 0:n], in_=PSB[b][:, 0:n], func=AF.Ln, scale=1.0 / width, bias=EPS),
                        writes=[PT[b], T_rr])
                    add("act", lambda e: e.activation(out=rr[:, 0:n], in_=rr[:, 0:n], func=AF.Exp, scale=-0.5), writes=[T_rr])
                    for m in range(2):
                        add("dve", lambda e, m=m: e.tensor_tensor(out=dst_fn(m), in0=PSB[ps_banks[m]][:, 0:n], in1=rr[:, 0:n], op=ALU.mult),
                            reads=[T_rr], writes=[PT[ps_banks[m]]] + T_dst_list)
                return tail

            def proj(bank, ncols_out, col0, n, rhs_fn, reads):
                for k in range(KC):
                    add("pe", lambda e, k=k: e.matmul(PSB[bank][0:ncols_out, 0:n], lhsT=w_in[:, k, col0:col0 + ncols_out], rhs=rhs_fn(k),
                                                      start=(k == 0), stop=(k == KC - 1)), reads=list(reads) + [T_win[k]], writes=[PT[bank]])

            def stage_proj(st):
                par = st % 2
                ci = st % 2
                dma(lambda e: e.dma_start(out=csk[:, ci, :], in_=P["csk"][st]), writes=[T_csk[ci]])
                bk = [rot_cv.next(), rot_cv.next()]
                groups = []
                state = {}

                def g_ckv(m):
                    def w():
                        proj(bk[m], 128, QL + m * 128, 512, lambda k: hT[:, par, k, :], T_hT[par])
                        if m == 1:
                            state["tail1"] = feat_rmsnorm(bk, 512, lambda mm: ckvn[:, mm, st * 512:(st + 1) * 512], [T_ckvn[st]], KVL)
                    return w
                groups.append(g_ckv(0))
                groups.append(g_ckv(1))

                def g_kr():
                    b = rot_pj.next()
                    proj(b, 64, QL + KVL, 512, lambda k: hT[:, par, k, :], T_hT[par])
                    add("dve", lambda e: e.tensor_tensor(out=krt[:, :], in0=PSB[b][0:64, :], in1=csk[:, ci, :], op=ALU.mult),
                        reads=[T_csk[ci]], writes=[PT[b], T_krt])
                groups.append(g_kr)

                def g_u(g):
                    def w():
                        b = rot_pj.next()
                        proj(b, 128, QL + KVL + 2 * DR + g * 128, 512, lambda k: hT[:, par, k, :], T_hT[par])
                        if g % 2 == 0:
                            add("act", lambda e: e.copy(out=ut[:, par, g, :], in_=PSB[b][:, :]), writes=[PT[b], T_ut[par][g]])
                        else:
                            add("dve", lambda e: e.tensor_copy(out=ut[:, par, g, :], in_=PSB[b][:, :]), writes=[PT[b], T_ut[par][g]])
                    return w
                for g in range(NG):
                    groups.append(g_u(g))

                def tail():
                    b2 = rot_pj.next()
                    add("pe", lambda e: e.matmul(PSB[b2][0:32, :], lhsT=sel[0:64, :], rhs=krt[:, :], start=True, stop=True),
                        reads=[T_krt, T_sel], writes=[PT[b2]])
                    add("dve", lambda e: e.tensor_copy(out=kt[64:96, 0, st * 512:(st + 1) * 512], in_=PSB[b2][0:32, :]),
                        writes=[PT[b2], T_ktr[0][st]])
                    add("act", lambda e: e.copy(out=kt[64:96, 1, st * 512:(st + 1) * 512], in_=PSB[b2][0:32, :]),
                        writes=[PT[b2], T_ktr[1][st]])
                groups.append(tail)
                groups.append(lambda: state["tail1"]())
                return groups

            def stage_fnet(st):
                par = st % 2
                pqs, zs = [], []
                for j in range(4):
                    pqs.append(lambda j=j: fnet_pq(st, par, j))
                    zs.append(lambda j=j: fnet_z(st, par, j))
                return [pqs[0], pqs[1], zs[0], pqs[2], zs[1], pqs[3], zs[2], zs[3]]

            def fnet_pq(st, par, j):
                if True:
                    t = st * 4 + j
                    di = t % 2
                    dma(lambda e, di=di, t=t: e.dma_start(out=dftb[:, di, :, :], in_=P["dft"][t]), writes=[T_dft[di]])
                    bb = [rot_pj.next(), rot_pj.next()]
                    for g in range(NG):
                        bsel = bb[g // 2]
                        add("pe", lambda e, g=g, j=j, bsel=bsel: e.matmul(PSB[bsel][:, (g % 2) * 256:(g % 2) * 256 + 256],
                                                                          lhsT=ut[:, par, g, j * 128:(j + 1) * 128], rhs=ab[:, g, :], start=True, stop=True),
                            reads=[T_ut[par][g], T_ab], writes=[PT[bsel]])
                    for h2 in range(2):
                        add("dve", lambda e, h2=h2, di=di, bb=bb: e.tensor_copy(
                            out=pq[:, di, :].rearrange("p (a g d) -> p g a d", a=2, g=NG)[:, 2 * h2:2 * h2 + 2, :, :],
                            in_=PSB[bb[h2]][:, :].rearrange("p (g a d) -> p g a d", g=2, a=2)), writes=[PT[bb[h2]], T_pq[di]])

            def fnet_z(st, par, j):
                if True:
                    t = st * 4 + j
                    di = t % 2
                    if True:
                        zb = [rot_pj.next(), rot_pj.next()]
                        Pv = pq[:, di, 0:512]
                        Qv = pq[:, di, 512:1024]
                        seq = [(zb[0], 0, Pv, True, False), (zb[1], 0, Qv, True, False), (zb[1], 1, Pv, False, True), (zb[0], 2, Qv, False, True)]
                        for (zbk, mi, rhs, st_, sp_) in seq:
                            add("pe", lambda e, zbk=zbk, mi=mi, rhs=rhs, st_=st_, sp_=sp_: e.matmul(
                                PSB[zbk][:, :], lhsT=dftb[:, di, mi, :], rhs=rhs, start=st_, stop=sp_),
                                reads=[T_dft[di], T_pq[di]], writes=[PT[zbk]])
                        add("act", lambda e: e.copy(out=zt[:, di, 0:512], in_=PSB[zb[0]][:, :]), writes=[PT[zb[0]], T_zt[di]])
                        add("dve", lambda e: e.tensor_copy(out=zt[:, di, 512:1024], in_=PSB[zb[1]][:, :]), writes=[PT[zb[1]], T_zt[di]])
                        for jj in range(NJ):
                            add("pool", lambda e, jj=jj: e.dma_start(out=P["zd"][:, NJ * t + jj, :], in_=zt[jj * N1:(jj + 1) * N1, di, :]),
                                reads=[T_zt[di]], is_dma=True)

            seq_tl = [[(P["xs"][(st * 4 + j) * 128:(st * 4 + j + 1) * 128, :], 128) for j in range(4)] for st in range(NST)]
            own_tl = []
            for (c0, n) in blocks:
                tl = []
                for j in range((n + 127) // 128):
                    nt = min(128, n - j * 128)
                    tl.append((P["xo"][c0 + j * 128:c0 + j * 128 + nt, :], nt))
                own_tl.append(tl)
            all_tl = seq_tl + own_tl
            NU = len(all_tl)

            def stage_cq(bi, par):
                c0, n = blocks[bi]
                bk = [rot_cv.next(), rot_cv.next()]
                state = {}

                def g(m):
                    def w():
                        proj(bk[m], 128, m * 128, n, lambda k: hT[:, par, k, 0:n], T_hT[par])
                        if m == 1:
                            state["t"] = feat_rmsnorm(bk, n, lambda mm: cqn[:, mm, c0:c0 + n], [T_cqn[bi]], QL)
                    return w
                return [g(0), g(1), lambda: state["t"]()]

            stage_load(0, all_tl[0])
            for it in range(NU + 3):
                if it + 1 < NU:
                    stage_load(it + 1, all_tl[it + 1])
                tr = stage_tr(it - 1, all_tl[it - 1]) if 0 <= it - 1 < NU else []
                u2 = it - 2
                if 0 <= u2 < NST:
                    ga = stage_proj(u2)
                    ga_slots = (1, 3, 6, 9, 13, 17, 19, 15, 11)
                elif NST <= u2 < NU:
                    ga = stage_cq(u2 - NST, u2 % 2)
                    ga_slots = (1, 3, 11)
                else:
                    ga, ga_slots = [], ()
                gb = stage_fnet(it - 3) if 0 <= it - 3 < NST else []
                sched_ = []
                for sl_, w_ in zip((0, 5, 9.5, 13.5), tr):
                    sched_.append((sl_, w_))
                for sl_, w_ in zip(ga_slots, ga):
                    sched_.append((sl_, w_))
                for sl_, w_ in zip((4, 7, 10, 14, 16, 18, 20, 21), gb):
                    sched_.append((sl_, w_))
                if it < NU:
                    ni = stage_norm(it, all_tl[it])
                    nsq = (len(ni) - 1) // 2
                    for sl_, w_ in zip((1.5, 3.5, 5.5, 7.5)[:nsq], ni[:nsq]):
                        sched_.append((sl_, w_))
                    sched_.append((9.2, ni[nsq]))
                    for sl_, w_ in zip((11.5, 13.2, 15.5, 17.5)[:nsq], ni[nsq + 1:]):
                        sched_.append((sl_, w_))
                for _, w_ in sorted(sched_, key=lambda x_: x_[0]):
                    w_()
            S_.barrier()

            w_uq = sb(p + "w_uq"); w_ukv = sb(p + "w_ukv"); csq = sb(p + "csq")
            if pi == 0:
                load_cast(w_uq, W["w_uq"], 2 * NH * DQ, gcol0=16, nkc=2)
                load_cast(w_ukv, W["w_ukv"], 1024, gcol0=18, nkc=2)
            else:
                dma(lambda e: e.dma_start(out=w_uq[:, :, :], in_=WS["w_uq"]))
                dma(lambda e: e.dma_start(out=w_ukv[:, :, :], in_=WS["w_ukv"]))
            dma(lambda e: e.dma_start(out=csq[64:96, :, :], in_=P["csq"]))
            c2 = sb(p + "c2")
            dma(lambda e: e.dma_start(out=c2[:, :, :], in_=P["c2"]))
            S_.barrier()
            if pi == 0:
                dma(lambda e: e.dma_start(out=WS["w_uq"], in_=w_uq[:, :, :]))
                dma(lambda e: e.dma_start(out=WS["w_ukv"], in_=w_ukv[:, :, :]))
            v2 = sb(p + "v2"); qt = sb(p + "qt"); pT = sb(p + "pT"); qtmp = sb(p + "qtmp"); den = sb(p + "den"); rb = den
            attnT = sb(p + "attnT"); fT = sb(p + "fT"); zk = sb(p + "zk")
            T_v2 = [[Tile() for _ in range(NKC // 4)] for _ in range(2)]
            add("pool", lambda e: e.memset(v2[:, :, :, :, DV:DV + 1].rearrange("p a k x o -> p (a k x) o"), 1.0),
                writes=[t_ for l_ in T_v2 for t_ in l_])
            T_zk = [Tile(), Tile()]
            if pi == 0:
                pst = sb("pc_st"); pob = sb("pc_ob")
                T_pst = Tile(); T_pob = [Tile(), Tile()]
                kk_ = 0
                for f in range(NF):
                    jobs = ((W["w_gate"][f].rearrange("p k c -> p (k c)"), WS["wg"][f][:, 0, :], True),
                            (W["w_up"][f].rearrange("p k c -> p (k c)"), WS["wg"][f][:, 1, :], True),
                            (W["w_down"][f], WS["wd"][f], False))
                    for (src_, dst_, sc_) in jobs:
                        oi = kk_ % 2
                        kk_ += 1
                        add("pool", lambda e, src_=src_: e.dma_start(out=pst[:, :], in_=src_), writes=[T_pst], is_dma=True)
                        if sc_:
                            add("pool", lambda e, oi=oi: e.tensor_tensor(
                                out=pob[:, oi, :].rearrange("p (k c) -> p k c", k=KC), in0=pst[:, :].rearrange("p (k c) -> p k c", k=KC),
                                in1=gcols[:, 8:16].unsqueeze(2).to_broadcast([128, KC, 128]), op=ALU.mult), reads=[T_pst], writes=[T_pob[oi]])
                        else:
                            add("pool", lambda e, oi=oi: e.tensor_copy(out=pob[:, oi, :], in_=pst[:, :]), reads=[T_pst], writes=[T_pob[oi]])
                        add("pool", lambda e, oi=oi, dst_=dst_: e.dma_start(out=dst_, in_=pob[:, oi, :]), reads=[T_pob[oi]], is_dma=True)

            def gen_F(k1):
                def w():
                    zi = k1 % 2
                    dma(lambda e: e.dma_start(out=zk[:, zi, :], in_=P["zd"][k1]), writes=[T_zk[zi]])
                    b = rot_g.next()
                    for c in range(4):
                        for ri in range(2):
                            add("pe", lambda e, c=c, ri=ri: e.matmul(
                                PSB[b][:, c * N2C:(c + 1) * N2C], lhsT=zk[:, zi, ri * 512 + c * 128:ri * 512 + (c + 1) * 128],
                                rhs=c2[:, ri, :], start=(ri == 0), stop=(ri == 1)), reads=[T_zk[zi]], writes=[PT[b]])
                    src = PSB[b][:, 0:4 * N2C].rearrange("p (c k) -> p c k", c=4)
                    add("dve", lambda e: e.tensor_copy(
                        out=fT[:, :, 0:NOWN].rearrange("p c (k a) -> p c k a", a=N1)[:, :, :, k1], in_=src[:, :, 1:1 + NK2]), writes=[PT[b]])
                    if k1 == N1 - 1:
                        add("dve", lambda e: e.tensor_copy(out=fT[:, :, NOWN:NOWN + 1], in_=src[:, :, 0:1]), writes=[PT[b]])
                    if k1 == 0:
                        add("dve", lambda e: e.tensor_copy(out=fT[:, :, NOWN + 1:NOWN + 2], in_=src[:, :, N2C - 1:N2C]), writes=[PT[b]])
                return w
            f_items = [gen_F(k1) for k1 in range(N1)]
            T_qt = [Tile() for _ in range(len(blocks))]
            T_pT = [Tile() for _ in range(4)]
            T_qtmp = [Tile(), Tile()]; T_den = [Tile(), Tile()]; T_rb = [Tile(), Tile()]
            T_attn = [[Tile() for _ in blocks] for _ in range(4)]
            rot_g = Rot([0, 1])
            rot_s = Rot([2, 3, 4, 5])
            rot_o = Rot([6, 7])
            pT_i = [0]
            qi_ = [0]

            def gen_K(h):
                hp = h % 2
                items = []
                for st in range(NST):
                    def w(st=st):
                        b = rot_g.next()
                        for m in range(2):
                            add("pe", lambda e, m=m: e.matmul(PSB[b][0:DN, :], lhsT=w_ukv[:, m, h * DN:(h + 1) * DN],
                                                             rhs=ckvn[:, m, st * 512:(st + 1) * 512], start=(m == 0), stop=(m == 1)),
                                reads=[T_ckvn[st]], writes=[PT[b]])
                        add("dve", lambda e: e.tensor_copy(out=kt[0:DN, hp, st * 512:(st + 1) * 512], in_=PSB[b][0:DN, :]),
                            writes=[PT[b], T_ktn[hp][st]])
                    items.append(w)
                return items

            def gen_V(pr):
                vp = pr % 2
                items = []
                for kg in range(NKC // 4):
                    def w(kg=kg):
                        b = rot_g.next()
                        for kk in range(4):
                            kc = kg * 4 + kk
                            for m in range(2):
                                add("pe", lambda e, m=m, kc=kc, kk=kk: e.matmul(
                                    PSB[b][:, kk * 128:(kk + 1) * 128], lhsT=ckvn[:, m, kc * 128:(kc + 1) * 128],
                                    rhs=w_ukv[:, m, 512 + pr * 128:512 + (pr + 1) * 128], start=(m == 0), stop=(m == 1)),
                                    reads=[T_ckvn[kc // 4]], writes=[PT[b]])
                        for x2 in range(2):
                            add("dve", lambda e, x2=x2: e.tensor_copy(
                                out=v2[:, vp, kg * 4:(kg + 1) * 4, x2, 0:DV],
                                in_=PSB[b][:, :].rearrange("p (k x d) -> p k x d", k=4, x=2)[:, :, x2, :]), writes=[PT[b], T_v2[vp][kg]])
                    items.append(w)
                return items

            def gen_Q(h, bi):
                c0, n = blocks[bi]

                def w():
                    ba, bb2 = rot_g.next(), rot_g.next()
                    qi = 0
                    for (bk_, off) in ((ba, 0), (bb2, NH * DQ)):
                        for m in range(2):
                            add("pe", lambda e, bk_=bk_, off=off, m=m: e.matmul(
                                PSB[bk_][0:DQ, 0:n], lhsT=w_uq[:, m, off + h * DQ:off + (h + 1) * DQ], rhs=cqn[:, m, c0:c0 + n],
                                start=(m == 0), stop=(m == 1)), reads=[T_cqn[bi]], writes=[PT[bk_]])
                    add("dve", lambda e: e.tensor_copy(out=qt[0:DN, c0:c0 + n], in_=PSB[ba][0:DN, 0:n]), writes=[PT[ba], T_qt[bi]])
                    add("dve", lambda e: e.tensor_tensor(out=qtmp[64:96, qi, 0, 0:n], in0=PSB[ba][64:96, 0:n],
                                                         in1=csq[64:96, 0, c0:c0 + n], op=ALU.mult), writes=[PT[ba], T_qtmp[qi]])
                    add("dve", lambda e: e.tensor_tensor(out=qtmp[64:96, qi, 1, 0:n], in0=PSB[bb2][64:96, 0:n],
                                                         in1=csq[64:96, 1, c0:c0 + n], op=ALU.mult), writes=[PT[bb2], T_qtmp[qi]])
                    add("dve", lambda e: e.tensor_tensor(out=qt[64:96, c0:c0 + n], in0=qtmp[64:96, qi, 0, 0:n],
                                                          in1=qtmp[64:96, qi, 1, 0:n], op=ALU.add), reads=[T_qtmp[qi]], writes=[T_qt[bi]])
                return [w]

            for w in gen_K(0) + gen_V(0) + gen_Q(0, 0):
                w()
            units = [(h, bi) for h in range(NH) for bi in range(len(blocks))]
            nfull = len(blocks) - 1
            carry = []
            fin_i = [0]
            for ui, (h, bi) in enumerate(units):
                hh = h % 2
                hp = h % 2
                vp = (h // 2) % 2
                c0, n = blocks[bi]
                todo = list(carry)
                carry = []
                if ui + 1 < len(units):
                    todo += gen_Q(*units[ui + 1])
                if bi < nfull:
                    perf = (N1 + NH * nfull - 1) // (NH * nfull)
                    for _ in range(perf):
                        if f_items:
                            todo.append(f_items.pop(0))
                if h + 1 < NH and bi < nfull:
                    ks = gen_K(h + 1)
                    per = (len(ks) + nfull - 1) // nfull
                    todo += ks[bi * per:(bi + 1) * per]
                    if hh == 1:
                        vs = gen_V(h // 2 + 1)
                        per = (len(vs) + nfull - 1) // nfull
                        todo += vs[bi * per:(bi + 1) * per]
                G = max(1, min(NKC, 512 // n))
                ngrp = NKC // G
                ob = rot_o.next()
                pend = []

                def pv(gi, slot, G=G, n=n, ob=ob, hh=hh, vp=vp):
                    for kk in range(G):
                        kc = gi * G + kk
                        add("pe", lambda e, kc=kc, kk=kk: e.matmul(
                            PSB[ob][0:DV + 1, 0:n], lhsT=v2[:, vp, kc, hh, :], rhs=pT[:, slot, kk * n:(kk + 1) * n],
                            start=(kc == 0), stop=(kc == NKC - 1)), reads=[T_v2[vp][kc // 4], T_pT[slot]], writes=[PT[ob]])

                every = max(1, (ngrp - 3) // max(1, len(todo))) if ngrp > 4 else 1
                for gi in range(ngrp):
                    sbk = rot_s.next()
                    for kk in range(G):
                        kc = gi * G + kk
                        add("pe", lambda e, sbk=sbk, kc=kc, kk=kk, n=n, hp=hp, c0=c0: e.matmul(
                            PSB[sbk][:, kk * n:(kk + 1) * n], lhsT=kt[0:DQ, hp, kc * 128:(kc + 1) * 128], rhs=qt[0:DQ, c0:c0 + n],
                            start=True, stop=True), reads=[T_ktn[hp][kc // 4], T_ktr[hp][kc // 4], T_qt[bi]], writes=[PT[sbk]])
                    slot = pT_i[0] % 4
                    pT_i[0] += 1
                    add("act", lambda e, sbk=sbk, slot=slot, G=G, n=n: e.activation(out=pT[:, slot, 0:G * n], in_=PSB[sbk][:, 0:G * n],
                                                                                    func=AF.Exp, scale=SCALE), writes=[PT[sbk], T_pT[slot]])
                    pend.append((gi, slot))
                    if len(pend) > 3:
                        pv(*pend.pop(0))
                    if todo and gi >= 2 and (gi - 2) % every == 0:
                        todo.pop(0)()
                while pend:
                    pv(*pend.pop(0))
                while todo:
                    todo.pop(0)()
                fi = fin_i[0] % 2
                fin_i[0] += 1
                add("dve", lambda e, ob=ob, n=n, fi=fi: e.reciprocal(out=den[64:65, fi, 0:n], in_=PSB[ob][64:65, 0:n]), writes=[PT[ob], T_den[fi]])

                def fin(ob=ob, n=n, c0=c0, h=h, hh=hh, bi=bi, fi=fi):
                    bb3 = rot_g.next()
                    add("pe", lambda e: e.matmul(PSB[bb3][0:DV, 0:n], lhsT=ones_f[64:65, 0:DV], rhs=den[64:65, fi, 0:n], start=True, stop=True),
                        reads=[T_den[fi]], writes=[PT[bb3]])
                    add("dve", lambda e: e.tensor_copy(out=rb[0:DV, fi, 0:n], in_=PSB[bb3][0:DV, 0:n]), writes=[PT[bb3], T_rb[fi]])
                    add("dve", lambda e: e.tensor_tensor(
                        out=attnT[hh * 64:(hh + 1) * 64, h // 2, c0:c0 + n], in0=PSB[ob][0:DV, 0:n], in1=rb[0:DV, fi, 0:n], op=ALU.mult),
                        reads=[T_rb[fi]], writes=[PT[ob], T_attn[h // 2][bi]])
                carry = [fin]
            for w in carry + f_items:
                w()
            S_.barrier()

            w_out = sb(p + "w_out")
            T_wout = [Tile() for _ in range(KC)]
            if pi == 0:
                load_cast(w_out, W["w_out"], D, gcol0=None, nkc=KC, T_dst=T_wout)
                add("pool", lambda e: e.dma_start(out=WS["w_out"], in_=w_out[:, :, :]), reads=T_wout, is_dma=True)
            else:
                dma(lambda e: e.dma_start(out=w_out[:, :, :], in_=WS["w_out"]), writes=T_wout)
            xb = sb(p + "xb"); x1 = sb(p + "x1"); h2b = sb(p + "h2b"); h2T = sb(p + "h2T")
            NSL = 6
            T_xb = [Tile() for _ in range(NSL)]; T_x1 = [Tile() for _ in range(NSL)]; T_h2b = [Tile() for _ in range(NSL)]
            T_h2T = [Tile() for _ in range((NQ + 127) // 128)]
            rot_y = Rot([0, 1, 2, 3, 4, 5])
            rot_tp = Rot([6, 7])
            tiles = [(r0, min(128, NOWN - r0)) for r0 in range(0, NOWN, 128)] + [(NOWN, 2)]
            ybs = {}

            def b_mm(ti):
                r0, nt = tiles[ti]
                i2 = ti % NSL
                bi = min(r0 // 512, len(blocks) - 1) if r0 < NOWN else len(blocks) - 1
                dma(lambda e: e.dma_start(out=xb[0:nt, i2, :], in_=P["xo"][r0:r0 + nt, :]), writes=[T_xb[i2]])
                yb = [rot_y.next(), rot_y.next()]
                ybs[ti] = yb
                for hf in range(2):
                    for c in range(8):
                        src_t = attnT[:, c, r0:r0 + nt] if c < 4 else fT[:, c - 4, r0:r0 + nt]
                        add("pe", lambda e, hf=hf, c=c, src_t=src_t: e.matmul(
                            PSB[yb[hf]][0:nt, :], lhsT=src_t, rhs=w_out[:, c, hf * 512:(hf + 1) * 512], start=(c == 0), stop=(c == 7)),
                            reads=[T_attn[cc][bi] for cc in range(4)] + [T_wout[c]], writes=[PT[yb[hf]]])

            sls = {}

            def b_epi_a(ti):
                r0, nt = tiles[ti]
                yb = ybs[ti]
                sl, T_s = sm_slot()
                sls[ti] = (sl, T_s)
                for hf in range(2):
                    add("act", lambda e, hf=hf: e.activation(out=junk[0:nt, 0:512], in_=PSB[yb[hf]][0:nt, :], func=AF.Square,
                                                             accum_out=sl[0:nt, hf:hf + 1]), writes=[PT[yb[hf]], T_s])
                add("dve", lambda e: e.tensor_tensor(out=sl[0:nt, 2:3], in0=sl[0:nt, 0:1], in1=sl[0:nt, 1:2], op=ALU.add), writes=[T_s])
                add("act", lambda e: e.activation(out=sl[0:nt, 3:4], in_=sl[0:nt, 2:3], func=AF.Ln, scale=1.0 / D, bias=EPS), writes=[T_s])
                add("act", lambda e: e.activation(out=sl[0:nt, 4:5], in_=sl[0:nt, 3:4], func=AF.Exp, scale=-0.5), writes=[T_s])

            def b_epi_b(ti):
                r0, nt = tiles[ti]
                i2 = ti % NSL
                yb = ybs[ti]
                sl, T_s = sls[ti]
                for hf in range(2):
                    add("dve", lambda e, hf=hf: e.scalar_tensor_tensor(
                        out=x1[0:nt, i2, hf * 512:(hf + 1) * 512], in0=PSB[yb[hf]][0:nt, :], scalar=sl[0:nt, 4:5],
                        in1=gpm[0:nt, hf * 512:(hf + 1) * 512], op0=ALU.mult, op1=ALU.mult), reads=[T_s], writes=[PT[yb[hf]], T_x1[i2]])
                add("dve", lambda e: e.tensor_tensor(out=x1[0:nt, i2, :], in0=x1[0:nt, i2, :], in1=xb[0:nt, i2, :], op=ALU.add),
                    reads=[T_xb[i2]], writes=[T_x1[i2]])
                if r0 < NOWN:
                    add("pool", lambda e: e.dma_start(out=P["x1s"][r0:r0 + nt, :], in_=x1[0:nt, i2, :]), reads=[T_x1[i2]], is_dma=True)

            def b_epi_c(ti):
                r0, nt = tiles[ti]
                i2 = ti % NSL
                sl, T_s = sls[ti]
                add("act", lambda e: e.activation(out=junk[0:nt, :], in_=x1[0:nt, i2, :], func=AF.Square, accum_out=sl[0:nt, 8:9]),
                    reads=[T_x1[i2]], writes=[T_s])
                add("act", lambda e: e.activation(out=sl[0:nt, 9:10], in_=sl[0:nt, 8:9], func=AF.Ln, scale=1.0 / D, bias=EPS), writes=[T_s])
                add("act", lambda e: e.activation(out=sl[0:nt, 10:11], in_=sl[0:nt, 9:10], func=AF.Exp, scale=-0.5), writes=[T_s])
                add("act", lambda e: e.activation(out=h2b[0:nt, i2, :], in_=x1[0:nt, i2, :], func=AF.Copy, scale=sl[0:nt, 10:11]),
                    reads=[T_x1[i2], T_s], writes=[T_h2b[i2]])

            def b_tr(ti):
                r0, nt = tiles[ti]
                i2 = ti % NSL
                transpose_rows(h2b[:, i2, :], T_h2b[i2], nt, h2T[:, :, r0:r0 + nt], T_h2T[ti], eng="dve")

            for it in range(len(tiles) + 4):
                if it < len(tiles):
                    b_mm(it)
                if 0 <= it - 1 < len(tiles):
                    b_epi_a(it - 1)
                if 0 <= it - 3 < len(tiles):
                    b_epi_c(it - 3)
                if 0 <= it - 2 < len(tiles):
                    b_epi_b(it - 2)
                if 0 <= it - 4 < len(tiles):
                    b_tr(it - 4)
            S_.barrier()

            actT = sb(p + "actT"); wd = sb(p + "wd"); wgb = sb(p + "wgb"); wgst = None; wdst = None
            gsb = sb(p + "gsb"); usb = sb(p + "usb"); cv = sb(p + "cv"); nbh = sb(p + "nbh"); maskt = sb(p + "mask")
            x1l = sb(p + "x1l"); yo = sb(p + "yo")
            T_mask = Tile()
            dma(lambda e: e.dma_start(out=maskt[:, :], in_=P["mask"]), writes=[T_mask])
            T_wgst = [Tile(), Tile()]; T_wgb = [Tile(), Tile()]; T_wdst = [Tile(), Tile()]
            T_wd = [Tile() for _ in range(NF)]
            T_gh = [Tile(), Tile()]; T_gb = [[Tile() for _ in range(FG // 512)] for _ in range(2)]
            T_usb = [[Tile() for _ in range(FG // 512)] for _ in range(2)]; T_cv = [Tile(), Tile()]; T_nbh = Tile()
            T_act = [Tile() for _ in range(NF)]
            T_x1l = [Tile(), Tile()]; T_yo = [Tile(), Tile()]
            for fg in range(NFG):
                first_group = False
                t0 = fg * FG
                lcol = NOWN if fg == 0 else t0 - 1
                rcol = NOWN + 1 if fg == NFG - 1 else t0 + FG
                lm = 0 if fg == 0 else 2
                rm = 1 if fg == NFG - 1 else 2
                nblk = FG // 512
                add("pool", lambda e, lcol=lcol: e.tensor_copy(out=nbh[:, :, 0:1], in_=h2T[:, :, lcol:lcol + 1]), writes=[T_nbh])
                add("pool", lambda e, rcol=rcol: e.tensor_copy(out=nbh[:, :, 1:2], in_=h2T[:, :, rcol:rcol + 1]), writes=[T_nbh])
                rot_gu = Rot([0, 1, 2, 3, 4, 5])
                rot_nb = Rot([6, 7])

                def conv_stage(f, bl, nblk=nblk):
                    fp = f % 2
                    ci = bl % 2
                    o = bl * 512
                    g_reads = [T_gh[fp]] + [T_gb[fp][x_] for x_ in range(max(0, bl - 1), min(nblk, bl + 2))]
                    add("dve", lambda e: e.tensor_scalar(out=cv[:, ci, 0, :], in0=gsb[:, fp, o:o + 512], scalar1=convw[:, f, 0:1],
                                                         scalar2=convw[:, f, 3:4], op0=ALU.mult, op1=ALU.add), reads=g_reads, writes=[T_cv[ci]])
                    add("dve", lambda e: e.scalar_tensor_tensor(out=cv[:, ci, 1, :], in0=gsb[:, fp, o + 1:o + 513], scalar=convw[:, f, 1:2],
                                                                in1=cv[:, ci, 0, :], op0=ALU.mult, op1=ALU.add), reads=g_reads, writes=[T_cv[ci]])
                    add("dve", lambda e: e.scalar_tensor_tensor(out=cv[:, ci, 2, :], in0=gsb[:, fp, o + 2:o + 514], scalar=convw[:, f, 2:3],
                                                                in1=cv[:, ci, 1, :], op0=ALU.mult, op1=ALU.add), reads=g_reads, writes=[T_cv[ci]])
                    add("act", lambda e: e.activation(out=cv[:, ci, 0, :], in_=cv[:, ci, 2, :], func=AF.Gelu_apprx_tanh), writes=[T_cv[ci]])
                    add("dve", lambda e: e.tensor_tensor(out=actT[:, f, o:o + 512], in0=cv[:, ci, 0, :], in1=usb[:, fp, o:o + 512], op=ALU.mult),
                        reads=[T_cv[ci], T_usb[fp][bl]], writes=[T_act[f]])

                for f in range(NF):
                    wi = f % 2
                    fp = f % 2
                    if first_group:
                        for gu, wsrc in ((0, W["w_gate"]), (1, W["w_up"])):
                            dma(lambda e, wi=wi, gu=gu, wsrc=wsrc, f=f: e.dma_start(out=wgst[:, wi, gu, :], in_=wsrc[f].rearrange("p k c -> p (k c)")),
                                writes=[T_wgst[wi]])
                        for gu in range(2):
                            add("pool", lambda e, wi=wi, gu=gu: e.tensor_tensor(
                                out=wgb[:, wi, gu, :].rearrange("p (k c) -> p k c", k=KC), in0=wgst[:, wi, gu, :].rearrange("p (k c) -> p k c", k=KC),
                                in1=gcols[:, 8:16].unsqueeze(2).to_broadcast([128, KC, 128]), op=ALU.mult), reads=[T_wgst[wi]], writes=[T_wgb[wi]])
                        dma(lambda e, wi=wi, f=f: e.dma_start(out=WS["wg"][f], in_=wgb[:, wi, :, :]), reads=[T_wgb[wi]])
                        dma(lambda e, wi=wi, f=f: e.dma_start(out=wdst[:, wi, :], in_=W["w_down"][f]), writes=[T_wdst[wi]])
                        add("act", lambda e, wi=wi, f=f: e.copy(out=wd[:, f, :], in_=wdst[:, wi, :]), reads=[T_wdst[wi]], writes=[T_wd[f]])
                        dma(lambda e, f=f: e.dma_start(out=WS["wd"][f], in_=wd[:, f, :]), reads=[T_wd[f]])
                    else:
                        dma(lambda e, wi=wi, f=f: e.dma_start(out=wgb[:, wi, :, :], in_=WS["wg"][f]), writes=[T_wgb[wi]])
                        add("pool", lambda e, f=f: e.dma_start(out=wd[:, f, :], in_=WS["wd"][f]), writes=[T_wd[f]], is_dma=True)
                    b = rot_nb.next()
                    for k in range(KC):
                        add("pe", lambda e, b=b, k=k, wi=wi: e.matmul(PSB[b][:, 0:2], lhsT=wgb[:, wi, 0, k * 128:(k + 1) * 128], rhs=nbh[:, k, :],
                                                                      start=(k == 0), stop=(k == KC - 1)), reads=[T_wgb[wi], T_nbh], writes=[PT[b]])
                    add("dve", lambda e, b=b, lm=lm, fp=fp: e.tensor_tensor(out=gsb[:, fp, 0:1], in0=PSB[b][:, 0:1], in1=maskt[:, lm:lm + 1], op=ALU.mult),
                        reads=[T_mask], writes=[PT[b], T_gh[fp]])
                    add("dve", lambda e, b=b, rm=rm, fp=fp: e.tensor_tensor(out=gsb[:, fp, FG + 1:FG + 2], in0=PSB[b][:, 1:2], in1=maskt[:, rm:rm + 1], op=ALU.mult),
                        reads=[T_mask], writes=[PT[b], T_gh[fp]])
                    for bl in range(nblk):
                        c0 = t0 + bl * 512
                        bg, bu = rot_gu.next(), rot_gu.next()
                        for (bk_, gu) in ((bg, 0), (bu, 1)):
                            for k in range(KC):
                                add("pe", lambda e, bk_=bk_, gu=gu, k=k, wi=wi, c0=c0: e.matmul(
                                    PSB[bk_][:, :], lhsT=wgb[:, wi, gu, k * 128:(k + 1) * 128], rhs=h2T[:, k, c0:c0 + 512],
                                    start=(k == 0), stop=(k == KC - 1)), reads=[T_wgb[wi]], writes=[PT[bk_]])
                        add("act", lambda e, bg=bg, bl=bl, fp=fp: e.copy(out=gsb[:, fp, 1 + bl * 512:1 + (bl + 1) * 512], in_=PSB[bg][:, :]),
                            writes=[PT[bg], T_gb[fp][bl]])
                        add("act", lambda e, bu=bu, bl=bl, fp=fp: e.copy(out=usb[:, fp, bl * 512:(bl + 1) * 512], in_=PSB[bu][:, :]),
                            writes=[PT[bu], T_usb[fp][bl]])
                        if f >= 1:
                            conv_stage(f - 1, bl)
                for bl in range(nblk):
                    conv_stage(NF - 1, bl)
                rot_d = Rot([0, 1, 2, 3, 4, 5, 6, 7])
                for tt in range(FG // 128):
                    r0 = t0 + tt * 128
                    i2 = tt % 2
                    dma(lambda e, i2=i2, r0=r0: e.dma_start(out=x1l[:, i2, :], in_=P["x1s"][r0:r0 + 128, :]), writes=[T_x1l[i2]])
                    yb = [rot_d.next(), rot_d.next()]
                    for hf in range(2):
                        for f in range(NF):
                            add("pe", lambda e, hf=hf, f=f, yb=yb, tt=tt: e.matmul(
                                PSB[yb[hf]][:, :], lhsT=actT[:, f, tt * 128:(tt + 1) * 128], rhs=wd[:, f, hf * 512:(hf + 1) * 512],
                                start=(f == 0), stop=(f == NF - 1)), reads=[T_act[f], T_wd[f]], writes=[PT[yb[hf]]])
                    sl, T_s = sm_slot()
                    for hf in range(2):
                        add("act", lambda e, hf=hf, yb=yb, sl=sl: e.activation(out=junk[:, 0:512], in_=PSB[yb[hf]][:, :], func=AF.Square,
                                                                             accum_out=sl[:, hf:hf + 1]), writes=[PT[yb[hf]], T_s])
                    add("dve", lambda e, sl=sl: e.tensor_tensor(out=sl[:, 2:3], in0=sl[:, 0:1], in1=sl[:, 1:2], op=ALU.add), writes=[T_s])
                    add("act", lambda e, sl=sl: e.activation(out=sl[:, 3:4], in_=sl[:, 2:3], func=AF.Ln, scale=1.0 / D, bias=EPS), writes=[T_s])
                    add("act", lambda e, sl=sl: e.activation(out=sl[:, 4:5], in_=sl[:, 3:4], func=AF.Exp, scale=-0.5), writes=[T_s])
                    for hf in range(2):
                        add("dve", lambda e, hf=hf, yb=yb, sl=sl, i2=i2: e.scalar_tensor_tensor(
                            out=yo[:, i2, hf * 512:(hf + 1) * 512], in0=PSB[yb[hf]][:, :], scalar=sl[:, 4:5],
                            in1=gpf[:, hf * 512:(hf + 1) * 512], op0=ALU.mult, op1=ALU.mult), reads=[T_s], writes=[PT[yb[hf]], T_yo[i2]])
                    add("dve", lambda e, i2=i2: e.tensor_tensor(out=yo[:, i2, :], in0=yo[:, i2, :], in1=x1l[:, i2, :], op=ALU.add),
                        reads=[T_x1l[i2]], writes=[T_yo[i2]])
                    add("pool", lambda e, i2=i2, r0=r0: e.dma_start(out=P["y"][r0:r0 + 128, :], in_=yo[:, i2, :]), reads=[T_yo[i2]], is_dma=True)
                if fg == NFG - 1:
                    S_.barrier()

        for pi_, pc_ in enumerate(parts):
            do_part(pi_, pc_)
        S_.emit(block, sems_eng, sems_dma)
    return nc


_CACHE = {}


def run(cfg, x_prompt, x_sample, g_pre_mix, w_in, g_q, w_uq, g_kv, w_ukv, w_fnet, w_out, g_post_mix, g_pre_ffn,
        w_gate, w_up, conv_w, conv_b, w_down, g_post_ffn):
    f = lambda a: np.asarray(a, dtype=np.float32)
    parts = cfg["parts"]
    xs = [f(x_prompt), f(x_sample)]
    hw = host_weights(f(w_in)[0], f(w_uq)[0], f(w_ukv)[0], f(w_fnet)[0], f(w_out)[0], f(w_gate)[0], f(w_up)[0], f(conv_w)[0],
                      f(conv_b)[0], f(w_down)[0], f(g_pre_mix)[0], f(g_q)[0], f(g_kv)[0], f(g_post_mix)[0], f(g_pre_ffn)[0],
                      f(g_post_ffn)[0])
    key = (parts[0]["S"], parts[1]["S"])
    if key not in _CACHE:
        _CACHE[key] = build_program(cfg)
    nc = _CACHE[key]
    in_maps = []
    for c in range(8):
        m = dict(hw)
        for pi, pc in enumerate(parts):
            seq = c // pc["nsplit"]
            q = c % pc["nsplit"]
            hc = host_consts(pc, q)
            x = xs[pi][seq]
            m["xs%d" % pi] = np.ascontiguousarray(x[hc["perm"]])
            m["xo%d" % pi] = np.ascontiguousarray(x[hc["pos_own"]])
            m["dft%d" % pi] = hc["dft"]
            m["csk%d" % pi] = np.ascontiguousarray(hc["csk"])
            m["csq%d" % pi] = np.ascontiguousarray(hc["csq"])
            m["c2own%d" % pi] = np.ascontiguousarray(hc["c2"])
            m["mask%d" % pi] = hc["mask"]
        in_maps.append(m)
    res = run_bass_kernel_spmd(nc, in_maps, core_ids=list(range(8)))
    outs = []
    for pi, pc in enumerate(parts):
        y = np.zeros((pc["nbatch"], pc["S"], D), np.float32)
        for c in range(8):
            seq = c // pc["nsplit"]
            q = c % pc["nsplit"]
            y[seq, q * pc["NOWN"]:(q + 1) * pc["NOWN"]] = res.results[c]["y%d" % pi]
        outs.append(y)
    return tuple(outs)


def kernel(**inputs):
    return run(make_cfg(), **inputs)
```

```python
import math
from contextlib import ExitStack

import numpy as np
import ml_dtypes

import concourse.bass as bass
import concourse.mybir as mybir
from concourse.bass_utils import run_bass_kernel_spmd

F32 = mybir.dt.float32
BF16 = mybir.dt.bfloat16
AF = mybir.ActivationFunctionType
ALU = mybir.AluOpType
NPBF = ml_dtypes.bfloat16

D = 1024
KC = 8
QL = 256
KVL = 256
NH = 8
DN = 64
DR = 32
DV = 64
DQ = DN + DR
FW = 512
NG = 4
DFF = 2816
NF = DFF // 128
EPS = 1e-6
THETA = 10000.0
SCALE = 1.0 / math.sqrt(DQ)
WIN_COLS = QL + KVL + 2 * DR + FW

ENGS = ("pe", "act", "dve", "pool", "sp")
NDMA = 24


class Tile:
    __slots__ = ("name", "w", "r", "rd")

    def __init__(self, name=""):
        self.name = name
        self.w = None
        self.r = {}
        self.rd = []


class Op:
    __slots__ = ("eng", "fn", "deps", "signal", "count", "is_dma", "sem", "waits", "prewait")

    def __init__(self, eng, fn, is_dma):
        self.eng = eng
        self.fn = fn
        self.deps = []
        self.signal = False
        self.count = 0
        self.is_dma = is_dma
        self.sem = None
        self.waits = None
        self.prewait = None


class Sched:
    def __init__(self):
        self.ops = {e: [] for e in ENGS}
        self.all_ops = []
        self.last = {e: None for e in ENGS}
        self.dma_rr = 0
        self.dma_rr2 = 0
        self.dma_last = [None] * NDMA

    def add(self, eng, fn, reads=(), writes=(), deps=(), is_dma=False):
        op = Op(eng, fn, is_dma)
        d = []
        for t in reads:
            if t.w is not None:
                d.append(t.w)
        for t in writes:
            if t.w is not None:
                d.append(t.w)
            d.extend(t.r.values())
            d.extend(t.rd)
        d.extend(deps)
        seen = set()
        for x in d:
            if x is None or x is op or id(x) in seen:
                continue
            seen.add(id(x))
            if x.eng == "pe" and eng == "pe" and not x.is_dma and not is_dma:
                continue
            op.deps.append(x)
        for t in reads:
            if is_dma:
                t.rd.append(op)
            else:
                t.r[eng] = op
        for t in writes:
            t.w = op
            t.r = {}
            t.rd = []
        if is_dma:
            if eng == "sp":
                s = self.dma_rr % 16
                self.dma_rr += 1
            else:
                s = 16 + self.dma_rr2 % (NDMA - 16)
                self.dma_rr2 += 1
            op.sem = s
            op.prewait = self.dma_last[s]
            self.dma_last[s] = op
        self.ops[eng].append(op)
        self.all_ops.append(op)
        if fn is not None:
            self.last[eng] = op
        return op

    def dma(self, fn, reads=(), writes=(), deps=()):
        return self.add("sp", fn, reads, writes, deps, is_dma=True)

    def barrier(self):
        pend = [self.last[e] for e in ENGS if self.last[e] is not None]
        pend += [o for o in self.dma_last if o is not None]
        for e in ENGS:
            self.add(e, None, deps=pend)

    def finalize(self):
        for op in self.all_ops:
            for d in op.deps:
                d.signal = True
            if op.prewait is not None:
                op.prewait.signal = True
        cnt = {e: 0 for e in ENGS}
        dcnt = [0] * NDMA
        for op in self.all_ops:
            if op.is_dma:
                op.signal = True
                dcnt[op.sem] += 16
                op.count = dcnt[op.sem]
            elif op.signal:
                assert op.fn is not None
                cnt[op.eng] += 1
                op.count = cnt[op.eng]
        waited = {e: {} for e in ENGS}
        for op in self.all_ops:
            w = {}
            dl = list(op.deps)
            if op.prewait is not None:
                dl.append(op.prewait)
            for d in dl:
                key = ("dma", d.sem) if d.is_dma else ("eng", d.eng)
                if w.get(key, 0) < d.count:
                    w[key] = d.count
            wd = waited[op.eng]
            out = []
            for key, val in w.items():
                if wd.get(key, 0) >= val:
                    continue
                wd[key] = val
                out.append((key, val))
            op.waits = out
        self.final_counts = cnt
        self.final_dma = dcnt

    def emit(self, block, sems_eng, sems_dma):
        self.finalize()
        engmap = {"pe": "tensor", "act": "scalar", "dve": "vector", "pool": "gpsimd", "sp": "sync"}

        def semof(key):
            return sems_dma[key[1]] if key[0] == "dma" else sems_eng[key[1]]

        def run(ename):
            def body(e):
                for op in self.ops[ename]:
                    for key, val in op.waits:
                        e.wait_ge(semof(key), val)
                    if op.fn is None:
                        continue
                    ins = op.fn(e)
                    if op.signal:
                        if op.is_dma:
                            ins.then_inc(sems_dma[op.sem], 16)
                        else:
                            ins.then_inc(sems_eng[ename], 1)
                if ename == "sp":
                    for k in ENGS:
                        if self.final_counts[k] > 0:
                            e.wait_ge(sems_eng[k], self.final_counts[k])
                    for s, c in enumerate(self.final_dma):
                        if c > 0:
                            e.wait_ge(sems_dma[s], c)
            return body

        for ename in ENGS:
            getattr(block, engmap[ename])(run(ename))


class Plan:
    def __init__(self, cap):
        self.cap = cap
        self.items = []

    def add(self, name, shape, dtype, live):
        esz = 4 if dtype == F32 else 2
        n = 1
        for s in shape[1:]:
            n *= s
        size = (n * esz + 63) // 64 * 64
        it = dict(name=name, shape=list(shape), dtype=dtype, live=frozenset(live), size=size, off=None)
        self.items.append(it)
        return it

    def solve(self):
        placed = []
        for it in sorted(self.items, key=lambda i: -i["size"]):
            cands = sorted([(p["off"], p["off"] + p["size"]) for p in placed if p["live"] & it["live"]])
            off = 0
            for a, b in cands:
                if off + it["size"] <= a:
                    break
                off = max(off, b)
            assert off + it["size"] <= self.cap, ("SBUF overflow", it["name"], off, it["size"])
            it["off"] = off
            placed.append(it)


def make_cfg(sp=8192, ss=4096):
    parts = []
    for (S, nsplit, nb) in ((sp, 4, 2), (ss, 2, 4)):
        nown = S // nsplit
        n1 = S // 128
        assert 128 % n1 == 0 and nown % 512 == 0 and nown % n1 == 0
        fg = min(1024, nown)
        parts.append(dict(S=S, nsplit=nsplit, nbatch=nb, NOWN=nown, NQ=nown + 2, N1=n1, NJ=128 // n1, NT=S // 128,
                          NST=S // 512, NK2=nown // n1, N2C=nown // n1 + 2, FG=fg, NFG=nown // fg))
    return dict(parts=parts)


def perm_rows(pc):
    N1, NJ, NT = pc["N1"], pc["NJ"], pc["NT"]
    t = np.arange(NT)[:, None, None]
    j = np.arange(NJ)[None, :, None]
    s1 = np.arange(N1)[None, None, :]
    return (128 * s1 + NJ * t + j).reshape(-1)


def rope_tab(pos):
    inv = THETA ** (-np.arange(0, DR, 2, dtype=np.float64) / DR)
    ang = pos.astype(np.float64)[None, :] * inv[:, None]
    c, s = np.cos(ang), np.sin(ang)
    cc = np.concatenate([c, c], 0)
    ss = np.concatenate([-s, s], 0)
    return cc, ss


def host_consts(pc, q):
    S, N1, NJ, NT, NOWN = pc["S"], pc["N1"], pc["NJ"], pc["NT"], pc["NOWN"]
    perm = perm_rows(pc)
    dft = np.zeros((NT, 128, 3, 128), np.float64)
    k1 = np.arange(N1)
    for t in range(NT):
        for j in range(NJ):
            s = perm[128 * t + j * N1: 128 * t + (j + 1) * N1]
            ang = 2 * np.pi * ((s[:, None] * k1[None, :]) % S) / S
            c, sn = np.cos(ang) / math.sqrt(S), np.sin(ang) / math.sqrt(S)
            sl = slice(j * N1, (j + 1) * N1)
            dft[t, sl, 0, sl] = c
            dft[t, sl, 1, sl] = sn
            dft[t, sl, 2, sl] = -sn
    cc, ss = rope_tab(perm)
    csk = np.concatenate([cc, ss], 0).reshape(64, pc["NST"], 512).transpose(1, 0, 2)
    a = q * NOWN
    pos_own = np.concatenate([np.arange(a, a + NOWN), [(a - 1) % S, (a + NOWN) % S]])
    cq, sq = rope_tab(pos_own)
    csq = np.stack([cq, sq], 1)
    k2lo = a // N1
    k2 = (np.arange(k2lo - 1, k2lo + pc["NK2"] + 1)) % 128
    s2 = np.arange(128)
    ang2 = 2 * np.pi * ((s2[:, None] * k2[None, :]) % 128) / 128
    c2 = np.stack([np.cos(ang2), -np.sin(ang2)], 1)
    mask = np.ones((128, 3), np.float32)
    mask[:, 0] = 0.0 if a == 0 else 1.0
    mask[:, 1] = 0.0 if a + NOWN >= S else 1.0
    return dict(dft=dft.astype(NPBF), csk=csk.astype(np.float32), csq=csq.astype(NPBF),
                c2=c2.astype(NPBF), mask=mask, pos_own=pos_own, perm=perm)


def host_weights(w_in, w_uq, w_ukv, w_fnet, w_out, w_gate, w_up, conv_w, conv_b, w_down,
                 g_pre_mix, g_q, g_kv, g_post_mix, g_pre_ffn, g_post_ffn):
    def kmaj(w):
        k, n = w.shape
        return np.ascontiguousarray(w.reshape(k // 128, 128, n).transpose(1, 0, 2))

    def gcol(g):
        return np.ascontiguousarray(g.reshape(-1, 128).T)

    o = {}
    kr = w_in[:, QL + KVL:QL + KVL + DR]
    kr_sw = np.concatenate([kr[:, DR // 2:], kr[:, :DR // 2]], 1)
    o["w_in"] = kmaj(np.concatenate([w_in[:, :QL + KVL], kr, kr_sw, w_in[:, QL + KVL + DR:]], 1))
    wq = w_uq.reshape(QL, NH, DQ)
    wq_sw = np.concatenate([wq[:, :, :DN], wq[:, :, DN + DR // 2:], wq[:, :, DN:DN + DR // 2]], 2)
    o["w_uq"] = kmaj(np.concatenate([wq.reshape(QL, -1), wq_sw.reshape(QL, -1)], 1))
    wkv = w_ukv.reshape(KVL, NH, DN + DV)
    o["w_ukv"] = kmaj(np.concatenate([wkv[:, :, :DN].reshape(KVL, -1), wkv[:, :, DN:].reshape(KVL, -1)], 1))
    o["w_fnet"] = np.ascontiguousarray(w_fnet.transpose(1, 0, 2))
    o["w_out"] = kmaj(w_out)
    o["w_gate"] = np.ascontiguousarray(w_gate.reshape(KC, 128, NF, 128).transpose(2, 1, 0, 3))
    o["w_up"] = np.ascontiguousarray(w_up.reshape(KC, 128, NF, 128).transpose(2, 1, 0, 3))
    o["w_down"] = np.ascontiguousarray(w_down.reshape(NF, 128, D))
    cw = np.concatenate([conv_w, conv_b[None, :]], 0)
    o["conv"] = np.ascontiguousarray(cw.reshape(4, NF, 128).transpose(2, 1, 0))
    o["gcols"] = np.ascontiguousarray(np.concatenate([gcol(g_pre_mix), gcol(g_pre_ffn), gcol(g_q), gcol(g_kv)], 1))
    o["g_post_mix"] = np.ascontiguousarray(np.broadcast_to(g_post_mix[None, :], (128, D)))
    o["g_post_ffn"] = np.ascontiguousarray(np.broadcast_to(g_post_ffn[None, :], (128, D)))
    c = np.arange(128)
    ang = 2 * np.pi * ((c[:, None] * c[None, :]) % 128) / 128
    o["ccsc"] = np.stack([np.cos(ang), np.sin(ang)], 1).astype(np.float64) / math.sqrt(128)
    o["ccsc"] = o["ccsc"].astype(NPBF)
    o["ident"] = np.eye(128, dtype=np.float32).astype(NPBF)
    return {k: (v if v.dtype == NPBF else np.ascontiguousarray(v, dtype=np.float32)) for k, v in o.items()}


def build_program(cfg):
    nc = bass.Bass("TRN2", target_bir_lowering=False)
    parts = cfg["parts"]
    S_ = Sched()

    def din(name, shape, dt=F32):
        return nc.dram_tensor(name, list(shape), dt, kind="ExternalInput").ap()

    def dscr(name, shape, dt):
        return nc.dram_tensor(name, list(shape), dt, kind="Internal").ap()

    W = dict(
        w_in=din("w_in", [128, KC, WIN_COLS]), w_uq=din("w_uq", [128, 2, 2 * NH * DQ]), w_ukv=din("w_ukv", [128, 2, 1024]),
        w_fnet=din("w_fnet", [128, NG, 128]), w_out=din("w_out", [128, KC, D]),
        w_gate=din("w_gate", [NF, 128, KC, 128]), w_up=din("w_up", [NF, 128, KC, 128]), w_down=din("w_down", [NF, 128, D]),
        conv=din("conv", [128, NF, 4]), gcols=din("gcols", [128, 20]), g_post_mix=din("g_post_mix", [128, D]),
        g_post_ffn=din("g_post_ffn", [128, D]), ccsc=din("ccsc", [128, 2, 128], BF16), ident=din("ident", [128, 128], BF16),
    )
    WS = dict(wg=dscr("wgs", [NF, 128, 2, KC * 128], BF16), wd=dscr("wds", [NF, 128, D], BF16),
              w_in=dscr("wins", [128, KC, WIN_COLS], BF16), w_uq=dscr("wuqs", [128, 2, 2 * NH * DQ], BF16),
              w_ukv=dscr("wukvs", [128, 2, 1024], BF16), w_out=dscr("wouts", [128, KC, D], BF16))
    PD = []
    for pi, pc in enumerate(parts):
        PD.append(dict(
            xs=din("xs%d" % pi, [pc["S"], D]), xo=din("xo%d" % pi, [pc["NQ"], D]),
            dft=din("dft%d" % pi, [pc["NT"], 128, 3, 128], BF16), csk=din("csk%d" % pi, [pc["NST"], 64, 512]),
            csq=din("csq%d" % pi, [32, 2, pc["NQ"]], BF16), c2=din("c2own%d" % pi, [128, 2, pc["N2C"]], BF16),
            mask=din("mask%d" % pi, [128, 3]),
            y=nc.dram_tensor("y%d" % pi, [pc["NOWN"], D], F32, kind="ExternalOutput").ap(),
            zd=dscr("zd%d" % pi, [pc["N1"], 128, 1024], BF16), x1s=dscr("x1s%d" % pi, [pc["NOWN"], D], F32),
        ))

    SB_BASE = 16512
    plan = Plan(229344 - SB_BASE)
    ALLPH = set(range(10))

    def PH(pi, *names):
        m = dict(A=0, ATTW=1, ATT=2, B=3, FFN=4)
        return {5 * pi + m[n] for n in names}

    B_ = {}

    def decl(name, shape, dt, live):
        B_[name] = plan.add(name, shape, dt, live)

    decl("ident", [128, 128], BF16, ALLPH)
    decl("ones_bf", [128, 128], BF16, ALLPH)
    decl("ones_f", [128, 64], F32, ALLPH)
    decl("sel", [64, 32], BF16, ALLPH)
    decl("gpm", [128, D], F32, ALLPH)
    decl("gpf", [128, D], F32, ALLPH)
    decl("gcols", [128, 20], F32, ALLPH)
    decl("conv", [128, NF, 4], F32, ALLPH)
    decl("ab", [128, NG, 256], BF16, ALLPH)
    decl("junk", [128, D], BF16, ALLPH)
    decl("sm", [128, 128], F32, ALLPH)
    MIX = lambda pi: PH(pi, "A", "ATT", "B")
    for pi, pc in enumerate(parts):
        S, NQ = pc["S"], pc["NQ"]
        p = "p%d_" % pi
        decl(p + "w_in", [128, KC, WIN_COLS], BF16, PH(pi, "A"))
        decl(p + "w_uq", [128, 2, 2 * NH * DQ], BF16, PH(pi, "ATTW", "ATT"))
        decl(p + "w_ukv", [128, 2, 1024], BF16, PH(pi, "ATTW", "ATT"))
        decl(p + "w_out", [128, KC, D], BF16, PH(pi, "B"))
        decl(p + "wstage", [128, 2, 1536], F32, PH(pi, "A", "ATTW", "B"))
        decl(p + "ckvn", [128, 2, S], BF16, PH(pi, "A", "ATTW", "ATT"))
        decl(p + "kt", [128, 2, S], BF16, PH(pi, "A", "ATTW", "ATT"))
        decl(p + "cqn", [128, 2, NQ], BF16, PH(pi, "A", "ATTW", "ATT"))
        decl(p + "csq", [128, 2, NQ], BF16, PH(pi, "ATTW", "ATT"))
        decl(p + "attnT", [128, 4, NQ], BF16, PH(pi, "ATT", "B"))
        decl(p + "fT", [128, 4, NQ], BF16, PH(pi, "ATT", "B"))
        decl(p + "xst", [128, 2, 4, D], F32, PH(pi, "A"))
        decl(p + "hb", [128, 2, 4, D], BF16, PH(pi, "A"))
        decl(p + "hT", [128, 2, KC, 512], BF16, PH(pi, "A"))
        decl(p + "sq", [128, 2, 512], BF16, PH(pi, "A"))
        decl(p + "rr", [128, 512], F32, PH(pi, "A"))
        decl(p + "ut", [128, 2, NG, 512], BF16, PH(pi, "A"))
        decl(p + "pq", [128, 2, 1024], BF16, PH(pi, "A"))
        decl(p + "zt", [128, 2, 1024], BF16, PH(pi, "A"))
        decl(p + "csk", [64, 2, 512], F32, PH(pi, "A"))
        decl(p + "krt", [64, 512], BF16, PH(pi, "A"))
        decl(p + "dft", [128, 2, 3, 128], BF16, PH(pi, "A"))
        decl(p + "v2", [128, 2, S // 128, 2, DV + 1], BF16, PH(pi, "ATT"))
        decl(p + "qt", [128, NQ], BF16, PH(pi, "ATT"))
        decl(p + "pT", [128, 4, 512], BF16, PH(pi, "ATT"))
        decl(p + "qtmp", [128, 1, 2, 512], F32, PH(pi, "ATT"))
        decl(p + "den", [128, 2, 512], F32, PH(pi, "ATT"))
        if pi == 0:
            decl("pc_st", [128, D], F32, PH(pi, "ATT"))
            decl("pc_ob", [128, 2, D], BF16, PH(pi, "ATT"))
        decl(p + "zk", [128, 2, 1024], BF16, PH(pi, "ATT"))
        decl(p + "c2", [128, 2, pc["N2C"]], BF16, PH(pi, "ATTW", "ATT"))
        decl(p + "xb", [128, 6, D], F32, PH(pi, "B"))
        decl(p + "x1", [128, 6, D], F32, PH(pi, "B"))
        decl(p + "h2b", [128, 6, D], BF16, PH(pi, "B"))
        decl(p + "h2T", [128, KC, NQ], BF16, PH(pi, "B", "FFN"))
        FG = pc["FG"]
        decl(p + "actT", [128, NF, FG], BF16, PH(pi, "FFN"))
        decl(p + "wd", [128, NF, D], BF16, PH(pi, "FFN"))
        decl(p + "wgb", [128, 2, 2, KC * 128], BF16, PH(pi, "FFN"))
        decl(p + "gsb", [128, 2, FG + 2], F32, PH(pi, "FFN"))
        decl(p + "usb", [128, 2, FG], BF16, PH(pi, "FFN"))
        decl(p + "cv", [128, 2, 3, 512], F32, PH(pi, "FFN"))
        decl(p + "nbh", [128, KC, 2], BF16, PH(pi, "FFN"))
        decl(p + "mask", [128, 3], F32, PH(pi, "FFN"))
        decl(p + "x1l", [128, 2, D], F32, PH(pi, "FFN"))
        decl(p + "yo", [128, 2, D], F32, PH(pi, "FFN"))
    plan.solve()

    def sb(name):
        it = B_[name]
        if "h" not in it:
            it["h"] = nc.alloc_sbuf_tensor_at(name, it["shape"], it["dtype"], offset=SB_BASE + it["off"])
        return it["h"]

    es = ExitStack()
    with es:
        PSB = [es.enter_context(nc.psum_tensor("psb%d" % i, [128, 512], F32)) for i in range(8)]
        PT = [Tile("ps%d" % i) for i in range(8)]
        sems_eng = {e: es.enter_context(nc.semaphore("s_" + e)) for e in ENGS}
        sems_dma = [es.enter_context(nc.semaphore("sd%d" % i)) for i in range(NDMA)]
        block = es.enter_context(nc.Block())

        add = S_.add
        dma = S_.dma

        class Rot:
            def __init__(self, banks):
                self.b = list(banks)
                self.i = 0

            def next(self):
                b = self.b[self.i % len(self.b)]
                self.i += 1
                return b

        sm = sb("sm")
        sm_t = [Tile("sm%d" % i) for i in range(8)]
        sm_i = [0]

        def sm_slot():
            i = sm_i[0] % 8
            sm_i[0] += 1
            return sm[:, i * 16:(i + 1) * 16], sm_t[i]

        ident = sb("ident"); ones_bf = sb("ones_bf"); ones_f = sb("ones_f"); gpm = sb("gpm"); gpf = sb("gpf")
        gcols = sb("gcols"); convw = sb("conv"); ab = sb("ab"); junk = sb("junk")
        T_const = Tile("const")
        T_junk = Tile("junk")
        sel = sb("sel")
        T_sel = Tile("sel")
        T_ident = Tile("ident")
        dma(lambda e: e.dma_start(out=ident[:], in_=W["ident"]), writes=[T_ident])
        dma(lambda e: e.dma_start(out=gpm[:], in_=W["g_post_mix"]))
        dma(lambda e: e.dma_start(out=gpf[:], in_=W["g_post_ffn"]))
        dma(lambda e: e.dma_start(out=gcols[:], in_=W["gcols"]))
        dma(lambda e: e.dma_start(out=convw[:], in_=W["conv"]))
        add("dve", lambda e: e.tensor_copy(out=sel[0:32, :], in_=ident[0:32, 0:32]), reads=[T_ident], writes=[T_sel])
        add("dve", lambda e: e.tensor_copy(out=sel[32:64, :], in_=ident[32:64, 32:64]), reads=[T_ident], writes=[T_sel])
        add("pool", lambda e: e.memset(ones_bf[:], 1.0))
        add("pool", lambda e: e.memset(ones_f[:], 1.0))
        wst0 = sb("p0_wstage")
        ccsc_sb = sb("p0_sq")
        T_ab = Tile("ab")
        o1 = dma(lambda e: e.dma_start(out=ccsc_sb[:, 0, 0:256], in_=W["ccsc"].rearrange("p a b -> p (a b)")))
        o2 = dma(lambda e: e.dma_start(out=wst0[:, 0, 0:512], in_=W["w_fnet"].rearrange("p a b -> p (a b)")))
        o3 = add("dve", lambda e: e.tensor_copy(out=ccsc_sb[:, 1, :], in_=wst0[:, 0, 0:512]), deps=[o2])
        for g in range(NG):
            for cs in range(2):
                add("pe", lambda e, g=g, cs=cs: e.matmul(PSB[cs][:, g * 128:(g + 1) * 128], lhsT=ccsc_sb[:, 0, cs * 128:(cs + 1) * 128],
                                                         rhs=ccsc_sb[:, 1, g * 128:(g + 1) * 128], start=True, stop=True),
                    writes=[PT[cs]], deps=[o1, o3])
        for cs in range(2):
            add("dve", lambda e, cs=cs: e.tensor_copy(out=ab[:, :, cs * 128:(cs + 1) * 128],
                                                      in_=PSB[cs][:, :].rearrange("p (g d) -> p g d", g=NG)),
                writes=[PT[cs], T_ab])
        S_.barrier()

        def do_part(pi, pc):
            S, NQ, NOWN, N1, NJ, NT, NST = pc["S"], pc["NQ"], pc["NOWN"], pc["N1"], pc["NJ"], pc["NT"], pc["NST"]
            NK2, N2C, FG, NFG = pc["NK2"], pc["N2C"], pc["FG"], pc["NFG"]
            P = PD[pi]
            p = "p%d_" % pi
            NKC = S // 128
            blocks = [(b * 512, 512) for b in range(NOWN // 512)] + [(NOWN, 2)]

            wstage = sb(p + "wstage")
            T_wst = [Tile(), Tile()]
            wst_i = [0]

            def load_cast(dst, src, ncols, gcol0=None, nkc=None, eng="dve", T_dst=None):
                ops = []
                per = max(1, 1536 // ncols)
                for k0 in range(0, nkc, per):
                    kn = min(per, nkc - k0)
                    i = wst_i[0] % 2
                    wst_i[0] += 1
                    dma(lambda e, i=i, k0=k0, kn=kn: e.dma_start(
                        out=wstage[:, i, 0:kn * ncols].rearrange("p (k c) -> p k c", k=kn), in_=src[:, k0:k0 + kn, :]), writes=[T_wst[i]])
                    for k in range(kn):
                        if gcol0 is None:
                            ops.append(add(eng, lambda e, i=i, k=k, k0=k0: e.tensor_copy(
                                out=dst[:, k0 + k, :], in_=wstage[:, i, k * ncols:(k + 1) * ncols]), reads=[T_wst[i]],
                                writes=([T_dst[k0 + k]] if T_dst else [])))
                        else:
                            ops.append(add(eng, lambda e, i=i, k=k, k0=k0: e.tensor_scalar(
                                out=dst[:, k0 + k, :], in0=wstage[:, i, k * ncols:(k + 1) * ncols],
                                scalar1=gcols[:, gcol0 + k0 + k:gcol0 + k0 + k + 1], scalar2=None, op0=ALU.mult), reads=[T_wst[i]],
                                writes=([T_dst[k0 + k]] if T_dst else [])))
                return ops

            w_in = sb(p + "w_in")
            T_win = [Tile() for _ in range(KC)]
            if pi == 0:
                load_cast(w_in, W["w_in"], WIN_COLS, gcol0=0, nkc=KC, T_dst=T_win)
                add("pool", lambda e: e.dma_start(out=WS["w_in"], in_=w_in[:, :, :]), reads=T_win, is_dma=True)
            else:
                dma(lambda e: e.dma_start(out=w_in[:, :, :], in_=WS["w_in"]), writes=T_win)

            ckvn = sb(p + "ckvn"); kt = sb(p + "kt"); cqn = sb(p + "cqn")
            xst = sb(p + "xst"); hb = sb(p + "hb"); hT = sb(p + "hT"); sq = sb(p + "sq"); rr = sb(p + "rr")
            ut = sb(p + "ut"); pq = sb(p + "pq"); zt = sb(p + "zt"); csk = sb(p + "csk"); krt = sb(p + "krt"); dftb = sb(p + "dft")
            T_xst = [[Tile() for _ in range(4)] for _ in range(2)]
            T_hb = [[Tile() for _ in range(4)] for _ in range(2)]
            T_hT = [[Tile() for _ in range(4)] for _ in range(2)]
            T_sq = Tile(); T_rr = Tile(); T_ut = [[Tile() for _ in range(NG)] for _ in range(2)]
            T_pq = [Tile(), Tile()]; T_zt = [Tile(), Tile()]; T_csk = [Tile(), Tile()]; T_krt = Tile(); T_dft = [Tile(), Tile()]
            T_ckvn = [Tile() for _ in range(NST)]
            T_ktr = [[Tile() for _ in range(NST)] for _ in range(2)]
            T_ktn = [[Tile() for _ in range(NST)] for _ in range(2)]
            T_cqn = [Tile() for _ in range(len(blocks))]
            rot_tp = Rot([0, 1])
            rot_cv = Rot([2, 3])
            rot_pj = Rot([4, 5, 6, 7])

            def stage_load(i, tl):
                par = i % 2
                for j, (src, nt) in enumerate(tl):
                    dma(lambda e, src=src, nt=nt, j=j: e.dma_start(out=xst[0:nt, par, j, :], in_=src), writes=[T_xst[par][j]])

            def stage_norm(i, tl):
                par = i % 2
                sl, T_s = sm_slot()
                ntm = tl[0][1]
                n = len(tl)
                items = []
                for j, (src, nt) in enumerate(tl):
                    items.append(lambda nt=nt, j=j: add("act", lambda e: e.activation(out=junk[0:nt, :], in_=xst[0:nt, par, j, :], func=AF.Square,
                                                                                    accum_out=sl[0:nt, j:j + 1]), reads=[T_xst[par][j]], writes=[T_s]))

                def lnexp():
                    add("act", lambda e: e.activation(out=sl[0:ntm, 4:4 + n], in_=sl[0:ntm, 0:n], func=AF.Ln, scale=1.0 / D, bias=EPS), writes=[T_s])
                    add("act", lambda e: e.activation(out=sl[0:ntm, 8:8 + n], in_=sl[0:ntm, 4:4 + n], func=AF.Exp, scale=-0.5), writes=[T_s])
                items.append(lnexp)
                for j, (src, nt) in enumerate(tl):
                    items.append(lambda nt=nt, j=j: add("act", lambda e: e.activation(out=hb[0:nt, par, j, :], in_=xst[0:nt, par, j, :], func=AF.Copy,
                                                                                    scale=sl[0:nt, 8 + j:9 + j]), reads=[T_xst[par][j], T_s], writes=[T_hb[par][j]]))
                return items

            def transpose_rows(h_src, T_h, nt, hT_dst_cols, T_dst, eng="act"):
                b = rot_tp.next()
                pb = PSB[b][:, :].bitcast(BF16)
                for c in range(KC):
                    add("pe", lambda e, c=c: e.transpose(out=pb[:, c * 128:c * 128 + nt], in_=h_src[0:nt, c * 128:(c + 1) * 128],
                                                         identity=ident[0:nt, 0:nt]), reads=[T_h], writes=[PT[b]])
                src = pb.rearrange("p (c t) -> p c t", c=KC)[:, :, 0:nt]
                if eng == "act":
                    add("act", lambda e: e.copy(out=hT_dst_cols, in_=src), writes=[PT[b], T_dst])
                else:
                    add("dve", lambda e: e.tensor_copy(out=hT_dst_cols, in_=src), writes=[PT[b], T_dst])

            def stage_tr(i, tl):
                par = i % 2
                items = []
                for j, (src, nt) in enumerate(tl):
                    items.append(lambda j=j, nt=nt: transpose_rows(hb[:, par, j, :], T_hb[par][j], nt, hT[:, par, :, j * 128:j * 128 + nt],
                                                                   T_hT[par][j], eng=("act" if j % 2 == 0 else "dve")))
                return items

            def feat_rmsnorm(ps_banks, n, dst_fn, T_dst_list, width):
                for m in range(2):
                    add("act", lambda e, m=m: e.activation(out=sq[:, m, 0:n], in_=PSB[ps_banks[m]][:, 0:n], func=AF.Square),
                        writes=[PT[ps_banks[m]], T_sq])

                def tail():
                    b = rot_pj.next()
                    for m in range(2):
                        add("pe", lambda e, m=m: e.matmul(PSB[b][:, 0:n], lhsT=ones_bf[:, :], rhs=sq[:, m, 0:n], start=(m == 0), stop=(m == 1)),
                            reads=[T_sq], writes=[PT[b]])
                    add("act", lambda e: e.activation(out=rr[:, 0:n], in_=PSB[b][:, 0:n], func=AF.Ln, scale=1.0 / width, bias=EPS),
                        writes=[PT[b], T_rr])
                    add("act", lambda e: e.activation(out=rr[:, 0:n], in_=rr[:, 0:n], func=AF.Exp, scale=-0.5), writes=[T_rr])
                    for m in range(2):
                        add("dve", lambda e, m=m: e.tensor_tensor(out=dst_fn(m), in0=PSB[ps_banks[m]][:, 0:n], in1=rr[:, 0:n], op=ALU.mult),
                            reads=[T_rr], writes=[PT[ps_banks[m]]] + T_dst_list)
                return tail

            def proj(bank, ncols_out, col0, n, rhs_fn, reads):
                for k in range(KC):
                    add("pe", lambda e, k=k: e.matmul(PSB[bank][0:ncols_out, 0:n], lhsT=w_in[:, k, col0:col0 + ncols_out], rhs=rhs_fn(k),
                                                      start=(k == 0), stop=(k == KC - 1)), reads=list(reads) + [T_win[k]], writes=[PT[bank]])

            def stage_proj(st):
                par = st % 2
                ci = st % 2
                dma(lambda e: e.dma_start(out=csk[:, ci, :], in_=P["csk"][st]), writes=[T_csk[ci]])
                bk = [rot_cv.next(), rot_cv.next()]
                groups = []
                state = {}

                def g_ckv(m):
                    def w():
                        proj(bk[m], 128, QL + m * 128, 512, lambda k: hT[:, par, k, :], T_hT[par])
                        if m == 1:
                            state["tail1"] = feat_rmsnorm(bk, 512, lambda mm: ckvn[:, mm, st * 512:(st + 1) * 512], [T_ckvn[st]], KVL)
                    return w
                groups.append(g_ckv(0))
                groups.append(g_ckv(1))

                def g_kr():
                    b = rot_pj.next()
                    proj(b, 64, QL + KVL, 512, lambda k: hT[:, par, k, :], T_hT[par])
                    add("dve", lambda e: e.tensor_tensor(out=krt[:, :], in0=PSB[b][0:64, :], in1=csk[:, ci, :], op=ALU.mult),
                        reads=[T_csk[ci]], writes=[PT[b], T_krt])
                groups.append(g_kr)

                def g_u(g):
                    def w():
                        b = rot_pj.next()
                        proj(b, 128, QL + KVL + 2 * DR + g * 128, 512, lambda k: hT[:, par, k, :], T_hT[par])
                        if g % 2 == 0:
                            add("act", lambda e: e.copy(out=ut[:, par, g, :], in_=PSB[b][:, :]), writes=[PT[b], T_ut[par][g]])
                        else:
                            add("dve", lambda e: e.tensor_copy(out=ut[:, par, g, :], in_=PSB[b][:, :]), writes=[PT[b], T_ut[par][g]])
                    return w
                for g in range(NG):
                    groups.append(g_u(g))

                def tail():
                    b2 = rot_pj.next()
                    add("pe", lambda e: e.matmul(PSB[b2][0:32, :], lhsT=sel[0:64, :], rhs=krt[:, :], start=True, stop=True),
                        reads=[T_krt, T_sel], writes=[PT[b2]])
                    add("dve", lambda e: e.tensor_copy(out=kt[64:96, 0, st * 512:(st + 1) * 512], in_=PSB[b2][0:32, :]),
                        writes=[PT[b2], T_ktr[0][st]])
                    add("act", lambda e: e.copy(out=kt[64:96, 1, st * 512:(st + 1) * 512], in_=PSB[b2][0:32, :]),
                        writes=[PT[b2], T_ktr[1][st]])
                groups.append(tail)
                groups.append(lambda: state["tail1"]())
                return groups

            def stage_fnet(st):
                par = st % 2
                pqs, zs = [], []
                for j in range(4):
                    pqs.append(lambda j=j: fnet_pq(st, par, j))
                    zs.append(lambda j=j: fnet_z(st, par, j))
                return [pqs[0], pqs[1], zs[0], pqs[2], zs[1], pqs[3], zs[2], zs[3]]

            def fnet_pq(st, par, j):
                if True:
                    t = st * 4 + j
                    di = t % 2
                    dma(lambda e, di=di, t=t: e.dma_start(out=dftb[:, di, :, :], in_=P["dft"][t]), writes=[T_dft[di]])
                    bb = [rot_pj.next(), rot_pj.next()]
                    for g in range(NG):
                        bsel = bb[g // 2]
                        add("pe", lambda e, g=g, j=j, bsel=bsel: e.matmul(PSB[bsel][:, (g % 2) * 256:(g % 2) * 256 + 256],
                                                                          lhsT=ut[:, par, g, j * 128:(j + 1) * 128], rhs=ab[:, g, :], start=True, stop=True),
                            reads=[T_ut[par][g], T_ab], writes=[PT[bsel]])
                    for h2 in range(2):
                        add("dve", lambda e, h2=h2, di=di, bb=bb: e.tensor_copy(
                            out=pq[:, di, :].rearrange("p (a g d) -> p g a d", a=2, g=NG)[:, 2 * h2:2 * h2 + 2, :, :],
                            in_=PSB[bb[h2]][:, :].rearrange("p (g a d) -> p g a d", g=2, a=2)), writes=[PT[bb[h2]], T_pq[di]])

            def fnet_z(st, par, j):
                if True:
                    t = st * 4 + j
                    di = t % 2
                    if True:
                        zb = [rot_pj.next(), rot_pj.next()]
                        Pv = pq[:, di, 0:512]
                        Qv = pq[:, di, 512:1024]
                        seq = [(zb[0], 0, Pv, True, False), (zb[1], 0, Qv, True, False), (zb[1], 1, Pv, False, True), (zb[0], 2, Qv, False, True)]
                        for (zbk, mi, rhs, st_, sp_) in seq:
                            add("pe", lambda e, zbk=zbk, mi=mi, rhs=rhs, st_=st_, sp_=sp_: e.matmul(
                                PSB[zbk][:, :], lhsT=dftb[:, di, mi, :], rhs=rhs, start=st_, stop=sp_),
                                reads=[T_dft[di], T_pq[di]], writes=[PT[zbk]])
                        add("act", lambda e: e.copy(out=zt[:, di, 0:512], in_=PSB[zb[0]][:, :]), writes=[PT[zb[0]], T_zt[di]])
                        add("dve", lambda e: e.tensor_copy(out=zt[:, di, 512:1024], in_=PSB[zb[1]][:, :]), writes=[PT[zb[1]], T_zt[di]])
                        for jj in range(NJ):
                            add("pool", lambda e, jj=jj: e.dma_start(out=P["zd"][:, NJ * t + jj, :], in_=zt[jj * N1:(jj + 1) * N1, di, :]),
                                reads=[T_zt[di]], is_dma=True)

            seq_tl = [[(P["xs"][(st * 4 + j) * 128:(st * 4 + j + 1) * 128, :], 128) for j in range(4)] for st in range(NST)]
            own_tl = []
            for (c0, n) in blocks:
                tl = []
                for j in range((n + 127) // 128):
                    nt = min(128, n - j * 128)
                    tl.append((P["xo"][c0 + j * 128:c0 + j * 128 + nt, :], nt))
                own_tl.append(tl)
            all_tl = seq_tl + own_tl
            NU = len(all_tl)

            def stage_cq(bi, par):
                c0, n = blocks[bi]
                bk = [rot_cv.next(), rot_cv.next()]
                state = {}

                def g(m):
                    def w():
                        proj(bk[m], 128, m * 128, n, lambda k: hT[:, par, k, 0:n], T_hT[par])
                        if m == 1:
                            state["t"] = feat_rmsnorm(bk, n, lambda mm: cqn[:, mm, c0:c0 + n], [T_cqn[bi]], QL)
                    return w
                return [g(0), g(1), lambda: state["t"]()]

            stage_load(0, all_tl[0])
            for it in range(NU + 3):
                if it + 1 < NU:
                    stage_load(it + 1, all_tl[it + 1])
                tr = stage_tr(it - 1, all_tl[it - 1]) if 0 <= it - 1 < NU else []
                u2 = it - 2
                if 0 <= u2 < NST:
                    ga = stage_proj(u2)
                    ga_slots = (1, 3, 6, 9, 13, 17, 19, 15, 11)
                elif NST <= u2 < NU:
                    ga = stage_cq(u2 - NST, u2 % 2)
                    ga_slots = (1, 3, 11)
                else:
                    ga, ga_slots = [], ()
                gb = stage_fnet(it - 3) if 0 <= it - 3 < NST else []
                sched_ = []
                for sl_, w_ in zip((0, 5, 9.5, 13.5), tr):
                    sched_.append((sl_, w_))
                for sl_, w_ in zip(ga_slots, ga):
                    sched_.append((sl_, w_))
                for sl_, w_ in zip((4, 7, 10, 14, 16, 18, 20, 21), gb):
                    sched_.append((sl_, w_))
                if it < NU:
                    ni = stage_norm(it, all_tl[it])
                    nsq = (len(ni) - 1) // 2
                    for sl_, w_ in zip((1.5, 3.5, 5.5, 7.5)[:nsq], ni[:nsq]):
                        sched_.append((sl_, w_))
                    sched_.append((9.2, ni[nsq]))
                    for sl_, w_ in zip((11.5, 13.2, 15.5, 17.5)[:nsq], ni[nsq + 1:]):
                        sched_.append((sl_, w_))
                for _, w_ in sorted(sched_, key=lambda x_: x_[0]):
                    w_()
            S_.barrier()

            w_uq = sb(p + "w_uq"); w_ukv = sb(p + "w_ukv"); csq = sb(p + "csq")
            if pi == 0:
                load_cast(w_uq, W["w_uq"], 2 * NH * DQ, gcol0=16, nkc=2)
                load_cast(w_ukv, W["w_ukv"], 1024, gcol0=18, nkc=2)
            else:
                dma(lambda e: e.dma_start(out=w_uq[:, :, :], in_=WS["w_uq"]))
                dma(lambda e: e.dma_start(out=w_ukv[:, :, :], in_=WS["w_ukv"]))
            dma(lambda e: e.dma_start(out=csq[64:96, :, :], in_=P["csq"]))
            c2 = sb(p + "c2")
            dma(lambda e: e.dma_start(out=c2[:, :, :], in_=P["c2"]))
            S_.barrier()
            if pi == 0:
                dma(lambda e: e.dma_start(out=WS["w_uq"], in_=w_uq[:, :, :]))
                dma(lambda e: e.dma_start(out=WS["w_ukv"], in_=w_ukv[:, :, :]))
            v2 = sb(p + "v2"); qt = sb(p + "qt"); pT = sb(p + "pT"); qtmp = sb(p + "qtmp"); den = sb(p + "den"); rb = den
            attnT = sb(p + "attnT"); fT = sb(p + "fT"); zk = sb(p + "zk")
            T_v2 = [[Tile() for _ in range(NKC // 4)] for _ in range(2)]
            add("pool", lambda e: e.memset(v2[:, :, :, :, DV:DV + 1].rearrange("p a k x o -> p (a k x) o"), 1.0),
                writes=[t_ for l_ in T_v2 for t_ in l_])
            T_zk = [Tile(), Tile()]
            if pi == 0:
                pst = sb("pc_st"); pob = sb("pc_ob")
                T_pst = Tile(); T_pob = [Tile(), Tile()]
                kk_ = 0
                for f in range(NF):
                    jobs = ((W["w_gate"][f].rearrange("p k c -> p (k c)"), WS["wg"][f][:, 0, :], True),
                            (W["w_up"][f].rearrange("p k c -> p (k c)"), WS["wg"][f][:, 1, :], True),
                            (W["w_down"][f], WS["wd"][f], False))
                    for (src_, dst_, sc_) in jobs:
                        oi = kk_ % 2
                        kk_ += 1
                        add("pool", lambda e, src_=src_: e.dma_start(out=pst[:, :], in_=src_), writes=[T_pst], is_dma=True)
                        if sc_:
                            add("pool", lambda e, oi=oi: e.tensor_tensor(
                                out=pob[:, oi, :].rearrange("p (k c) -> p k c", k=KC), in0=pst[:, :].rearrange("p (k c) -> p k c", k=KC),
                                in1=gcols[:, 8:16].unsqueeze(2).to_broadcast([128, KC, 128]), op=ALU.mult), reads=[T_pst], writes=[T_pob[oi]])
                        else:
                            add("pool", lambda e, oi=oi: e.tensor_copy(out=pob[:, oi, :], in_=pst[:, :]), reads=[T_pst], writes=[T_pob[oi]])
                        add("pool", lambda e, oi=oi, dst_=dst_: e.dma_start(out=dst_, in_=pob[:, oi, :]), reads=[T_pob[oi]], is_dma=True)

            def gen_F(k1):
                def w():
                    zi = k1 % 2
                    dma(lambda e: e.dma_start(out=zk[:, zi, :], in_=P["zd"][k1]), writes=[T_zk[zi]])
                    b = rot_g.next()
                    for c in range(4):
                        for ri in range(2):
                            add("pe", lambda e, c=c, ri=ri: e.matmul(
                                PSB[b][:, c * N2C:(c + 1) * N2C], lhsT=zk[:, zi, ri * 512 + c * 128:ri * 512 + (c + 1) * 128],
                                rhs=c2[:, ri, :], start=(ri == 0), stop=(ri == 1)), reads=[T_zk[zi]], writes=[PT[b]])
                    src = PSB[b][:, 0:4 * N2C].rearrange("p (c k) -> p c k", c=4)
                    add("dve", lambda e: e.tensor_copy(
                        out=fT[:, :, 0:NOWN].rearrange("p c (k a) -> p c k a", a=N1)[:, :, :, k1], in_=src[:, :, 1:1 + NK2]), writes=[PT[b]])
                    if k1 == N1 - 1:
                        add("dve", lambda e: e.tensor_copy(out=fT[:, :, NOWN:NOWN + 1], in_=src[:, :, 0:1]), writes=[PT[b]])
                    if k1 == 0:
                        add("dve", lambda e: e.tensor_copy(out=fT[:, :, NOWN + 1:NOWN + 2], in_=src[:, :, N2C - 1:N2C]), writes=[PT[b]])
                return w
            f_items = [gen_F(k1) for k1 in range(N1)]
            T_qt = [Tile() for _ in range(len(blocks))]
            T_pT = [Tile() for _ in range(4)]
            T_qtmp = [Tile(), Tile()]; T_den = [Tile(), Tile()]; T_rb = [Tile(), Tile()]
            T_attn = [[Tile() for _ in blocks] for _ in range(4)]
            rot_g = Rot([0, 1])
            rot_s = Rot([2, 3, 4, 5])
            rot_o = Rot([6, 7])
            pT_i = [0]
            qi_ = [0]

            def gen_K(h):
                hp = h % 2
                items = []
                for st in range(NST):
                    def w(st=st):
                        b = rot_g.next()
                        for m in range(2):
                            add("pe", lambda e, m=m: e.matmul(PSB[b][0:DN, :], lhsT=w_ukv[:, m, h * DN:(h + 1) * DN],
                                                             rhs=ckvn[:, m, st * 512:(st + 1) * 512], start=(m == 0), stop=(m == 1)),
                                reads=[T_ckvn[st]], writes=[PT[b]])
                        add("dve", lambda e: e.tensor_copy(out=kt[0:DN, hp, st * 512:(st + 1) * 512], in_=PSB[b][0:DN, :]),
                            writes=[PT[b], T_ktn[hp][st]])
                    items.append(w)
                return items

            def gen_V(pr):
                vp = pr % 2
                items = []
                for kg in range(NKC // 4):
                    def w(kg=kg):
                        b = rot_g.next()
                        for kk in range(4):
                            kc = kg * 4 + kk
                            for m in range(2):
                                add("pe", lambda e, m=m, kc=kc, kk=kk: e.matmul(
                                    PSB[b][:, kk * 128:(kk + 1) * 128], lhsT=ckvn[:, m, kc * 128:(kc + 1) * 128],
                                    rhs=w_ukv[:, m, 512 + pr * 128:512 + (pr + 1) * 128], start=(m == 0), stop=(m == 1)),
                                    reads=[T_ckvn[kc // 4]], writes=[PT[b]])
                        for x2 in range(2):
                            add("dve", lambda e, x2=x2: e.tensor_copy(
                                out=v2[:, vp, kg * 4:(kg + 1) * 4, x2, 0:DV],
                                in_=PSB[b][:, :].rearrange("p (k x d) -> p k x d", k=4, x=2)[:, :, x2, :]), writes=[PT[b], T_v2[vp][kg]])
                    items.append(w)
                return items

            def gen_Q(h, bi):
                c0, n = blocks[bi]

                def w():
                    ba, bb2 = rot_g.next(), rot_g.next()
                    qi = 0
                    for (bk_, off) in ((ba, 0), (bb2, NH * DQ)):
                        for m in range(2):
                            add("pe", lambda e, bk_=bk_, off=off, m=m: e.matmul(
                                PSB[bk_][0:DQ, 0:n], lhsT=w_uq[:, m, off + h * DQ:off + (h + 1) * DQ], rhs=cqn[:, m, c0:c0 + n],
                                start=(m == 0), stop=(m == 1)), reads=[T_cqn[bi]], writes=[PT[bk_]])
                    add("dve", lambda e: e.tensor_copy(out=qt[0:DN, c0:c0 + n], in_=PSB[ba][0:DN, 0:n]), writes=[PT[ba], T_qt[bi]])
                    add("dve", lambda e: e.tensor_tensor(out=qtmp[64:96, qi, 0, 0:n], in0=PSB[ba][64:96, 0:n],
                                                         in1=csq[64:96, 0, c0:c0 + n], op=ALU.mult), writes=[PT[ba], T_qtmp[qi]])
                    add("dve", lambda e: e.tensor_tensor(out=qtmp[64:96, qi, 1, 0:n], in0=PSB[bb2][64:96, 0:n],
                                                         in1=csq[64:96, 1, c0:c0 + n], op=ALU.mult), writes=[PT[bb2], T_qtmp[qi]])
                    add("dve", lambda e: e.tensor_tensor(out=qt[64:96, c0:c0 + n], in0=qtmp[64:96, qi, 0, 0:n],
                                                          in1=qtmp[64:96, qi, 1, 0:n], op=ALU.add), reads=[T_qtmp[qi]], writes=[T_qt[bi]])
                return [w]

            for w in gen_K(0) + gen_V(0) + gen_Q(0, 0):
                w()
            units = [(h, bi) for h in range(NH) for bi in range(len(blocks))]
            nfull = len(blocks) - 1
            fin_i = [0]
            pendq = []
            cur_todo = [None]

            pending_fin = {}

            def pop_pv():
                pvfn, gi_, slot_, last_cb = pendq.pop(0)
                if gi_ == 0:
                    ob_ = pvfn.__defaults__[2]
                    if ob_ in pending_fin:
                        pending_fin.pop(ob_)()
                pvfn(gi_, slot_)
                if last_cb is not None:
                    last_cb()

            for ui, (h, bi) in enumerate(units):
                hh = h % 2
                hp = h % 2
                vp = (h // 2) % 2
                c0, n = blocks[bi]
                todo = []
                cur_todo[0] = todo
                if ui + 1 < len(units):
                    todo += gen_Q(*units[ui + 1])
                if bi < nfull:
                    perf = (N1 + NH * nfull - 1) // (NH * nfull)
                    for _ in range(perf):
                        if f_items:
                            todo.append(f_items.pop(0))
                if h + 1 < NH and bi < nfull:
                    ks = gen_K(h + 1)
                    per = (len(ks) + nfull - 1) // nfull
                    todo += ks[bi * per:(bi + 1) * per]
                    if hh == 1:
                        vs = gen_V(h // 2 + 1)
                        per = (len(vs) + nfull - 1) // nfull
                        todo += vs[bi * per:(bi + 1) * per]
                G = max(1, min(NKC, 512 // n))
                ngrp = NKC // G
                ob = rot_o.next()

                def pv(gi, slot, G=G, n=n, ob=ob, hh=hh, vp=vp):
                    for kk in range(G):
                        kc = gi * G + kk
                        add("pe", lambda e, kc=kc, kk=kk: e.matmul(
                            PSB[ob][0:DV + 1, 0:n], lhsT=v2[:, vp, kc, hh, :], rhs=pT[:, slot, kk * n:(kk + 1) * n],
                            start=(kc == 0), stop=(kc == NKC - 1)), reads=[T_v2[vp][kc // 4], T_pT[slot]], writes=[PT[ob]])

                def finalize(ob=ob, n=n, c0=c0, h=h, hh=hh, bi=bi):
                    fi = fin_i[0] % 2
                    fin_i[0] += 1
                    add("dve", lambda e: e.reciprocal(out=den[64:65, fi, 0:n], in_=PSB[ob][64:65, 0:n]), writes=[PT[ob], T_den[fi]])

                    state_f = {"done": False}

                    def fin():
                        if state_f["done"]:
                            return
                        state_f["done"] = True
                        pending_fin.pop(ob, None)
                        bb3 = rot_g.next()
                        add("pe", lambda e: e.matmul(PSB[bb3][0:DV, 0:n], lhsT=ones_f[64:65, 0:DV], rhs=den[64:65, fi, 0:n], start=True, stop=True),
                            reads=[T_den[fi]], writes=[PT[bb3]])
                        add("dve", lambda e: e.tensor_copy(out=rb[0:DV, fi, 0:n], in_=PSB[bb3][0:DV, 0:n]), writes=[PT[bb3], T_rb[fi]])
                        add("dve", lambda e: e.tensor_tensor(
                            out=attnT[hh * 64:(hh + 1) * 64, h // 2, c0:c0 + n], in0=PSB[ob][0:DV, 0:n], in1=rb[0:DV, fi, 0:n], op=ALU.mult),
                            reads=[T_rb[fi]], writes=[PT[ob], T_attn[h // 2][bi]])
                    pending_fin[ob] = fin
                    cur_todo[0].append(fin)

                every = max(1, (ngrp - 3) // max(1, len(todo) + 1)) if ngrp > 4 else 1
                for gi in range(ngrp):
                    sbk = rot_s.next()
                    for kk in range(G):
                        kc = gi * G + kk
                        add("pe", lambda e, sbk=sbk, kc=kc, kk=kk, n=n, hp=hp, c0=c0: e.matmul(
                            PSB[sbk][:, kk * n:(kk + 1) * n], lhsT=kt[0:DQ, hp, kc * 128:(kc + 1) * 128], rhs=qt[0:DQ, c0:c0 + n],
                            start=True, stop=True), reads=[T_ktn[hp][kc // 4], T_ktr[hp][kc // 4], T_qt[bi]], writes=[PT[sbk]])
                    slot = pT_i[0] % 4
                    pT_i[0] += 1
                    add("act", lambda e, sbk=sbk, slot=slot, G=G, n=n: e.activation(out=pT[:, slot, 0:G * n], in_=PSB[sbk][:, 0:G * n],
                                                                                    func=AF.Exp, scale=SCALE), writes=[PT[sbk], T_pT[slot]])
                    pendq.append((pv, gi, slot, finalize if gi == ngrp - 1 else None))
                    if len(pendq) > 3:
                        pop_pv()
                    if todo and gi >= 2 and (gi - 2) % every == 0:
                        todo.pop(0)()
                while todo:
                    todo.pop(0)()
            while pendq:
                pop_pv()
            while cur_todo[0]:
                cur_todo[0].pop(0)()
            for w in f_items:
                w()
            S_.barrier()

            w_out = sb(p + "w_out")
            T_wout = [Tile() for _ in range(KC)]
            if pi == 0:
                load_cast(w_out, W["w_out"], D, gcol0=None, nkc=KC, T_dst=T_wout)
                add("pool", lambda e: e.dma_start(out=WS["w_out"], in_=w_out[:, :, :]), reads=T_wout, is_dma=True)
            else:
                dma(lambda e: e.dma_start(out=w_out[:, :, :], in_=WS["w_out"]), writes=T_wout)
            xb = sb(p + "xb"); x1 = sb(p + "x1"); h2b = sb(p + "h2b"); h2T = sb(p + "h2T")
            NSL = 6
            T_xb = [Tile() for _ in range(NSL)]; T_x1 = [Tile() for _ in range(NSL)]; T_h2b = [Tile() for _ in range(NSL)]
            T_h2T = [Tile() for _ in range((NQ + 127) // 128)]
            rot_y = Rot([0, 1, 2, 3, 4, 5])
            rot_tp = Rot([6, 7])
            tiles = [(r0, min(128, NOWN - r0)) for r0 in range(0, NOWN, 128)] + [(NOWN, 2)]
            ybs = {}

            def b_mm(ti):
                r0, nt = tiles[ti]
                i2 = ti % NSL
                bi = min(r0 // 512, len(blocks) - 1) if r0 < NOWN else len(blocks) - 1
                dma(lambda e: e.dma_start(out=xb[0:nt, i2, :], in_=P["xo"][r0:r0 + nt, :]), writes=[T_xb[i2]])
                yb = [rot_y.next(), rot_y.next()]
                ybs[ti] = yb
                for hf in range(2):
                    for c in range(8):
                        src_t = attnT[:, c, r0:r0 + nt] if c < 4 else fT[:, c - 4, r0:r0 + nt]
                        add("pe", lambda e, hf=hf, c=c, src_t=src_t: e.matmul(
                            PSB[yb[hf]][0:nt, :], lhsT=src_t, rhs=w_out[:, c, hf * 512:(hf + 1) * 512], start=(c == 0), stop=(c == 7)),
                            reads=[T_attn[cc][bi] for cc in range(4)] + [T_wout[c]], writes=[PT[yb[hf]]])

            sls = {}

            def b_epi_a(ti):
                r0, nt = tiles[ti]
                yb = ybs[ti]
                sl, T_s = sm_slot()
                sls[ti] = (sl, T_s)
                for hf in range(2):
                    add("act", lambda e, hf=hf: e.activation(out=junk[0:nt, 0:512], in_=PSB[yb[hf]][0:nt, :], func=AF.Square,
                                                             accum_out=sl[0:nt, hf:hf + 1]), writes=[PT[yb[hf]], T_s])
                add("dve", lambda e: e.tensor_tensor(out=sl[0:nt, 2:3], in0=sl[0:nt, 0:1], in1=sl[0:nt, 1:2], op=ALU.add), writes=[T_s])
                add("act", lambda e: e.activation(out=sl[0:nt, 3:4], in_=sl[0:nt, 2:3], func=AF.Ln, scale=1.0 / D, bias=EPS), writes=[T_s])
                add("act", lambda e: e.activation(out=sl[0:nt, 4:5], in_=sl[0:nt, 3:4], func=AF.Exp, scale=-0.5), writes=[T_s])

            def b_epi_b(ti):
                r0, nt = tiles[ti]
                i2 = ti % NSL
                yb = ybs[ti]
                sl, T_s = sls[ti]
                for hf in range(2):
                    add("dve", lambda e, hf=hf: e.scalar_tensor_tensor(
                        out=x1[0:nt, i2, hf * 512:(hf + 1) * 512], in0=PSB[yb[hf]][0:nt, :], scalar=sl[0:nt, 4:5],
                        in1=gpm[0:nt, hf * 512:(hf + 1) * 512], op0=ALU.mult, op1=ALU.mult), reads=[T_s], writes=[PT[yb[hf]], T_x1[i2]])
                add("dve", lambda e: e.tensor_tensor(out=x1[0:nt, i2, :], in0=x1[0:nt, i2, :], in1=xb[0:nt, i2, :], op=ALU.add),
                    reads=[T_xb[i2]], writes=[T_x1[i2]])
                if r0 < NOWN:
                    add("pool", lambda e: e.dma_start(out=P["x1s"][r0:r0 + nt, :], in_=x1[0:nt, i2, :]), reads=[T_x1[i2]], is_dma=True)

            def b_epi_c(ti):
                r0, nt = tiles[ti]
                i2 = ti % NSL
                sl, T_s = sls[ti]
                add("act", lambda e: e.activation(out=junk[0:nt, :], in_=x1[0:nt, i2, :], func=AF.Square, accum_out=sl[0:nt, 8:9]),
                    reads=[T_x1[i2]], writes=[T_s])
                add("act", lambda e: e.activation(out=sl[0:nt, 9:10], in_=sl[0:nt, 8:9], func=AF.Ln, scale=1.0 / D, bias=EPS), writes=[T_s])
                add("act", lambda e: e.activation(out=sl[0:nt, 10:11], in_=sl[0:nt, 9:10], func=AF.Exp, scale=-0.5), writes=[T_s])
                add("act", lambda e: e.activation(out=h2b[0:nt, i2, :], in_=x1[0:nt, i2, :], func=AF.Copy, scale=sl[0:nt, 10:11]),
                    reads=[T_x1[i2], T_s], writes=[T_h2b[i2]])

            def b_tr(ti):
                r0, nt = tiles[ti]
                i2 = ti % NSL
                transpose_rows(h2b[:, i2, :], T_h2b[i2], nt, h2T[:, :, r0:r0 + nt], T_h2T[ti], eng="dve")

            for it in range(len(tiles) + 4):
                if it < len(tiles):
                    b_mm(it)
                if 0 <= it - 1 < len(tiles):
                    b_epi_a(it - 1)
                if 0 <= it - 3 < len(tiles):
                    b_epi_c(it - 3)
                if 0 <= it - 2 < len(tiles):
                    b_epi_b(it - 2)
                if 0 <= it - 4 < len(tiles):
                    b_tr(it - 4)
            S_.barrier()

            actT = sb(p + "actT"); wd = sb(p + "wd"); wgb = sb(p + "wgb"); wgst = None; wdst = None
            gsb = sb(p + "gsb"); usb = sb(p + "usb"); cv = sb(p + "cv"); nbh = sb(p + "nbh"); maskt = sb(p + "mask")
            x1l = sb(p + "x1l"); yo = sb(p + "yo")
            T_mask = Tile()
            dma(lambda e: e.dma_start(out=maskt[:, :], in_=P["mask"]), writes=[T_mask])
            T_wgst = [Tile(), Tile()]; T_wgb = [Tile(), Tile()]; T_wdst = [Tile(), Tile()]
            T_wd = [Tile() for _ in range(NF)]
            T_gh = [Tile(), Tile()]; T_gb = [[Tile() for _ in range(FG // 512)] for _ in range(2)]
            T_usb = [[Tile() for _ in range(FG // 512)] for _ in range(2)]; T_cv = [Tile(), Tile()]; T_nbh = Tile()
            T_act = [Tile() for _ in range(NF)]
            T_x1l = [Tile(), Tile()]; T_yo = [Tile(), Tile()]
            for fg in range(NFG):
                first_group = False
                t0 = fg * FG
                lcol = NOWN if fg == 0 else t0 - 1
                rcol = NOWN + 1 if fg == NFG - 1 else t0 + FG
                lm = 0 if fg == 0 else 2
                rm = 1 if fg == NFG - 1 else 2
                nblk = FG // 512
                add("pool", lambda e, lcol=lcol: e.tensor_copy(out=nbh[:, :, 0:1], in_=h2T[:, :, lcol:lcol + 1]), writes=[T_nbh])
                add("pool", lambda e, rcol=rcol: e.tensor_copy(out=nbh[:, :, 1:2], in_=h2T[:, :, rcol:rcol + 1]), writes=[T_nbh])
                rot_gu = Rot([0, 1, 2, 3, 4, 5])
                rot_nb = Rot([6, 7])

                def conv_stage(f, bl, nblk=nblk):
                    fp = f % 2
                    ci = bl % 2
                    o = bl * 512
                    g_reads = [T_gh[fp]] + [T_gb[fp][x_] for x_ in range(max(0, bl - 1), min(nblk, bl + 2))]
                    add("dve", lambda e: e.tensor_scalar(out=cv[:, ci, 0, :], in0=gsb[:, fp, o:o + 512], scalar1=convw[:, f, 0:1],
                                                         scalar2=convw[:, f, 3:4], op0=ALU.mult, op1=ALU.add), reads=g_reads, writes=[T_cv[ci]])
                    add("dve", lambda e: e.scalar_tensor_tensor(out=cv[:, ci, 1, :], in0=gsb[:, fp, o + 1:o + 513], scalar=convw[:, f, 1:2],
                                                                in1=cv[:, ci, 0, :], op0=ALU.mult, op1=ALU.add), reads=g_reads, writes=[T_cv[ci]])
                    add("dve", lambda e: e.scalar_tensor_tensor(out=cv[:, ci, 2, :], in0=gsb[:, fp, o + 2:o + 514], scalar=convw[:, f, 2:3],
                                                                in1=cv[:, ci, 1, :], op0=ALU.mult, op1=ALU.add), reads=g_reads, writes=[T_cv[ci]])
                    add("act", lambda e: e.activation(out=cv[:, ci, 0, :], in_=cv[:, ci, 2, :], func=AF.Gelu_apprx_tanh), writes=[T_cv[ci]])
                    add("dve", lambda e: e.tensor_tensor(out=actT[:, f, o:o + 512], in0=cv[:, ci, 0, :], in1=usb[:, fp, o:o + 512], op=ALU.mult),
                        reads=[T_cv[ci], T_usb[fp][bl]], writes=[T_act[f]])

                for f in range(NF):
                    wi = f % 2
                    fp = f % 2
                    if first_group:
                        for gu, wsrc in ((0, W["w_gate"]), (1, W["w_up"])):
                            dma(lambda e, wi=wi, gu=gu, wsrc=wsrc, f=f: e.dma_start(out=wgst[:, wi, gu, :], in_=wsrc[f].rearrange("p k c -> p (k c)")),
                                writes=[T_wgst[wi]])
                        for gu in range(2):
                            add("pool", lambda e, wi=wi, gu=gu: e.tensor_tensor(
                                out=wgb[:, wi, gu, :].rearrange("p (k c) -> p k c", k=KC), in0=wgst[:, wi, gu, :].rearrange("p (k c) -> p k c", k=KC),
                                in1=gcols[:, 8:16].unsqueeze(2).to_broadcast([128, KC, 128]), op=ALU.mult), reads=[T_wgst[wi]], writes=[T_wgb[wi]])
                        dma(lambda e, wi=wi, f=f: e.dma_start(out=WS["wg"][f], in_=wgb[:, wi, :, :]), reads=[T_wgb[wi]])
                        dma(lambda e, wi=wi, f=f: e.dma_start(out=wdst[:, wi, :], in_=W["w_down"][f]), writes=[T_wdst[wi]])
                        add("act", lambda e, wi=wi, f=f: e.copy(out=wd[:, f, :], in_=wdst[:, wi, :]), reads=[T_wdst[wi]], writes=[T_wd[f]])
                        dma(lambda e, f=f: e.dma_start(out=WS["wd"][f], in_=wd[:, f, :]), reads=[T_wd[f]])
                    else:
                        dma(lambda e, wi=wi, f=f: e.dma_start(out=wgb[:, wi, :, :], in_=WS["wg"][f]), writes=[T_wgb[wi]])
                        add("pool", lambda e, f=f: e.dma_start(out=wd[:, f, :], in_=WS["wd"][f]), writes=[T_wd[f]], is_dma=True)
                    b = rot_nb.next()
                    for k in range(KC):
                        add("pe", lambda e, b=b, k=k, wi=wi: e.matmul(PSB[b][:, 0:2], lhsT=wgb[:, wi, 0, k * 128:(k + 1) * 128], rhs=nbh[:, k, :],
                                                                      start=(k == 0), stop=(k == KC - 1)), reads=[T_wgb[wi], T_nbh], writes=[PT[b]])
                    add("dve", lambda e, b=b, lm=lm, fp=fp: e.tensor_tensor(out=gsb[:, fp, 0:1], in0=PSB[b][:, 0:1], in1=maskt[:, lm:lm + 1], op=ALU.mult),
                        reads=[T_mask], writes=[PT[b], T_gh[fp]])
                    add("dve", lambda e, b=b, rm=rm, fp=fp: e.tensor_tensor(out=gsb[:, fp, FG + 1:FG + 2], in0=PSB[b][:, 1:2], in1=maskt[:, rm:rm + 1], op=ALU.mult),
                        reads=[T_mask], writes=[PT[b], T_gh[fp]])
                    for bl in range(nblk):
                        c0 = t0 + bl * 512
                        bg, bu = rot_gu.next(), rot_gu.next()
                        for (bk_, gu) in ((bg, 0), (bu, 1)):
                            for k in range(KC):
                                add("pe", lambda e, bk_=bk_, gu=gu, k=k, wi=wi, c0=c0: e.matmul(
                                    PSB[bk_][:, :], lhsT=wgb[:, wi, gu, k * 128:(k + 1) * 128], rhs=h2T[:, k, c0:c0 + 512],
                                    start=(k == 0), stop=(k == KC - 1)), reads=[T_wgb[wi]], writes=[PT[bk_]])
                        add("act", lambda e, bg=bg, bl=bl, fp=fp: e.copy(out=gsb[:, fp, 1 + bl * 512:1 + (bl + 1) * 512], in_=PSB[bg][:, :]),
                            writes=[PT[bg], T_gb[fp][bl]])
                        add("act", lambda e, bu=bu, bl=bl, fp=fp: e.copy(out=usb[:, fp, bl * 512:(bl + 1) * 512], in_=PSB[bu][:, :]),
                            writes=[PT[bu], T_usb[fp][bl]])
                        if f >= 1:
                            conv_stage(f - 1, bl)
                for bl in range(nblk):
                    conv_stage(NF - 1, bl)
                rot_d = Rot([0, 1, 2, 3, 4, 5, 6, 7])
                for tt in range(FG // 128):
                    r0 = t0 + tt * 128
                    i2 = tt % 2
                    dma(lambda e, i2=i2, r0=r0: e.dma_start(out=x1l[:, i2, :], in_=P["x1s"][r0:r0 + 128, :]), writes=[T_x1l[i2]])
                    yb = [rot_d.next(), rot_d.next()]
                    for hf in range(2):
                        for f in range(NF):
                            add("pe", lambda e, hf=hf, f=f, yb=yb, tt=tt: e.matmul(
                                PSB[yb[hf]][:, :], lhsT=actT[:, f, tt * 128:(tt + 1) * 128], rhs=wd[:, f, hf * 512:(hf + 1) * 512],
                                start=(f == 0), stop=(f == NF - 1)), reads=[T_act[f], T_wd[f]], writes=[PT[yb[hf]]])
                    sl, T_s = sm_slot()
                    for hf in range(2):
                        add("act", lambda e, hf=hf, yb=yb, sl=sl: e.activation(out=junk[:, 0:512], in_=PSB[yb[hf]][:, :], func=AF.Square,
                                                                             accum_out=sl[:, hf:hf + 1]), writes=[PT[yb[hf]], T_s])
                    add("dve", lambda e, sl=sl: e.tensor_tensor(out=sl[:, 2:3], in0=sl[:, 0:1], in1=sl[:, 1:2], op=ALU.add), writes=[T_s])
                    add("act", lambda e, sl=sl: e.activation(out=sl[:, 3:4], in_=sl[:, 2:3], func=AF.Ln, scale=1.0 / D, bias=EPS), writes=[T_s])
                    add("act", lambda e, sl=sl: e.activation(out=sl[:, 4:5], in_=sl[:, 3:4], func=AF.Exp, scale=-0.5), writes=[T_s])
                    for hf in range(2):
                        add("dve", lambda e, hf=hf, yb=yb, sl=sl, i2=i2: e.scalar_tensor_tensor(
                            out=yo[:, i2, hf * 512:(hf + 1) * 512], in0=PSB[yb[hf]][:, :], scalar=sl[:, 4:5],
                            in1=gpf[:, hf * 512:(hf + 1) * 512], op0=ALU.mult, op1=ALU.mult), reads=[T_s], writes=[PT[yb[hf]], T_yo[i2]])
                    add("dve", lambda e, i2=i2: e.tensor_tensor(out=yo[:, i2, :], in0=yo[:, i2, :], in1=x1l[:, i2, :], op=ALU.add),
                        reads=[T_x1l[i2]], writes=[T_yo[i2]])
                    add("pool", lambda e, i2=i2, r0=r0: e.dma_start(out=P["y"][r0:r0 + 128, :], in_=yo[:, i2, :]), reads=[T_yo[i2]], is_dma=True)
                if fg == NFG - 1:
                    S_.barrier()

        for pi_, pc_ in enumerate(parts):
            do_part(pi_, pc_)
        S_.emit(block, sems_eng, sems_dma)
    return nc


_CACHE = {}


def run(cfg, x_prompt, x_sample, g_pre_mix, w_in, g_q, w_uq, g_kv, w_ukv, w_fnet, w_out, g_post_mix, g_pre_ffn,
        w_gate, w_up, conv_w, conv_b, w_down, g_post_ffn):
    f = lambda a: np.asarray(a, dtype=np.float32)
    parts = cfg["parts"]
    xs = [f(x_prompt), f(x_sample)]
    hw = host_weights(f(w_in)[0], f(w_uq)[0], f(w_ukv)[0], f(w_fnet)[0], f(w_out)[0], f(w_gate)[0], f(w_up)[0], f(conv_w)[0],
                      f(conv_b)[0], f(w_down)[0], f(g_pre_mix)[0], f(g_q)[0], f(g_kv)[0], f(g_post_mix)[0], f(g_pre_ffn)[0],
                      f(g_post_ffn)[0])
    key = (parts[0]["S"], parts[1]["S"])
    if key not in _CACHE:
        _CACHE[key] = build_program(cfg)
    nc = _CACHE[key]
    in_maps = []
    for c in range(8):
        m = dict(hw)
        for pi, pc in enumerate(parts):
            seq = c // pc["nsplit"]
            q = c % pc["nsplit"]
            hc = host_consts(pc, q)
            x = xs[pi][seq]
            m["xs%d" % pi] = np.ascontiguousarray(x[hc["perm"]])
            m["xo%d" % pi] = np.ascontiguousarray(x[hc["pos_own"]])
            m["dft%d" % pi] = hc["dft"]
            m["csk%d" % pi] = np.ascontiguousarray(hc["csk"])
            m["csq%d" % pi] = np.ascontiguousarray(hc["csq"])
            m["c2own%d" % pi] = np.ascontiguousarray(hc["c2"])
            m["mask%d" % pi] = hc["mask"]
        in_maps.append(m)
    res = run_bass_kernel_spmd(nc, in_maps, core_ids=list(range(8)))
    outs = []
    for pi, pc in enumerate(parts):
        y = np.zeros((pc["nbatch"], pc["S"], D), np.float32)
        for c in range(8):
            seq = c // pc["nsplit"]
            q = c % pc["nsplit"]
            y[seq, q * pc["NOWN"]:(q + 1) * pc["NOWN"]] = res.results[c]["y%d" % pi]
        outs.append(y)
    return tuple(outs)


def kernel(**inputs):
    return run(make_cfg(), **inputs)
```

```python
import math
from contextlib import ExitStack

import numpy as np
import ml_dtypes

import concourse.bass as bass
import concourse.mybir as mybir
from concourse.bass_utils import run_bass_kernel_spmd

F32 = mybir.dt.float32
BF16 = mybir.dt.bfloat16
AF = mybir.ActivationFunctionType
ALU = mybir.AluOpType
NPBF = ml_dtypes.bfloat16

D = 1024
KC = 8
QL = 256
KVL = 256
NH = 8
DN = 64
DR = 32
DV = 64
DQ = DN + DR
FW = 512
NG = 4
DFF = 2816
NF = DFF // 128
EPS = 1e-6
THETA = 10000.0
SCALE = 1.0 / math.sqrt(DQ)
WIN_COLS = QL + KVL + 2 * DR + FW

ENGS = ("pe", "act", "dve", "pool", "sp")
NDMA = 24


class Tile:
    __slots__ = ("name", "w", "r", "rd")

    def __init__(self, name=""):
        self.name = name
        self.w = None
        self.r = {}
        self.rd = []


class Op:
    __slots__ = ("eng", "fn", "deps", "signal", "count", "is_dma", "sem", "waits", "prewait")

    def __init__(self, eng, fn, is_dma):
        self.eng = eng
        self.fn = fn
        self.deps = []
        self.signal = False
        self.count = 0
        self.is_dma = is_dma
        self.sem = None
        self.waits = None
        self.prewait = None


class Sched:
    def __init__(self):
        self.ops = {e: [] for e in ENGS}
        self.all_ops = []
        self.last = {e: None for e in ENGS}
        self.dma_rr = 0
        self.dma_rr2 = 0
        self.dma_last = [None] * NDMA

    def add(self, eng, fn, reads=(), writes=(), deps=(), is_dma=False):
        op = Op(eng, fn, is_dma)
        d = []
        for t in reads:
            if t.w is not None:
                d.append(t.w)
        for t in writes:
            if t.w is not None:
                d.append(t.w)
            d.extend(t.r.values())
            d.extend(t.rd)
        d.extend(deps)
        seen = set()
        for x in d:
            if x is None or x is op or id(x) in seen:
                continue
            seen.add(id(x))
            if x.eng == "pe" and eng == "pe" and not x.is_dma and not is_dma:
                continue
            op.deps.append(x)
        for t in reads:
            if is_dma:
                t.rd.append(op)
            else:
                t.r[eng] = op
        for t in writes:
            t.w = op
            t.r = {}
            t.rd = []
        if is_dma:
            if eng == "sp":
                s = self.dma_rr % 16
                self.dma_rr += 1
            else:
                s = 16 + self.dma_rr2 % (NDMA - 16)
                self.dma_rr2 += 1
            op.sem = s
            op.prewait = self.dma_last[s]
            self.dma_last[s] = op
        self.ops[eng].append(op)
        self.all_ops.append(op)
        if fn is not None:
            self.last[eng] = op
        return op

    def dma(self, fn, reads=(), writes=(), deps=()):
        return self.add("sp", fn, reads, writes, deps, is_dma=True)

    def barrier(self):
        pend = [self.last[e] for e in ENGS if self.last[e] is not None]
        pend += [o for o in self.dma_last if o is not None]
        for e in ENGS:
            self.add(e, None, deps=pend)

    def finalize(self):
        for op in self.all_ops:
            for d in op.deps:
                d.signal = True
            if op.prewait is not None:
                op.prewait.signal = True
        cnt = {e: 0 for e in ENGS}
        dcnt = [0] * NDMA
        for op in self.all_ops:
            if op.is_dma:
                op.signal = True
                dcnt[op.sem] += 16
                op.count = dcnt[op.sem]
            elif op.signal:
                assert op.fn is not None
                cnt[op.eng] += 1
                op.count = cnt[op.eng]
        waited = {e: {} for e in ENGS}
        for op in self.all_ops:
            w = {}
            dl = list(op.deps)
            if op.prewait is not None:
                dl.append(op.prewait)
            for d in dl:
                key = ("dma", d.sem) if d.is_dma else ("eng", d.eng)
                if w.get(key, 0) < d.count:
                    w[key] = d.count
            wd = waited[op.eng]
            out = []
            for key, val in w.items():
                if wd.get(key, 0) >= val:
                    continue
                wd[key] = val
                out.append((key, val))
            op.waits = out
        self.final_counts = cnt
        self.final_dma = dcnt

    def emit(self, block, sems_eng, sems_dma):
        self.finalize()
        engmap = {"pe": "tensor", "act": "scalar", "dve": "vector", "pool": "gpsimd", "sp": "sync"}

        def semof(key):
            return sems_dma[key[1]] if key[0] == "dma" else sems_eng[key[1]]

        def run(ename):
            def body(e):
                for op in self.ops[ename]:
                    for key, val in op.waits:
                        e.wait_ge(semof(key), val)
                    if op.fn is None:
                        continue
                    ins = op.fn(e)
                    if op.signal:
                        if op.is_dma:
                            ins.then_inc(sems_dma[op.sem], 16)
                        else:
                            ins.then_inc(sems_eng[ename], 1)
                if ename == "sp":
                    for k in ENGS:
                        if self.final_counts[k] > 0:
                            e.wait_ge(sems_eng[k], self.final_counts[k])
                    for s, c in enumerate(self.final_dma):
                        if c > 0:
                            e.wait_ge(sems_dma[s], c)
            return body

        for ename in ENGS:
            getattr(block, engmap[ename])(run(ename))


class Plan:
    def __init__(self, cap):
        self.cap = cap
        self.items = []

    def add(self, name, shape, dtype, live):
        esz = 4 if dtype == F32 else 2
        n = 1
        for s in shape[1:]:
            n *= s
        size = (n * esz + 63) // 64 * 64
        it = dict(name=name, shape=list(shape), dtype=dtype, live=frozenset(live), size=size, off=None)
        self.items.append(it)
        return it

    def solve(self):
        placed = []
        for it in sorted(self.items, key=lambda i: -i["size"]):
            cands = sorted([(p["off"], p["off"] + p["size"]) for p in placed if p["live"] & it["live"]])
            off = 0
            for a, b in cands:
                if off + it["size"] <= a:
                    break
                off = max(off, b)
            assert off + it["size"] <= self.cap, ("SBUF overflow", it["name"], off, it["size"])
            it["off"] = off
            placed.append(it)


def make_cfg(sp=8192, ss=4096):
    parts = []
    for (S, nsplit, nb) in ((sp, 4, 2), (ss, 2, 4)):
        nown = S // nsplit
        n1 = S // 128
        assert 128 % n1 == 0 and nown % 512 == 0 and nown % n1 == 0
        fg = min(1024, nown)
        parts.append(dict(S=S, nsplit=nsplit, nbatch=nb, NOWN=nown, NQ=nown + 2, N1=n1, NJ=128 // n1, NT=S // 128,
                          NST=S // 512, NK2=nown // n1, N2C=nown // n1 + 2, FG=fg, NFG=nown // fg))
    return dict(parts=parts)


def perm_rows(pc):
    N1, NJ, NT = pc["N1"], pc["NJ"], pc["NT"]
    t = np.arange(NT)[:, None, None]
    j = np.arange(NJ)[None, :, None]
    s1 = np.arange(N1)[None, None, :]
    return (128 * s1 + NJ * t + j).reshape(-1)


def rope_tab(pos):
    inv = THETA ** (-np.arange(0, DR, 2, dtype=np.float64) / DR)
    ang = pos.astype(np.float64)[None, :] * inv[:, None]
    c, s = np.cos(ang), np.sin(ang)
    cc = np.concatenate([c, c], 0)
    ss = np.concatenate([-s, s], 0)
    return cc, ss


def host_consts(pc, q):
    S, N1, NJ, NT, NOWN = pc["S"], pc["N1"], pc["NJ"], pc["NT"], pc["NOWN"]
    perm = perm_rows(pc)
    dft = np.zeros((NT, 128, 3, 128), np.float64)
    k1 = np.arange(N1)
    for t in range(NT):
        for j in range(NJ):
            s = perm[128 * t + j * N1: 128 * t + (j + 1) * N1]
            ang = 2 * np.pi * ((s[:, None] * k1[None, :]) % S) / S
            c, sn = np.cos(ang) / math.sqrt(S), np.sin(ang) / math.sqrt(S)
            sl = slice(j * N1, (j + 1) * N1)
            dft[t, sl, 0, sl] = c
            dft[t, sl, 1, sl] = sn
            dft[t, sl, 2, sl] = -sn
    cc, ss = rope_tab(perm)
    csk = np.concatenate([cc, ss], 0).reshape(64, pc["NST"], 512).transpose(1, 0, 2)
    a = q * NOWN
    pos_own = np.concatenate([np.arange(a, a + NOWN), [(a - 1) % S, (a + NOWN) % S]])
    cq, sq = rope_tab(pos_own)
    csq = np.stack([cq, sq], 1)
    k2lo = a // N1
    k2 = (np.arange(k2lo - 1, k2lo + pc["NK2"] + 1)) % 128
    s2 = np.arange(128)
    ang2 = 2 * np.pi * ((s2[:, None] * k2[None, :]) % 128) / 128
    c2 = np.stack([np.cos(ang2), -np.sin(ang2)], 1)
    mask = np.ones((128, 3), np.float32)
    mask[:, 0] = 0.0 if a == 0 else 1.0
    mask[:, 1] = 0.0 if a + NOWN >= S else 1.0
    return dict(dft=dft.astype(NPBF), csk=csk.astype(np.float32), csq=csq.astype(NPBF),
                c2=c2.astype(NPBF), mask=mask, pos_own=pos_own, perm=perm)


def host_weights(w_in, w_uq, w_ukv, w_fnet, w_out, w_gate, w_up, conv_w, conv_b, w_down,
                 g_pre_mix, g_q, g_kv, g_post_mix, g_pre_ffn, g_post_ffn):
    def kmaj(w):
        k, n = w.shape
        return np.ascontiguousarray(w.reshape(k // 128, 128, n).transpose(1, 0, 2))

    def gcol(g):
        return np.ascontiguousarray(g.reshape(-1, 128).T)

    o = {}
    kr = w_in[:, QL + KVL:QL + KVL + DR]
    kr_sw = np.concatenate([kr[:, DR // 2:], kr[:, :DR // 2]], 1)
    o["w_in"] = kmaj(np.concatenate([w_in[:, :QL + KVL], kr, kr_sw, w_in[:, QL + KVL + DR:]], 1))
    wq = w_uq.reshape(QL, NH, DQ)
    wq_sw = np.concatenate([wq[:, :, :DN], wq[:, :, DN + DR // 2:], wq[:, :, DN:DN + DR // 2]], 2)
    o["w_uq"] = kmaj(np.concatenate([wq.reshape(QL, -1), wq_sw.reshape(QL, -1)], 1))
    wkv = w_ukv.reshape(KVL, NH, DN + DV)
    o["w_ukv"] = kmaj(np.concatenate([wkv[:, :, :DN].reshape(KVL, -1), wkv[:, :, DN:].reshape(KVL, -1)], 1))
    o["w_fnet"] = np.ascontiguousarray(w_fnet.transpose(1, 0, 2))
    o["w_out"] = kmaj(w_out)
    o["w_gate"] = np.ascontiguousarray(w_gate.reshape(KC, 128, NF, 128).transpose(2, 1, 0, 3))
    o["w_up"] = np.ascontiguousarray(w_up.reshape(KC, 128, NF, 128).transpose(2, 1, 0, 3))
    o["w_down"] = np.ascontiguousarray(w_down.reshape(NF, 128, D))
    cw = np.concatenate([conv_w, conv_b[None, :]], 0)
    o["conv"] = np.ascontiguousarray(cw.reshape(4, NF, 128).transpose(2, 1, 0))
    o["gcols"] = np.ascontiguousarray(np.concatenate([gcol(g_pre_mix), gcol(g_pre_ffn), gcol(g_q), gcol(g_kv)], 1))
    o["g_post_mix"] = np.ascontiguousarray(np.broadcast_to(g_post_mix[None, :], (128, D)))
    o["g_post_ffn"] = np.ascontiguousarray(np.broadcast_to(g_post_ffn[None, :], (128, D)))
    c = np.arange(128)
    ang = 2 * np.pi * ((c[:, None] * c[None, :]) % 128) / 128
    o["ccsc"] = np.stack([np.cos(ang), np.sin(ang)], 1).astype(np.float64) / math.sqrt(128)
    o["ccsc"] = o["ccsc"].astype(NPBF)
    o["ident"] = np.eye(128, dtype=np.float32).astype(NPBF)
    return {k: (v if v.dtype == NPBF else np.ascontiguousarray(v, dtype=np.float32)) for k, v in o.items()}


def build_program(cfg):
    nc = bass.Bass("TRN2", target_bir_lowering=False)
    parts = cfg["parts"]
    S_ = Sched()

    def din(name, shape, dt=F32):
        return nc.dram_tensor(name, list(shape), dt, kind="ExternalInput").ap()

    def dscr(name, shape, dt):
        return nc.dram_tensor(name, list(shape), dt, kind="Internal").ap()

    W = dict(
        w_in=din("w_in", [128, KC, WIN_COLS]), w_uq=din("w_uq", [128, 2, 2 * NH * DQ]), w_ukv=din("w_ukv", [128, 2, 1024]),
        w_fnet=din("w_fnet", [128, NG, 128]), w_out=din("w_out", [128, KC, D]),
        w_gate=din("w_gate", [NF, 128, KC, 128]), w_up=din("w_up", [NF, 128, KC, 128]), w_down=din("w_down", [NF, 128, D]),
        conv=din("conv", [128, NF, 4]), gcols=din("gcols", [128, 20]), g_post_mix=din("g_post_mix", [128, D]),
        g_post_ffn=din("g_post_ffn", [128, D]), ccsc=din("ccsc", [128, 2, 128], BF16), ident=din("ident", [128, 128], BF16),
    )
    WS = dict(wg=dscr("wgs", [NF, 128, 2, KC * 128], BF16), wd=dscr("wds", [NF, 128, D], BF16),
              w_in=dscr("wins", [128, KC, WIN_COLS], BF16), w_uq=dscr("wuqs", [128, 2, 2 * NH * DQ], BF16),
              w_ukv=dscr("wukvs", [128, 2, 1024], BF16), w_out=dscr("wouts", [128, KC, D], BF16))
    PD = []
    for pi, pc in enumerate(parts):
        PD.append(dict(
            xs=din("xs%d" % pi, [pc["S"], D]), xo=din("xo%d" % pi, [pc["NQ"], D]),
            dft=din("dft%d" % pi, [pc["NT"], 128, 3, 128], BF16), csk=din("csk%d" % pi, [pc["NST"], 64, 512]),
            csq=din("csq%d" % pi, [32, 2, pc["NQ"]], BF16), c2=din("c2own%d" % pi, [128, 2, pc["N2C"]], BF16),
            mask=din("mask%d" % pi, [128, 3]),
            y=nc.dram_tensor("y%d" % pi, [pc["NOWN"], D], F32, kind="ExternalOutput").ap(),
            zd=dscr("zd%d" % pi, [pc["N1"], 128, 1024], BF16), x1s=dscr("x1s%d" % pi, [pc["NOWN"], D], F32),
            dens=dscr("dens%d" % pi, [2, 512], F32),
        ))

    SB_BASE = 16512
    plan = Plan(229344 - SB_BASE)
    ALLPH = set(range(10))

    def PH(pi, *names):
        m = dict(A=0, ATTW=1, ATT=2, B=3, FFN=4)
        return {5 * pi + m[n] for n in names}

    B_ = {}

    def decl(name, shape, dt, live):
        B_[name] = plan.add(name, shape, dt, live)

    decl("ident", [128, 128], BF16, ALLPH)
    decl("ones_bf", [128, 128], BF16, ALLPH)
    decl("ones_f", [128, 64], F32, ALLPH)
    decl("sel", [64, 32], BF16, ALLPH)
    decl("gpm", [128, D], F32, ALLPH)
    decl("gpf", [128, D], F32, ALLPH)
    decl("gcols", [128, 20], F32, ALLPH)
    decl("conv", [128, NF, 4], F32, ALLPH)
    decl("ab", [128, NG, 256], BF16, ALLPH)
    decl("junk", [128, D], BF16, ALLPH)
    decl("sm", [128, 128], F32, ALLPH)
    MIX = lambda pi: PH(pi, "A", "ATT", "B")
    for pi, pc in enumerate(parts):
        S, NQ = pc["S"], pc["NQ"]
        p = "p%d_" % pi
        decl(p + "w_in", [128, KC, WIN_COLS], BF16, PH(pi, "A"))
        decl(p + "w_uq", [128, 2, 2 * NH * DQ], BF16, PH(pi, "ATTW", "ATT"))
        decl(p + "w_ukv", [128, 2, 1024], BF16, PH(pi, "ATTW", "ATT"))
        decl(p + "w_out", [128, KC, D], BF16, PH(pi, "B"))
        decl(p + "wstage", [128, 2, 1536], F32, PH(pi, "A", "ATTW", "B"))
        decl(p + "ckvn", [128, 2, S], BF16, PH(pi, "A", "ATTW", "ATT"))
        decl(p + "kt", [128, 2, S], BF16, PH(pi, "A", "ATTW", "ATT"))
        decl(p + "cqn", [128, 2, NQ], BF16, PH(pi, "A", "ATTW", "ATT"))
        decl(p + "csq", [128, 2, NQ], BF16, PH(pi, "ATTW", "ATT"))
        decl(p + "attnT", [128, 4, NQ], BF16, PH(pi, "ATT", "B"))
        decl(p + "fT", [128, 4, NQ], BF16, PH(pi, "ATT", "B"))
        decl(p + "xst", [128, 2, 4, D], F32, PH(pi, "A"))
        decl(p + "hb", [128, 2, 4, D], BF16, PH(pi, "A"))
        decl(p + "hT", [128, 2, KC, 512], BF16, PH(pi, "A"))
        decl(p + "sq", [128, 2, 512], BF16, PH(pi, "A"))
        decl(p + "rr", [128, 512], F32, PH(pi, "A"))
        decl(p + "ut", [128, 2, NG, 512], BF16, PH(pi, "A"))
        decl(p + "pq", [128, 2, 1024], BF16, PH(pi, "A"))
        decl(p + "zt", [128, 2, 1024], BF16, PH(pi, "A"))
        decl(p + "csk", [64, 2, 512], F32, PH(pi, "A"))
        decl(p + "krt", [64, 512], BF16, PH(pi, "A"))
        decl(p + "dft", [128, 2, 3, 128], BF16, PH(pi, "A"))
        decl(p + "v2", [128, 2, S // 128, 2, DV + 1], BF16, PH(pi, "ATT"))
        decl(p + "qt", [128, NQ], BF16, PH(pi, "ATT"))
        decl(p + "pT", [128, 4, 512], BF16, PH(pi, "ATT"))
        decl(p + "qtmp", [128, 1, 2, 512], F32, PH(pi, "ATT"))
        decl(p + "den", [128, 2, 512], F32, PH(pi, "ATT"))
        if pi == 0:
            decl("pc_st", [128, D], F32, PH(pi, "ATT"))
            decl("pc_ob", [128, 2, D], BF16, PH(pi, "ATT"))
        decl(p + "zk", [128, 2, 1024], BF16, PH(pi, "ATT"))
        decl(p + "c2", [128, 2, pc["N2C"]], BF16, PH(pi, "ATTW", "ATT"))
        decl(p + "xb", [128, 6, D], F32, PH(pi, "B"))
        decl(p + "x1", [128, 6, D], F32, PH(pi, "B"))
        decl(p + "h2b", [128, 6, D], BF16, PH(pi, "B"))
        decl(p + "h2T", [128, KC, NQ], BF16, PH(pi, "B", "FFN"))
        FG = pc["FG"]
        decl(p + "actT", [128, NF, FG], BF16, PH(pi, "FFN"))
        decl(p + "wd", [128, NF, D], BF16, PH(pi, "FFN"))
        decl(p + "wgb", [128, 2, 2, KC * 128], BF16, PH(pi, "FFN"))
        decl(p + "gsb", [128, 2, FG + 2], F32, PH(pi, "FFN"))
        decl(p + "usb", [128, 2, FG], BF16, PH(pi, "FFN"))
        decl(p + "cv", [128, 2, 3, 512], F32, PH(pi, "FFN"))
        decl(p + "nbh", [128, KC, 2], BF16, PH(pi, "FFN"))
        decl(p + "mask", [128, 3], F32, PH(pi, "FFN"))
        decl(p + "x1l", [128, 2, D], F32, PH(pi, "FFN"))
        decl(p + "yo", [128, 2, D], F32, PH(pi, "FFN"))
    plan.solve()

    def sb(name):
        it = B_[name]
        if "h" not in it:
            it["h"] = nc.alloc_sbuf_tensor_at(name, it["shape"], it["dtype"], offset=SB_BASE + it["off"])
        return it["h"]

    es = ExitStack()
    with es:
        PSB = [es.enter_context(nc.psum_tensor("psb%d" % i, [128, 512], F32)) for i in range(8)]
        PT = [Tile("ps%d" % i) for i in range(8)]
        sems_eng = {e: es.enter_context(nc.semaphore("s_" + e)) for e in ENGS}
        sems_dma = [es.enter_context(nc.semaphore("sd%d" % i)) for i in range(NDMA)]
        block = es.enter_context(nc.Block())

        add = S_.add
        dma = S_.dma

        class Rot:
            def __init__(self, banks):
                self.b = list(banks)
                self.i = 0

            def next(self):
                b = self.b[self.i % len(self.b)]
                self.i += 1
                return b

        sm = sb("sm")
        sm_t = [Tile("sm%d" % i) for i in range(8)]
        sm_i = [0]

        def sm_slot():
            i = sm_i[0] % 8
            sm_i[0] += 1
            return sm[:, i * 16:(i + 1) * 16], sm_t[i]

        ident = sb("ident"); ones_bf = sb("ones_bf"); ones_f = sb("ones_f"); gpm = sb("gpm"); gpf = sb("gpf")
        gcols = sb("gcols"); convw = sb("conv"); ab = sb("ab"); junk = sb("junk")
        T_const = Tile("const")
        T_junk = Tile("junk")
        sel = sb("sel")
        T_sel = Tile("sel")
        T_ident = Tile("ident")
        dma(lambda e: e.dma_start(out=ident[:], in_=W["ident"]), writes=[T_ident])
        dma(lambda e: e.dma_start(out=gpm[:], in_=W["g_post_mix"]))
        dma(lambda e: e.dma_start(out=gpf[:], in_=W["g_post_ffn"]))
        dma(lambda e: e.dma_start(out=gcols[:], in_=W["gcols"]))
        dma(lambda e: e.dma_start(out=convw[:], in_=W["conv"]))
        add("dve", lambda e: e.tensor_copy(out=sel[0:32, :], in_=ident[0:32, 0:32]), reads=[T_ident], writes=[T_sel])
        add("dve", lambda e: e.tensor_copy(out=sel[32:64, :], in_=ident[32:64, 32:64]), reads=[T_ident], writes=[T_sel])
        add("pool", lambda e: e.memset(ones_bf[:], 1.0))
        add("pool", lambda e: e.memset(ones_f[:], 1.0))
        wst0 = sb("p0_wstage")
        ccsc_sb = sb("p0_sq")
        T_ab = Tile("ab")
        o1 = dma(lambda e: e.dma_start(out=ccsc_sb[:, 0, 0:256], in_=W["ccsc"].rearrange("p a b -> p (a b)")))
        o2 = dma(lambda e: e.dma_start(out=wst0[:, 0, 0:512], in_=W["w_fnet"].rearrange("p a b -> p (a b)")))
        o3 = add("dve", lambda e: e.tensor_copy(out=ccsc_sb[:, 1, :], in_=wst0[:, 0, 0:512]), deps=[o2])
        for g in range(NG):
            for cs in range(2):
                add("pe", lambda e, g=g, cs=cs: e.matmul(PSB[cs][:, g * 128:(g + 1) * 128], lhsT=ccsc_sb[:, 0, cs * 128:(cs + 1) * 128],
                                                         rhs=ccsc_sb[:, 1, g * 128:(g + 1) * 128], start=True, stop=True),
                    writes=[PT[cs]], deps=[o1, o3])
        for cs in range(2):
            add("dve", lambda e, cs=cs: e.tensor_copy(out=ab[:, :, cs * 128:(cs + 1) * 128],
                                                      in_=PSB[cs][:, :].rearrange("p (g d) -> p g d", g=NG)),
                writes=[PT[cs], T_ab])
        S_.barrier()

        def do_part(pi, pc):
            S, NQ, NOWN, N1, NJ, NT, NST = pc["S"], pc["NQ"], pc["NOWN"], pc["N1"], pc["NJ"], pc["NT"], pc["NST"]
            NK2, N2C, FG, NFG = pc["NK2"], pc["N2C"], pc["FG"], pc["NFG"]
            P = PD[pi]
            p = "p%d_" % pi
            NKC = S // 128
            blocks = [(b * 512, 512) for b in range(NOWN // 512)] + [(NOWN, 2)]

            wstage = sb(p + "wstage")
            T_wst = [Tile(), Tile()]
            wst_i = [0]

            def load_cast(dst, src, ncols, gcol0=None, nkc=None, eng="dve", T_dst=None):
                ops = []
                per = max(1, 1536 // ncols)
                for k0 in range(0, nkc, per):
                    kn = min(per, nkc - k0)
                    i = wst_i[0] % 2
                    wst_i[0] += 1
                    dma(lambda e, i=i, k0=k0, kn=kn: e.dma_start(
                        out=wstage[:, i, 0:kn * ncols].rearrange("p (k c) -> p k c", k=kn), in_=src[:, k0:k0 + kn, :]), writes=[T_wst[i]])
                    for k in range(kn):
                        if gcol0 is None:
                            ops.append(add(eng, lambda e, i=i, k=k, k0=k0: e.tensor_copy(
                                out=dst[:, k0 + k, :], in_=wstage[:, i, k * ncols:(k + 1) * ncols]), reads=[T_wst[i]],
                                writes=([T_dst[k0 + k]] if T_dst else [])))
                        else:
                            ops.append(add(eng, lambda e, i=i, k=k, k0=k0: e.tensor_scalar(
                                out=dst[:, k0 + k, :], in0=wstage[:, i, k * ncols:(k + 1) * ncols],
                                scalar1=gcols[:, gcol0 + k0 + k:gcol0 + k0 + k + 1], scalar2=None, op0=ALU.mult), reads=[T_wst[i]],
                                writes=([T_dst[k0 + k]] if T_dst else [])))
                return ops

            w_in = sb(p + "w_in")
            T_win = [Tile() for _ in range(KC)]
            if pi == 0:
                load_cast(w_in, W["w_in"], WIN_COLS, gcol0=0, nkc=KC, T_dst=T_win)
                add("pool", lambda e: e.dma_start(out=WS["w_in"], in_=w_in[:, :, :]), reads=T_win, is_dma=True)
            else:
                dma(lambda e: e.dma_start(out=w_in[:, :, :], in_=WS["w_in"]), writes=T_win)

            ckvn = sb(p + "ckvn"); kt = sb(p + "kt"); cqn = sb(p + "cqn")
            xst = sb(p + "xst"); hb = sb(p + "hb"); hT = sb(p + "hT"); sq = sb(p + "sq"); rr = sb(p + "rr")
            ut = sb(p + "ut"); pq = sb(p + "pq"); zt = sb(p + "zt"); csk = sb(p + "csk"); krt = sb(p + "krt"); dftb = sb(p + "dft")
            T_xst = [[Tile() for _ in range(4)] for _ in range(2)]
            T_hb = [[Tile() for _ in range(4)] for _ in range(2)]
            T_hT = [[Tile() for _ in range(4)] for _ in range(2)]
            T_sq = Tile(); T_rr = Tile(); T_ut = [[Tile() for _ in range(NG)] for _ in range(2)]
            T_pq = [Tile(), Tile()]; T_zt = [Tile(), Tile()]; T_csk = [Tile(), Tile()]; T_krt = Tile(); T_dft = [Tile(), Tile()]
            T_ckvn = [Tile() for _ in range(NST)]
            T_ktr = [[Tile() for _ in range(NST)] for _ in range(2)]
            T_ktn = [[Tile() for _ in range(NST)] for _ in range(2)]
            T_cqn = [Tile() for _ in range(len(blocks))]
            rot_tp = Rot([0, 1])
            rot_cv = Rot([2, 3])
            rot_pj = Rot([4, 5, 6, 7])

            def stage_load(i, tl):
                par = i % 2
                for j, (src, nt) in enumerate(tl):
                    dma(lambda e, src=src, nt=nt, j=j: e.dma_start(out=xst[0:nt, par, j, :], in_=src), writes=[T_xst[par][j]])

            def stage_norm(i, tl):
                par = i % 2
                sl, T_s = sm_slot()
                ntm = tl[0][1]
                n = len(tl)
                items = []
                for j, (src, nt) in enumerate(tl):
                    items.append(lambda nt=nt, j=j: add("act", lambda e: e.activation(out=junk[0:nt, :], in_=xst[0:nt, par, j, :], func=AF.Square,
                                                                                    accum_out=sl[0:nt, j:j + 1]), reads=[T_xst[par][j]], writes=[T_s]))

                def lnexp():
                    add("act", lambda e: e.activation(out=sl[0:ntm, 4:4 + n], in_=sl[0:ntm, 0:n], func=AF.Ln, scale=1.0 / D, bias=EPS), writes=[T_s])
                    add("act", lambda e: e.activation(out=sl[0:ntm, 8:8 + n], in_=sl[0:ntm, 4:4 + n], func=AF.Exp, scale=-0.5), writes=[T_s])
                items.append(lnexp)
                for j, (src, nt) in enumerate(tl):
                    items.append(lambda nt=nt, j=j: add("act", lambda e: e.activation(out=hb[0:nt, par, j, :], in_=xst[0:nt, par, j, :], func=AF.Copy,
                                                                                    scale=sl[0:nt, 8 + j:9 + j]), reads=[T_xst[par][j], T_s], writes=[T_hb[par][j]]))
                return items

            def transpose_rows(h_src, T_h, nt, hT_dst_cols, T_dst, eng="act"):
                b = rot_tp.next()
                pb = PSB[b][:, :].bitcast(BF16)
                for c in range(KC):
                    add("pe", lambda e, c=c: e.transpose(out=pb[:, c * 128:c * 128 + nt], in_=h_src[0:nt, c * 128:(c + 1) * 128],
                                                         identity=ident[0:nt, 0:nt]), reads=[T_h], writes=[PT[b]])
                src = pb.rearrange("p (c t) -> p c t", c=KC)[:, :, 0:nt]
                if eng == "act":
                    add("act", lambda e: e.copy(out=hT_dst_cols, in_=src), writes=[PT[b], T_dst])
                else:
                    add("dve", lambda e: e.tensor_copy(out=hT_dst_cols, in_=src), writes=[PT[b], T_dst])

            def stage_tr(i, tl):
                par = i % 2
                items = []
                for j, (src, nt) in enumerate(tl):
                    items.append(lambda j=j, nt=nt: transpose_rows(hb[:, par, j, :], T_hb[par][j], nt, hT[:, par, :, j * 128:j * 128 + nt],
                                                                   T_hT[par][j], eng=("act" if j % 2 == 0 else "dve")))
                return items

            def feat_rmsnorm(ps_banks, n, dst_fn, T_dst_list, width):
                for m in range(2):
                    add("act", lambda e, m=m: e.activation(out=sq[:, m, 0:n], in_=PSB[ps_banks[m]][:, 0:n], func=AF.Square),
                        writes=[PT[ps_banks[m]], T_sq])

                def tail():
                    b = rot_pj.next()
                    for m in range(2):
                        add("pe", lambda e, m=m: e.matmul(PSB[b][:, 0:n], lhsT=ones_bf[:, :], rhs=sq[:, m, 0:n], start=(m == 0), stop=(m == 1)),
                            reads=[T_sq], writes=[PT[b]])
                    add("act", lambda e: e.activation(out=rr[:, 0:n], in_=PSB[b][:, 0:n], func=AF.Ln, scale=1.0 / width, bias=EPS),
                        writes=[PT[b], T_rr])
                    add("act", lambda e: e.activation(out=rr[:, 0:n], in_=rr[:, 0:n], func=AF.Exp, scale=-0.5), writes=[T_rr])
                    for m in range(2):
                        add("dve", lambda e, m=m: e.tensor_tensor(out=dst_fn(m), in0=PSB[ps_banks[m]][:, 0:n], in1=rr[:, 0:n], op=ALU.mult),
                            reads=[T_rr], writes=[PT[ps_banks[m]]] + T_dst_list)
                return tail

            def proj(bank, ncols_out, col0, n, rhs_fn, reads):
                for k in range(KC):
                    add("pe", lambda e, k=k: e.matmul(PSB[bank][0:ncols_out, 0:n], lhsT=w_in[:, k, col0:col0 + ncols_out], rhs=rhs_fn(k),
                                                      start=(k == 0), stop=(k == KC - 1)), reads=list(reads) + [T_win[k]], writes=[PT[bank]])

            def stage_proj(st):
                par = st % 2
                ci = st % 2
                dma(lambda e: e.dma_start(out=csk[:, ci, :], in_=P["csk"][st]), writes=[T_csk[ci]])
                bk = [rot_cv.next(), rot_cv.next()]
                groups = []
                state = {}

                def g_ckv(m):
                    def w():
                        proj(bk[m], 128, QL + m * 128, 512, lambda k: hT[:, par, k, :], T_hT[par])
                        if m == 1:
                            state["tail1"] = feat_rmsnorm(bk, 512, lambda mm: ckvn[:, mm, st * 512:(st + 1) * 512], [T_ckvn[st]], KVL)
                    return w
                groups.append(g_ckv(0))
                groups.append(g_ckv(1))

                def g_kr():
                    b = rot_pj.next()
                    proj(b, 64, QL + KVL, 512, lambda k: hT[:, par, k, :], T_hT[par])
                    add("dve", lambda e: e.tensor_tensor(out=krt[:, :], in0=PSB[b][0:64, :], in1=csk[:, ci, :], op=ALU.mult),
                        reads=[T_csk[ci]], writes=[PT[b], T_krt])
                groups.append(g_kr)

                def g_u(g):
                    def w():
                        b = rot_pj.next()
                        proj(b, 128, QL + KVL + 2 * DR + g * 128, 512, lambda k: hT[:, par, k, :], T_hT[par])
                        if g % 2 == 0:
                            add("act", lambda e: e.copy(out=ut[:, par, g, :], in_=PSB[b][:, :]), writes=[PT[b], T_ut[par][g]])
                        else:
                            add("dve", lambda e: e.tensor_copy(out=ut[:, par, g, :], in_=PSB[b][:, :]), writes=[PT[b], T_ut[par][g]])
                    return w
                for g in range(NG):
                    groups.append(g_u(g))

                def tail():
                    b2 = rot_pj.next()
                    add("pe", lambda e: e.matmul(PSB[b2][0:32, :], lhsT=sel[0:64, :], rhs=krt[:, :], start=True, stop=True),
                        reads=[T_krt, T_sel], writes=[PT[b2]])
                    add("dve", lambda e: e.tensor_copy(out=kt[64:96, 0, st * 512:(st + 1) * 512], in_=PSB[b2][0:32, :]),
                        writes=[PT[b2], T_ktr[0][st]])
                    add("act", lambda e: e.copy(out=kt[64:96, 1, st * 512:(st + 1) * 512], in_=PSB[b2][0:32, :]),
                        writes=[PT[b2], T_ktr[1][st]])
                groups.append(tail)
                groups.append(lambda: state["tail1"]())
                return groups

            def stage_fnet(st):
                par = st % 2
                pqs, zs = [], []
                for j in range(4):
                    pqs.append(lambda j=j: fnet_pq(st, par, j))
                    zs.append(lambda j=j: fnet_z(st, par, j))
                return [pqs[0], pqs[1], zs[0], pqs[2], zs[1], pqs[3], zs[2], zs[3]]

            def fnet_pq(st, par, j):
                if True:
                    t = st * 4 + j
                    di = t % 2
                    dma(lambda e, di=di, t=t: e.dma_start(out=dftb[:, di, :, :], in_=P["dft"][t]), writes=[T_dft[di]])
                    bb = [rot_pj.next(), rot_pj.next()]
                    for g in range(NG):
                        bsel = bb[g // 2]
                        add("pe", lambda e, g=g, j=j, bsel=bsel: e.matmul(PSB[bsel][:, (g % 2) * 256:(g % 2) * 256 + 256],
                                                                          lhsT=ut[:, par, g, j * 128:(j + 1) * 128], rhs=ab[:, g, :], start=True, stop=True),
                            reads=[T_ut[par][g], T_ab], writes=[PT[bsel]])
                    for h2 in range(2):
                        add("dve", lambda e, h2=h2, di=di, bb=bb: e.tensor_copy(
                            out=pq[:, di, :].rearrange("p (a g d) -> p g a d", a=2, g=NG)[:, 2 * h2:2 * h2 + 2, :, :],
                            in_=PSB[bb[h2]][:, :].rearrange("p (g a d) -> p g a d", g=2, a=2)), writes=[PT[bb[h2]], T_pq[di]])

            def fnet_z(st, par, j):
                if True:
                    t = st * 4 + j
                    di = t % 2
                    if True:
                        zb = [rot_pj.next(), rot_pj.next()]
                        Pv = pq[:, di, 0:512]
                        Qv = pq[:, di, 512:1024]
                        seq = [(zb[0], 0, Pv, True, False), (zb[1], 0, Qv, True, False), (zb[1], 1, Pv, False, True), (zb[0], 2, Qv, False, True)]
                        for (zbk, mi, rhs, st_, sp_) in seq:
                            add("pe", lambda e, zbk=zbk, mi=mi, rhs=rhs, st_=st_, sp_=sp_: e.matmul(
                                PSB[zbk][:, :], lhsT=dftb[:, di, mi, :], rhs=rhs, start=st_, stop=sp_),
                                reads=[T_dft[di], T_pq[di]], writes=[PT[zbk]])
                        add("act", lambda e: e.copy(out=zt[:, di, 0:512], in_=PSB[zb[0]][:, :]), writes=[PT[zb[0]], T_zt[di]])
                        add("dve", lambda e: e.tensor_copy(out=zt[:, di, 512:1024], in_=PSB[zb[1]][:, :]), writes=[PT[zb[1]], T_zt[di]])
                        for jj in range(NJ):
                            add("pool", lambda e, jj=jj: e.dma_start(out=P["zd"][:, NJ * t + jj, :], in_=zt[jj * N1:(jj + 1) * N1, di, :]),
                                reads=[T_zt[di]], is_dma=True)

            seq_tl = [[(P["xs"][(st * 4 + j) * 128:(st * 4 + j + 1) * 128, :], 128) for j in range(4)] for st in range(NST)]
            own_tl = []
            for (c0, n) in blocks:
                tl = []
                for j in range((n + 127) // 128):
                    nt = min(128, n - j * 128)
                    tl.append((P["xo"][c0 + j * 128:c0 + j * 128 + nt, :], nt))
                own_tl.append(tl)
            all_tl = seq_tl + own_tl
            NU = len(all_tl)

            def stage_cq(bi, par):
                c0, n = blocks[bi]
                bk = [rot_cv.next(), rot_cv.next()]
                state = {}

                def g(m):
                    def w():
                        proj(bk[m], 128, m * 128, n, lambda k: hT[:, par, k, 0:n], T_hT[par])
                        if m == 1:
                            state["t"] = feat_rmsnorm(bk, n, lambda mm: cqn[:, mm, c0:c0 + n], [T_cqn[bi]], QL)
                    return w
                return [g(0), g(1), lambda: state["t"]()]

            stage_load(0, all_tl[0])
            for it in range(NU + 3):
                if it + 1 < NU:
                    stage_load(it + 1, all_tl[it + 1])
                tr = stage_tr(it - 1, all_tl[it - 1]) if 0 <= it - 1 < NU else []
                u2 = it - 2
                if 0 <= u2 < NST:
                    ga = stage_proj(u2)
                    ga_slots = (1, 3, 6, 9, 13, 17, 19, 15, 11)
                elif NST <= u2 < NU:
                    ga = stage_cq(u2 - NST, u2 % 2)
                    ga_slots = (1, 3, 11)
                else:
                    ga, ga_slots = [], ()
                gb = stage_fnet(it - 3) if 0 <= it - 3 < NST else []
                sched_ = []
                for sl_, w_ in zip((0, 5, 9.5, 13.5), tr):
                    sched_.append((sl_, w_))
                for sl_, w_ in zip(ga_slots, ga):
                    sched_.append((sl_, w_))
                for sl_, w_ in zip((4, 7, 10, 14, 16, 18, 20, 21), gb):
                    sched_.append((sl_, w_))
                if it < NU:
                    ni = stage_norm(it, all_tl[it])
                    nsq = (len(ni) - 1) // 2
                    for sl_, w_ in zip((1.5, 3.5, 5.5, 7.5)[:nsq], ni[:nsq]):
                        sched_.append((sl_, w_))
                    sched_.append((9.2, ni[nsq]))
                    for sl_, w_ in zip((11.5, 13.2, 15.5, 17.5)[:nsq], ni[nsq + 1:]):
                        sched_.append((sl_, w_))
                for _, w_ in sorted(sched_, key=lambda x_: x_[0]):
                    w_()
            S_.barrier()

            w_uq = sb(p + "w_uq"); w_ukv = sb(p + "w_ukv"); csq = sb(p + "csq")
            if pi == 0:
                load_cast(w_uq, W["w_uq"], 2 * NH * DQ, gcol0=16, nkc=2)
                load_cast(w_ukv, W["w_ukv"], 1024, gcol0=18, nkc=2)
            else:
                dma(lambda e: e.dma_start(out=w_uq[:, :, :], in_=WS["w_uq"]))
                dma(lambda e: e.dma_start(out=w_ukv[:, :, :], in_=WS["w_ukv"]))
            dma(lambda e: e.dma_start(out=csq[64:96, :, :], in_=P["csq"]))
            c2 = sb(p + "c2")
            dma(lambda e: e.dma_start(out=c2[:, :, :], in_=P["c2"]))
            S_.barrier()
            if pi == 0:
                dma(lambda e: e.dma_start(out=WS["w_uq"], in_=w_uq[:, :, :]))
                dma(lambda e: e.dma_start(out=WS["w_ukv"], in_=w_ukv[:, :, :]))
            v2 = sb(p + "v2"); qt = sb(p + "qt"); pT = sb(p + "pT"); qtmp = sb(p + "qtmp"); den = sb(p + "den"); rb = den
            attnT = sb(p + "attnT"); fT = sb(p + "fT"); zk = sb(p + "zk")
            T_v2 = [[Tile() for _ in range(NKC // 4)] for _ in range(2)]
            add("pool", lambda e: e.memset(v2[:, :, :, :, DV:DV + 1].rearrange("p a k x o -> p (a k x) o"), 1.0),
                writes=[t_ for l_ in T_v2 for t_ in l_])
            T_zk = [Tile(), Tile()]
            if pi == 0:
                pst = sb("pc_st"); pob = sb("pc_ob")
                T_pst = Tile(); T_pob = [Tile(), Tile()]
                kk_ = 0
                for f in range(NF):
                    jobs = ((W["w_gate"][f].rearrange("p k c -> p (k c)"), WS["wg"][f][:, 0, :], True),
                            (W["w_up"][f].rearrange("p k c -> p (k c)"), WS["wg"][f][:, 1, :], True),
                            (W["w_down"][f], WS["wd"][f], False))
                    for (src_, dst_, sc_) in jobs:
                        oi = kk_ % 2
                        kk_ += 1
                        add("pool", lambda e, src_=src_: e.dma_start(out=pst[:, :], in_=src_), writes=[T_pst], is_dma=True)
                        if sc_:
                            add("pool", lambda e, oi=oi: e.tensor_tensor(
                                out=pob[:, oi, :].rearrange("p (k c) -> p k c", k=KC), in0=pst[:, :].rearrange("p (k c) -> p k c", k=KC),
                                in1=gcols[:, 8:16].unsqueeze(2).to_broadcast([128, KC, 128]), op=ALU.mult), reads=[T_pst], writes=[T_pob[oi]])
                        else:
                            add("pool", lambda e, oi=oi: e.tensor_copy(out=pob[:, oi, :], in_=pst[:, :]), reads=[T_pst], writes=[T_pob[oi]])
                        add("pool", lambda e, oi=oi, dst_=dst_: e.dma_start(out=dst_, in_=pob[:, oi, :]), reads=[T_pob[oi]], is_dma=True)

            def gen_F(k1):
                def w():
                    zi = k1 % 2
                    dma(lambda e: e.dma_start(out=zk[:, zi, :], in_=P["zd"][k1]), writes=[T_zk[zi]])
                    b = rot_g.next()
                    for c in range(4):
                        for ri in range(2):
                            add("pe", lambda e, c=c, ri=ri: e.matmul(
                                PSB[b][:, c * N2C:(c + 1) * N2C], lhsT=zk[:, zi, ri * 512 + c * 128:ri * 512 + (c + 1) * 128],
                                rhs=c2[:, ri, :], start=(ri == 0), stop=(ri == 1)), reads=[T_zk[zi]], writes=[PT[b]])
                    src = PSB[b][:, 0:4 * N2C].rearrange("p (c k) -> p c k", c=4)
                    add("dve", lambda e: e.tensor_copy(
                        out=fT[:, :, 0:NOWN].rearrange("p c (k a) -> p c k a", a=N1)[:, :, :, k1], in_=src[:, :, 1:1 + NK2]), writes=[PT[b]])
                    if k1 == N1 - 1:
                        add("dve", lambda e: e.tensor_copy(out=fT[:, :, NOWN:NOWN + 1], in_=src[:, :, 0:1]), writes=[PT[b]])
                    if k1 == 0:
                        add("dve", lambda e: e.tensor_copy(out=fT[:, :, NOWN + 1:NOWN + 2], in_=src[:, :, N2C - 1:N2C]), writes=[PT[b]])
                return w
            f_items = [gen_F(k1) for k1 in range(N1)]
            T_qt = [Tile() for _ in range(len(blocks))]
            T_pT = [Tile() for _ in range(4)]
            T_qtmp = [Tile(), Tile()]; T_den = [Tile(), Tile()]; T_rb = [Tile(), Tile()]; T_dens = [Tile(), Tile()]
            T_attn = [[Tile() for _ in blocks] for _ in range(4)]
            rot_g = Rot([0, 1])
            rot_s = Rot([2, 3, 4, 5])
            rot_o = Rot([6, 7])
            pT_i = [0]
            qi_ = [0]

            def gen_K(h):
                hp = h % 2
                items = []
                for st in range(NST):
                    def w(st=st):
                        b = rot_g.next()
                        for m in range(2):
                            add("pe", lambda e, m=m: e.matmul(PSB[b][0:DN, :], lhsT=w_ukv[:, m, h * DN:(h + 1) * DN],
                                                             rhs=ckvn[:, m, st * 512:(st + 1) * 512], start=(m == 0), stop=(m == 1)),
                                reads=[T_ckvn[st]], writes=[PT[b]])
                        add("dve", lambda e: e.tensor_copy(out=kt[0:DN, hp, st * 512:(st + 1) * 512], in_=PSB[b][0:DN, :]),
                            writes=[PT[b], T_ktn[hp][st]])
                    items.append(w)
                return items

            def gen_V(pr):
                vp = pr % 2
                items = []
                for kg in range(NKC // 4):
                    def w(kg=kg):
                        b = rot_g.next()
                        for kk in range(4):
                            kc = kg * 4 + kk
                            for m in range(2):
                                add("pe", lambda e, m=m, kc=kc, kk=kk: e.matmul(
                                    PSB[b][:, kk * 128:(kk + 1) * 128], lhsT=ckvn[:, m, kc * 128:(kc + 1) * 128],
                                    rhs=w_ukv[:, m, 512 + pr * 128:512 + (pr + 1) * 128], start=(m == 0), stop=(m == 1)),
                                    reads=[T_ckvn[kc // 4]], writes=[PT[b]])
                        for x2 in range(2):
                            add("dve", lambda e, x2=x2: e.tensor_copy(
                                out=v2[:, vp, kg * 4:(kg + 1) * 4, x2, 0:DV],
                                in_=PSB[b][:, :].rearrange("p (k x d) -> p k x d", k=4, x=2)[:, :, x2, :]), writes=[PT[b], T_v2[vp][kg]])
                    items.append(w)
                return items

            def gen_Q(h, bi):
                c0, n = blocks[bi]

                def w():
                    ba, bb2 = rot_g.next(), rot_g.next()
                    qi = 0
                    for (bk_, off) in ((ba, 0), (bb2, NH * DQ)):
                        for m in range(2):
                            add("pe", lambda e, bk_=bk_, off=off, m=m: e.matmul(
                                PSB[bk_][0:DQ, 0:n], lhsT=w_uq[:, m, off + h * DQ:off + (h + 1) * DQ], rhs=cqn[:, m, c0:c0 + n],
                                start=(m == 0), stop=(m == 1)), reads=[T_cqn[bi]], writes=[PT[bk_]])
                    add("dve", lambda e: e.tensor_copy(out=qt[0:DN, c0:c0 + n], in_=PSB[ba][0:DN, 0:n]), writes=[PT[ba], T_qt[bi]])
                    add("dve", lambda e: e.tensor_tensor(out=qtmp[64:96, qi, 0, 0:n], in0=PSB[ba][64:96, 0:n],
                                                         in1=csq[64:96, 0, c0:c0 + n], op=ALU.mult), writes=[PT[ba], T_qtmp[qi]])
                    add("dve", lambda e: e.tensor_tensor(out=qtmp[64:96, qi, 1, 0:n], in0=PSB[bb2][64:96, 0:n],
                                                         in1=csq[64:96, 1, c0:c0 + n], op=ALU.mult), writes=[PT[bb2], T_qtmp[qi]])
                    add("dve", lambda e: e.tensor_tensor(out=qt[64:96, c0:c0 + n], in0=qtmp[64:96, qi, 0, 0:n],
                                                          in1=qtmp[64:96, qi, 1, 0:n], op=ALU.add), reads=[T_qtmp[qi]], writes=[T_qt[bi]])
                return [w]

            for w in gen_K(0) + gen_V(0) + gen_Q(0, 0):
                w()
            units = [(h, bi) for h in range(NH) for bi in range(len(blocks))]
            nfull = len(blocks) - 1
            fin_i = [0]
            pendq = []
            cur_todo = [None]

            pending_fin = {}

            def pop_pv():
                pvfn, gi_, slot_, last_cb = pendq.pop(0)
                if gi_ == 0:
                    ob_ = pvfn.__defaults__[2]
                    if ob_ in pending_fin:
                        pending_fin.pop(ob_)()
                pvfn(gi_, slot_)
                if last_cb is not None:
                    last_cb()

            for ui, (h, bi) in enumerate(units):
                hh = h % 2
                hp = h % 2
                vp = (h // 2) % 2
                c0, n = blocks[bi]
                todo = []
                cur_todo[0] = todo
                if ui + 1 < len(units):
                    todo += gen_Q(*units[ui + 1])
                if bi < nfull:
                    perf = (N1 + NH * nfull - 1) // (NH * nfull)
                    for _ in range(perf):
                        if f_items:
                            todo.append(f_items.pop(0))
                if h + 1 < NH and bi < nfull:
                    ks = gen_K(h + 1)
                    per = (len(ks) + nfull - 1) // nfull
                    todo += ks[bi * per:(bi + 1) * per]
                    if hh == 1:
                        vs = gen_V(h // 2 + 1)
                        per = (len(vs) + nfull - 1) // nfull
                        todo += vs[bi * per:(bi + 1) * per]
                G = max(1, min(NKC, 512 // n))
                ngrp = NKC // G
                ob = rot_o.next()

                def pv(gi, slot, G=G, n=n, ob=ob, hh=hh, vp=vp):
                    for kk in range(G):
                        kc = gi * G + kk
                        add("pe", lambda e, kc=kc, kk=kk: e.matmul(
                            PSB[ob][0:DV + 1, 0:n], lhsT=v2[:, vp, kc, hh, :], rhs=pT[:, slot, kk * n:(kk + 1) * n],
                            start=(kc == 0), stop=(kc == NKC - 1)), reads=[T_v2[vp][kc // 4], T_pT[slot]], writes=[PT[ob]])

                def finalize(ob=ob, n=n, c0=c0, h=h, hh=hh, bi=bi):
                    fi = fin_i[0] % 2
                    fin_i[0] += 1
                    add("dve", lambda e: e.reciprocal(out=den[64:65, fi, 0:n], in_=PSB[ob][64:65, 0:n]), writes=[PT[ob], T_den[fi]])

                    state_f = {"done": False}

                    def fin():
                        if state_f["done"]:
                            return
                        state_f["done"] = True
                        pending_fin.pop(ob, None)
                        dma(lambda e: e.dma_start(out=P["dens"][fi:fi + 1, 0:n], in_=den[64:65, fi, 0:n]), reads=[T_den[fi]], writes=[T_dens[fi]])
                        dma(lambda e: e.dma_start(out=rb[0:DV, fi, 0:n], in_=P["dens"][fi:fi + 1, 0:n].partition_broadcast(DV)),
                            reads=[T_dens[fi]], writes=[T_rb[fi]])
                        add("dve", lambda e: e.tensor_tensor(
                            out=attnT[hh * 64:(hh + 1) * 64, h // 2, c0:c0 + n], in0=PSB[ob][0:DV, 0:n], in1=rb[0:DV, fi, 0:n], op=ALU.mult),
                            reads=[T_rb[fi]], writes=[PT[ob], T_attn[h // 2][bi]])
                    pending_fin[ob] = fin
                    cur_todo[0].append(fin)

                every = max(1, (ngrp - 3) // max(1, len(todo) + 1)) if ngrp > 4 else 1
                for gi in range(ngrp):
                    sbk = rot_s.next()
                    for kk in range(G):
                        kc = gi * G + kk
                        add("pe", lambda e, sbk=sbk, kc=kc, kk=kk, n=n, hp=hp, c0=c0: e.matmul(
                            PSB[sbk][:, kk * n:(kk + 1) * n], lhsT=kt[0:DQ, hp, kc * 128:(kc + 1) * 128], rhs=qt[0:DQ, c0:c0 + n],
                            start=True, stop=True), reads=[T_ktn[hp][kc // 4], T_ktr[hp][kc // 4], T_qt[bi]], writes=[PT[sbk]])
                    slot = pT_i[0] % 4
                    pT_i[0] += 1
                    add("act", lambda e, sbk=sbk, slot=slot, G=G, n=n: e.activation(out=pT[:, slot, 0:G * n], in_=PSB[sbk][:, 0:G * n],
                                                                                    func=AF.Exp, scale=SCALE), writes=[PT[sbk], T_pT[slot]])
                    pendq.append((pv, gi, slot, finalize if gi == ngrp - 1 else None))
                    if len(pendq) > 3:
                        pop_pv()
                    if todo and gi >= 2 and (gi - 2) % every == 0:
                        todo.pop(0)()
                while todo:
                    todo.pop(0)()
            while pendq:
                pop_pv()
            while cur_todo[0]:
                cur_todo[0].pop(0)()
            for w in f_items:
                w()
            S_.barrier()

            w_out = sb(p + "w_out")
            T_wout = [Tile() for _ in range(KC)]
            if pi == 0:
                load_cast(w_out, W["w_out"], D, gcol0=None, nkc=KC, T_dst=T_wout)
                add("pool", lambda e: e.dma_start(out=WS["w_out"], in_=w_out[:, :, :]), reads=T_wout, is_dma=True)
            else:
                dma(lambda e: e.dma_start(out=w_out[:, :, :], in_=WS["w_out"]), writes=T_wout)
            xb = sb(p + "xb"); x1 = sb(p + "x1"); h2b = sb(p + "h2b"); h2T = sb(p + "h2T")
            NSL = 6
            T_xb = [Tile() for _ in range(NSL)]; T_x1 = [Tile() for _ in range(NSL)]; T_h2b = [Tile() for _ in range(NSL)]
            T_h2T = [Tile() for _ in range((NQ + 127) // 128)]
            rot_y = Rot([0, 1, 2, 3, 4, 5])
            rot_tp = Rot([6, 7])
            tiles = [(r0, min(128, NOWN - r0)) for r0 in range(0, NOWN, 128)] + [(NOWN, 2)]
            ybs = {}

            def b_mm(ti):
                r0, nt = tiles[ti]
                i2 = ti % NSL
                bi = min(r0 // 512, len(blocks) - 1) if r0 < NOWN else len(blocks) - 1
                dma(lambda e: e.dma_start(out=xb[0:nt, i2, :], in_=P["xo"][r0:r0 + nt, :]), writes=[T_xb[i2]])
                yb = [rot_y.next(), rot_y.next()]
                ybs[ti] = yb
                for hf in range(2):
                    for c in range(8):
                        src_t = attnT[:, c, r0:r0 + nt] if c < 4 else fT[:, c - 4, r0:r0 + nt]
                        add("pe", lambda e, hf=hf, c=c, src_t=src_t: e.matmul(
                            PSB[yb[hf]][0:nt, :], lhsT=src_t, rhs=w_out[:, c, hf * 512:(hf + 1) * 512], start=(c == 0), stop=(c == 7)),
                            reads=[T_attn[cc][bi] for cc in range(4)] + [T_wout[c]], writes=[PT[yb[hf]]])

            sls = {}

            def b_epi_a(ti):
                r0, nt = tiles[ti]
                yb = ybs[ti]
                sl, T_s = sm_slot()
                sls[ti] = (sl, T_s)
                for hf in range(2):
                    add("act", lambda e, hf=hf: e.activation(out=junk[0:nt, 0:512], in_=PSB[yb[hf]][0:nt, :], func=AF.Square,
                                                             accum_out=sl[0:nt, hf:hf + 1]), writes=[PT[yb[hf]], T_s])
                add("dve", lambda e: e.tensor_tensor(out=sl[0:nt, 2:3], in0=sl[0:nt, 0:1], in1=sl[0:nt, 1:2], op=ALU.add), writes=[T_s])
                add("act", lambda e: e.activation(out=sl[0:nt, 3:4], in_=sl[0:nt, 2:3], func=AF.Ln, scale=1.0 / D, bias=EPS), writes=[T_s])
                add("act", lambda e: e.activation(out=sl[0:nt, 4:5], in_=sl[0:nt, 3:4], func=AF.Exp, scale=-0.5), writes=[T_s])

            def b_epi_b(ti):
                r0, nt = tiles[ti]
                i2 = ti % NSL
                yb = ybs[ti]
                sl, T_s = sls[ti]
                for hf in range(2):
                    add("dve", lambda e, hf=hf: e.scalar_tensor_tensor(
                        out=x1[0:nt, i2, hf * 512:(hf + 1) * 512], in0=PSB[yb[hf]][0:nt, :], scalar=sl[0:nt, 4:5],
                        in1=gpm[0:nt, hf * 512:(hf + 1) * 512], op0=ALU.mult, op1=ALU.mult), reads=[T_s], writes=[PT[yb[hf]], T_x1[i2]])
                add("dve", lambda e: e.tensor_tensor(out=x1[0:nt, i2, :], in0=x1[0:nt, i2, :], in1=xb[0:nt, i2, :], op=ALU.add),
                    reads=[T_xb[i2]], writes=[T_x1[i2]])
                if r0 < NOWN:
                    add("pool", lambda e: e.dma_start(out=P["x1s"][r0:r0 + nt, :], in_=x1[0:nt, i2, :]), reads=[T_x1[i2]], is_dma=True)

            def b_epi_c(ti):
                r0, nt = tiles[ti]
                i2 = ti % NSL
                sl, T_s = sls[ti]
                add("act", lambda e: e.activation(out=junk[0:nt, :], in_=x1[0:nt, i2, :], func=AF.Square, accum_out=sl[0:nt, 8:9]),
                    reads=[T_x1[i2]], writes=[T_s])
                add("act", lambda e: e.activation(out=sl[0:nt, 9:10], in_=sl[0:nt, 8:9], func=AF.Ln, scale=1.0 / D, bias=EPS), writes=[T_s])
                add("act", lambda e: e.activation(out=sl[0:nt, 10:11], in_=sl[0:nt, 9:10], func=AF.Exp, scale=-0.5), writes=[T_s])
                add("act", lambda e: e.activation(out=h2b[0:nt, i2, :], in_=x1[0:nt, i2, :], func=AF.Copy, scale=sl[0:nt, 10:11]),
                    reads=[T_x1[i2], T_s], writes=[T_h2b[i2]])

            def b_tr(ti):
                r0, nt = tiles[ti]
                i2 = ti % NSL
                transpose_rows(h2b[:, i2, :], T_h2b[i2], nt, h2T[:, :, r0:r0 + nt], T_h2T[ti], eng="dve")

            for it in range(len(tiles) + 4):
                if it < len(tiles):
                    b_mm(it)
                if 0 <= it - 1 < len(tiles):
                    b_epi_a(it - 1)
                if 0 <= it - 3 < len(tiles):
                    b_epi_c(it - 3)
                if 0 <= it - 2 < len(tiles):
                    b_epi_b(it - 2)
                if 0 <= it - 4 < len(tiles):
                    b_tr(it - 4)
            S_.barrier()

            actT = sb(p + "actT"); wd = sb(p + "wd"); wgb = sb(p + "wgb"); wgst = None; wdst = None
            gsb = sb(p + "gsb"); usb = sb(p + "usb"); cv = sb(p + "cv"); nbh = sb(p + "nbh"); maskt = sb(p + "mask")
            x1l = sb(p + "x1l"); yo = sb(p + "yo")
            T_mask = Tile()
            dma(lambda e: e.dma_start(out=maskt[:, :], in_=P["mask"]), writes=[T_mask])
            T_wgst = [Tile(), Tile()]; T_wgb = [Tile(), Tile()]; T_wdst = [Tile(), Tile()]
            T_wd = [Tile() for _ in range(NF)]
            T_gh = [Tile(), Tile()]; T_gb = [[Tile() for _ in range(FG // 512)] for _ in range(2)]
            T_usb = [[Tile() for _ in range(FG // 512)] for _ in range(2)]; T_cv = [Tile(), Tile()]; T_nbh = Tile()
            T_act = [Tile() for _ in range(NF)]
            T_x1l = [Tile(), Tile()]; T_yo = [Tile(), Tile()]
            for fg in range(NFG):
                first_group = False
                t0 = fg * FG
                lcol = NOWN if fg == 0 else t0 - 1
                rcol = NOWN + 1 if fg == NFG - 1 else t0 + FG
                lm = 0 if fg == 0 else 2
                rm = 1 if fg == NFG - 1 else 2
                nblk = FG // 512
                add("pool", lambda e, lcol=lcol: e.tensor_copy(out=nbh[:, :, 0:1], in_=h2T[:, :, lcol:lcol + 1]), writes=[T_nbh])
                add("pool", lambda e, rcol=rcol: e.tensor_copy(out=nbh[:, :, 1:2], in_=h2T[:, :, rcol:rcol + 1]), writes=[T_nbh])
                rot_gu = Rot([0, 1, 2, 3, 4, 5])
                rot_nb = Rot([6, 7])

                def conv_stage(f, bl, nblk=nblk):
                    fp = f % 2
                    ci = bl % 2
                    o = bl * 512
                    g_reads = [T_gh[fp]] + [T_gb[fp][x_] for x_ in range(max(0, bl - 1), min(nblk, bl + 2))]
                    add("dve", lambda e: e.tensor_scalar(out=cv[:, ci, 0, :], in0=gsb[:, fp, o:o + 512], scalar1=convw[:, f, 0:1],
                                                         scalar2=convw[:, f, 3:4], op0=ALU.mult, op1=ALU.add), reads=g_reads, writes=[T_cv[ci]])
                    add("dve", lambda e: e.scalar_tensor_tensor(out=cv[:, ci, 1, :], in0=gsb[:, fp, o + 1:o + 513], scalar=convw[:, f, 1:2],
                                                                in1=cv[:, ci, 0, :], op0=ALU.mult, op1=ALU.add), reads=g_reads, writes=[T_cv[ci]])
                    add("dve", lambda e: e.scalar_tensor_tensor(out=cv[:, ci, 2, :], in0=gsb[:, fp, o + 2:o + 514], scalar=convw[:, f, 2:3],
                                                                in1=cv[:, ci, 1, :], op0=ALU.mult, op1=ALU.add), reads=g_reads, writes=[T_cv[ci]])
                    add("act", lambda e: e.activation(out=cv[:, ci, 0, :], in_=cv[:, ci, 2, :], func=AF.Gelu_apprx_tanh), writes=[T_cv[ci]])
                    add("dve", lambda e: e.tensor_tensor(out=actT[:, f, o:o + 512], in0=cv[:, ci, 0, :], in1=usb[:, fp, o:o + 512], op=ALU.mult),
                        reads=[T_cv[ci], T_usb[fp][bl]], writes=[T_act[f]])

                for f in range(NF):
                    wi = f % 2
                    fp = f % 2
                    if first_group:
                        for gu, wsrc in ((0, W["w_gate"]), (1, W["w_up"])):
                            dma(lambda e, wi=wi, gu=gu, wsrc=wsrc, f=f: e.dma_start(out=wgst[:, wi, gu, :], in_=wsrc[f].rearrange("p k c -> p (k c)")),
                                writes=[T_wgst[wi]])
                        for gu in range(2):
                            add("pool", lambda e, wi=wi, gu=gu: e.tensor_tensor(
                                out=wgb[:, wi, gu, :].rearrange("p (k c) -> p k c", k=KC), in0=wgst[:, wi, gu, :].rearrange("p (k c) -> p k c", k=KC),
                                in1=gcols[:, 8:16].unsqueeze(2).to_broadcast([128, KC, 128]), op=ALU.mult), reads=[T_wgst[wi]], writes=[T_wgb[wi]])
                        dma(lambda e, wi=wi, f=f: e.dma_start(out=WS["wg"][f], in_=wgb[:, wi, :, :]), reads=[T_wgb[wi]])
                        dma(lambda e, wi=wi, f=f: e.dma_start(out=wdst[:, wi, :], in_=W["w_down"][f]), writes=[T_wdst[wi]])
                        add("act", lambda e, wi=wi, f=f: e.copy(out=wd[:, f, :], in_=wdst[:, wi, :]), reads=[T_wdst[wi]], writes=[T_wd[f]])
                        dma(lambda e, f=f: e.dma_start(out=WS["wd"][f], in_=wd[:, f, :]), reads=[T_wd[f]])
                    else:
                        dma(lambda e, wi=wi, f=f: e.dma_start(out=wgb[:, wi, :, :], in_=WS["wg"][f]), writes=[T_wgb[wi]])
                        add("pool", lambda e, f=f: e.dma_start(out=wd[:, f, :], in_=WS["wd"][f]), writes=[T_wd[f]], is_dma=True)
                    b = rot_nb.next()
                    for k in range(KC):
                        add("pe", lambda e, b=b, k=k, wi=wi: e.matmul(PSB[b][:, 0:2], lhsT=wgb[:, wi, 0, k * 128:(k + 1) * 128], rhs=nbh[:, k, :],
                                                                      start=(k == 0), stop=(k == KC - 1)), reads=[T_wgb[wi], T_nbh], writes=[PT[b]])
                    add("dve", lambda e, b=b, lm=lm, fp=fp: e.tensor_tensor(out=gsb[:, fp, 0:1], in0=PSB[b][:, 0:1], in1=maskt[:, lm:lm + 1], op=ALU.mult),
                        reads=[T_mask], writes=[PT[b], T_gh[fp]])
                    add("dve", lambda e, b=b, rm=rm, fp=fp: e.tensor_tensor(out=gsb[:, fp, FG + 1:FG + 2], in0=PSB[b][:, 1:2], in1=maskt[:, rm:rm + 1], op=ALU.mult),
                        reads=[T_mask], writes=[PT[b], T_gh[fp]])
                    for bl in range(nblk):
                        c0 = t0 + bl * 512
                        bg, bu = rot_gu.next(), rot_gu.next()
                        for (bk_, gu) in ((bg, 0), (bu, 1)):
                            for k in range(KC):
                                add("pe", lambda e, bk_=bk_, gu=gu, k=k, wi=wi, c0=c0: e.matmul(
                                    PSB[bk_][:, :], lhsT=wgb[:, wi, gu, k * 128:(k + 1) * 128], rhs=h2T[:, k, c0:c0 + 512],
                                    start=(k == 0), stop=(k == KC - 1)), reads=[T_wgb[wi]], writes=[PT[bk_]])
                        add("act", lambda e, bg=bg, bl=bl, fp=fp: e.copy(out=gsb[:, fp, 1 + bl * 512:1 + (bl + 1) * 512], in_=PSB[bg][:, :]),
                            writes=[PT[bg], T_gb[fp][bl]])
                        add("act", lambda e, bu=bu, bl=bl, fp=fp: e.copy(out=usb[:, fp, bl * 512:(bl + 1) * 512], in_=PSB[bu][:, :]),
                            writes=[PT[bu], T_usb[fp][bl]])
                        if f >= 1:
                            conv_stage(f - 1, bl)
                for bl in range(nblk):
                    conv_stage(NF - 1, bl)
                rot_d = Rot([0, 1, 2, 3, 4, 5, 6, 7])
                for tt in range(FG // 128):
                    r0 = t0 + tt * 128
                    i2 = tt % 2
                    dma(lambda e, i2=i2, r0=r0: e.dma_start(out=x1l[:, i2, :], in_=P["x1s"][r0:r0 + 128, :]), writes=[T_x1l[i2]])
                    yb = [rot_d.next(), rot_d.next()]
                    for hf in range(2):
                        for f in range(NF):
                            add("pe", lambda e, hf=hf, f=f, yb=yb, tt=tt: e.matmul(
                                PSB[yb[hf]][:, :], lhsT=actT[:, f, tt * 128:(tt + 1) * 128], rhs=wd[:, f, hf * 512:(hf + 1) * 512],
                                start=(f == 0), stop=(f == NF - 1)), reads=[T_act[f], T_wd[f]], writes=[PT[yb[hf]]])
                    sl, T_s = sm_slot()
                    for hf in range(2):
                        add("act", lambda e, hf=hf, yb=yb, sl=sl: e.activation(out=junk[:, 0:512], in_=PSB[yb[hf]][:, :], func=AF.Square,
                                                                             accum_out=sl[:, hf:hf + 1]), writes=[PT[yb[hf]], T_s])
                    add("dve", lambda e, sl=sl: e.tensor_tensor(out=sl[:, 2:3], in0=sl[:, 0:1], in1=sl[:, 1:2], op=ALU.add), writes=[T_s])
                    add("act", lambda e, sl=sl: e.activation(out=sl[:, 3:4], in_=sl[:, 2:3], func=AF.Ln, scale=1.0 / D, bias=EPS), writes=[T_s])
                    add("act", lambda e, sl=sl: e.activation(out=sl[:, 4:5], in_=sl[:, 3:4], func=AF.Exp, scale=-0.5), writes=[T_s])
                    for hf in range(2):
                        add("dve", lambda e, hf=hf, yb=yb, sl=sl, i2=i2: e.scalar_tensor_tensor(
                            out=yo[:, i2, hf * 512:(hf + 1) * 512], in0=PSB[yb[hf]][:, :], scalar=sl[:, 4:5],
                            in1=gpf[:, hf * 512:(hf + 1) * 512], op0=ALU.mult, op1=ALU.mult), reads=[T_s], writes=[PT[yb[hf]], T_yo[i2]])
                    add("dve", lambda e, i2=i2: e.tensor_tensor(out=yo[:, i2, :], in0=yo[:, i2, :], in1=x1l[:, i2, :], op=ALU.add),
                        reads=[T_x1l[i2]], writes=[T_yo[i2]])
                    add("pool", lambda e, i2=i2, r0=r0: e.dma_start(out=P["y"][r0:r0 + 128, :], in_=yo[:, i2, :]), reads=[T_yo[i2]], is_dma=True)
                if fg == NFG - 1:
                    S_.barrier()

        for pi_, pc_ in enumerate(parts):
            do_part(pi_, pc_)
        S_.emit(block, sems_eng, sems_dma)
    return nc


_CACHE = {}


def run(cfg, x_prompt, x_sample, g_pre_mix, w_in, g_q, w_uq, g_kv, w_ukv, w_fnet, w_out, g_post_mix, g_pre_ffn,
        w_gate, w_up, conv_w, conv_b, w_down, g_post_ffn):
    f = lambda a: np.asarray(a, dtype=np.float32)
    parts = cfg["parts"]
    xs = [f(x_prompt), f(x_sample)]
    hw = host_weights(f(w_in)[0], f(w_uq)[0], f(w_ukv)[0], f(w_fnet)[0], f(w_out)[0], f(w_gate)[0], f(w_up)[0], f(conv_w)[0],
                      f(conv_b)[0], f(w_down)[0], f(g_pre_mix)[0], f(g_q)[0], f(g_kv)[0], f(g_post_mix)[0], f(g_pre_ffn)[0],
                      f(g_post_ffn)[0])
    key = (parts[0]["S"], parts[1]["S"])
    if key not in _CACHE:
        _CACHE[key] = build_program(cfg)
    nc = _CACHE[key]
    in_maps = []
    for c in range(8):
        m = dict(hw)
        for pi, pc in enumerate(parts):
            seq = c // pc["nsplit"]
            q = c % pc["nsplit"]
            hc = host_consts(pc, q)
            x = xs[pi][seq]
            m["xs%d" % pi] = np.ascontiguousarray(x[hc["perm"]])
            m["xo%d" % pi] = np.ascontiguousarray(x[hc["pos_own"]])
            m["dft%d" % pi] = hc["dft"]
            m["csk%d" % pi] = np.ascontiguousarray(hc["csk"])
            m["csq%d" % pi] = np.ascontiguousarray(hc["csq"])
            m["c2own%d" % pi] = np.ascontiguousarray(hc["c2"])
            m["mask%d" % pi] = hc["mask"]
        in_maps.append(m)
    res = run_bass_kernel_spmd(nc, in_maps, core_ids=list(range(8)))
    outs = []
    for pi, pc in enumerate(parts):
        y = np.zeros((pc["nbatch"], pc["S"], D), np.float32)
        for c in range(8):
            seq = c // pc["nsplit"]
            q = c % pc["nsplit"]
            y[seq, q * pc["NOWN"]:(q + 1) * pc["NOWN"]] = res.results[c]["y%d" % pi]
        outs.append(y)
    return tuple(outs)


def kernel(**inputs):
    return run(make_cfg(), **inputs)
```
